# Optimizing a Trainium2 kernel written in Bass

```python
import math
import jax, jax.numpy as jnp
from jax import lax
import numpy as np

D_MODEL = 1024
BATCH = 8
SEQ = 2048
DEPTH = 2

D_FF = 2816
EPS = 1e-6
ROPE_THETA = 10000.0
Q_BLOCK = 128
N_BRANCH = 3
SB_HEADS = 8
SB_HD = 64
MLA_HEADS = 8
MLA_NOPE = 64
MLA_ROPE = 32
MLA_V = 64
MLA_Q_LORA = 256
MLA_KV_LORA = 128
NSA_HEADS = 8
NSA_GROUPS = 2
NSA_REP = NSA_HEADS // NSA_GROUPS
NSA_HD = 64
CMP_LEN = 32
CMP_STRIDE = 16
CMP_HIDDEN = 128
SLC_LEN = 64
SLC_TOPK = 16
SLC_Q_CHUNK = 64
WIN = 512
FORCE_SCORE = 1e3
NEG_INF = -1e30
SB_W = SB_HEADS * SB_HD
NSA_KV_W = NSA_GROUPS * NSA_HD
IN_SIZES = (SB_W, SB_W, SB_W,
            MLA_Q_LORA, MLA_KV_LORA, MLA_ROPE,
            NSA_HEADS * NSA_HD,
            NSA_KV_W, NSA_KV_W, NSA_KV_W, NSA_KV_W, NSA_KV_W, NSA_KV_W,
            N_BRANCH * NSA_HEADS,
            N_BRANCH * D_MODEL)
N_IN = sum(IN_SIZES)

kernel_name = 'hybrid_sb_mla_nsa_macaron'


def rms_norm(x, g):
    xf = x.astype(jnp.float32)
    y = xf * lax.rsqrt(jnp.mean(xf * xf, axis=-1, keepdims=True) + EPS)
    return (y * g.astype(jnp.float32)).astype(x.dtype)


def rope(x, pos):
    d = x.shape[-1]
    half = d // 2
    inv = jnp.exp(-math.log(ROPE_THETA) * jnp.arange(half, dtype=jnp.float32) * (2.0 / d))
    ang = pos.astype(jnp.float32)[:, None] * inv[None, :]
    cos = jnp.cos(ang)[None, :, None, :]
    sin = jnp.sin(ang)[None, :, None, :]
    xf = x.astype(jnp.float32)
    x1, x2 = xf[..., :half], xf[..., half:]
    return jnp.concatenate([x1 * cos - x2 * sin, x1 * sin + x2 * cos], axis=-1).astype(x.dtype)


def masked_softmax(s, mask):
    s = jnp.where(mask, s.astype(jnp.float32), NEG_INF)
    m = jnp.max(s, axis=-1, keepdims=True)
    e = jnp.where(mask, jnp.exp(s - m), 0.0)
    return e / jnp.maximum(jnp.sum(e, axis=-1, keepdims=True), 1e-30)


def swiglu_ffn(x, g, wi, wo):
    a, b = jnp.split(rms_norm(x, g) @ wi, 2, axis=-1)
    return (jax.nn.silu(a) * b) @ wo


def to_blocks(x, blk):
    B, S = x.shape[:2]
    return jnp.moveaxis(x.reshape((B, S // blk, blk) + x.shape[2:]), 1, 0)


def from_blocks(y):
    y = jnp.moveaxis(y, 0, 1)
    return y.reshape((y.shape[0], -1) + y.shape[3:])


def stick_breaking_attention(q, k, v):
    B, S, H, d = q.shape
    scale = 1.0 / math.sqrt(d)
    kpos = jnp.arange(S)

    def block(args):
        qb, i = args
        qpos = i * Q_BLOCK + jnp.arange(Q_BLOCK)
        z = jnp.einsum('bqhd,bkhd->bhqk', qb, k).astype(jnp.float32) * scale
        strict = kpos[None, :] < qpos[:, None]
        log_one_minus = jnp.where(strict, -jax.nn.softplus(z), 0.0)
        suffix = lax.cumsum(log_one_minus, axis=3, reverse=True) - log_one_minus
        w = jnp.where(strict, jnp.exp(jax.nn.log_sigmoid(z) + suffix), 0.0)
        return jnp.einsum('bhqk,bkhd->bqhd', w.astype(v.dtype), v)

    return from_blocks(lax.map(block, (to_blocks(q, Q_BLOCK), jnp.arange(S // Q_BLOCK))))


def causal_attention(q, k, v):
    B, S, H, d = q.shape
    scale = 1.0 / math.sqrt(d)
    kpos = jnp.arange(S)

    def block(args):
        qb, i = args
        qpos = i * Q_BLOCK + jnp.arange(Q_BLOCK)
        s = jnp.einsum('bqhd,bkhd->bhqk', qb, k).astype(jnp.float32) * scale
        p = masked_softmax(s, kpos[None, :] <= qpos[:, None])
        return jnp.einsum('bhqk,bkhd->bqhd', p.astype(v.dtype), v)

    return from_blocks(lax.map(block, (to_blocks(q, Q_BLOCK), jnp.arange(S // Q_BLOCK))))


def mla_attention(c_q, c_kv, k_r, q_norm, w_uq, kv_norm, w_ukv, gain_q, gain_k, pos):
    B, S, _ = c_q.shape
    q = (rms_norm(c_q, q_norm) @ w_uq).reshape(B, S, MLA_HEADS, MLA_NOPE + MLA_ROPE)
    kv = (rms_norm(c_kv, kv_norm) @ w_ukv).reshape(B, S, MLA_HEADS, MLA_NOPE + MLA_V)
    k_nope, v = kv[..., :MLA_NOPE], kv[..., MLA_NOPE:]
    k = jnp.concatenate([k_nope, jnp.broadcast_to(k_r[:, :, None, :], (B, S, MLA_HEADS, MLA_ROPE))], axis=-1)
    q = rms_norm(q, gain_q)
    k = rms_norm(k, gain_k)
    q = jnp.concatenate([q[..., :MLA_NOPE], rope(q[..., MLA_NOPE:], pos)], axis=-1)
    k = jnp.concatenate([k[..., :MLA_NOPE], rope(k[..., MLA_NOPE:], pos)], axis=-1)
    return causal_attention(q, k, v).reshape(B, S, MLA_HEADS * MLA_V)


def nsa_attention(q_in, cmp_k_in, cmp_v_in, slc_k_in, slc_v_in, win_k_in, win_v_in, gate_logits,
                  q_gain, k_gain, cmp_pos_k, cmp_pos_v, cmp_wk1, cmp_wk2, cmp_wv1, cmp_wv2, pos):
    B, S, _ = q_in.shape
    G, R, hd = NSA_GROUPS, NSA_REP, NSA_HD
    scale = 1.0 / math.sqrt(hd)
    q = rope(rms_norm(q_in.reshape(B, S, NSA_HEADS, hd), q_gain), pos).reshape(B, S, G, R, hd)

    n_cmp = (S - CMP_LEN) // CMP_STRIDE + 1
    starts = jnp.arange(n_cmp) * CMP_STRIDE
    ends = starts + (CMP_LEN - 1)
    idx = starts[:, None] + jnp.arange(CMP_LEN)[None, :]

    def compress(t, pe, w1, w2):
        tb = t.reshape(B, S, G, hd)[:, idx] + pe[None, None, :, None, :]
        tb = jnp.moveaxis(tb, 3, 2).reshape(B, n_cmp, G, CMP_LEN * hd)
        return jax.nn.gelu(tb @ w1) @ w2

    k_cmp = rope(rms_norm(compress(cmp_k_in, cmp_pos_k, cmp_wk1, cmp_wk2), k_gain), ends)
    v_cmp = compress(cmp_v_in, cmp_pos_v, cmp_wv1, cmp_wv2)
    s_cmp = jnp.einsum('bsgrd,bngd->bgrsn', q, k_cmp).astype(jnp.float32) * scale
    p_cmp = masked_softmax(s_cmp, ends[None, :] <= pos[:, None])
    o_cmp = jnp.einsum('bgrsn,bngd->bsgrd', p_cmp.astype(v_cmp.dtype), v_cmp)

    n_slc = S // SLC_LEN
    n_sel = min(SLC_TOPK, n_slc)
    c0 = np.arange(n_cmp)[:, None] * CMP_STRIDE
    s0 = np.arange(n_slc)[None, :] * SLC_LEN
    overlap = np.clip(np.minimum(c0 + CMP_LEN, s0 + SLC_LEN) - np.maximum(c0, s0), 0, None) / CMP_LEN
    importance = jnp.einsum('bgrsn,nj->bgsj', p_cmp, jnp.asarray(overlap, jnp.float32))
    blk = jnp.arange(n_slc)[None, :]
    cur = (pos // SLC_LEN)[:, None]
    forced = (blk == 0) | (blk == cur) | (blk == cur - 1)
    score = jnp.where(blk > cur, -1.0, jnp.where(forced, FORCE_SCORE, importance))
    _, sel = lax.top_k(score, n_sel)

    k_s = rope(rms_norm(slc_k_in.reshape(B, S, G, hd), k_gain), pos)
    kb = jnp.moveaxis(k_s.reshape(B, n_slc, SLC_LEN, G, hd), 3, 1)
    vb = jnp.moveaxis(slc_v_in.reshape(B, n_slc, SLC_LEN, G, hd), 3, 1)
    bi = jnp.arange(B)[:, None, None, None]
    gi = jnp.arange(G)[None, :, None, None]
    n_chunk = S // SLC_Q_CHUNK
    n_tok = n_sel * SLC_LEN
    sel_c = jnp.moveaxis(sel.reshape(B, G, n_chunk, SLC_Q_CHUNK, n_sel), 2, 0)

    def slc_chunk(args):
        qc, sc, i = args
        qpos = i * SLC_Q_CHUNK + jnp.arange(SLC_Q_CHUNK)
        kg = kb[bi, gi, sc].reshape(B, G, SLC_Q_CHUNK, n_tok, hd)
        vg = vb[bi, gi, sc].reshape(B, G, SLC_Q_CHUNK, n_tok, hd)
        tok = (sc[..., None] * SLC_LEN + jnp.arange(SLC_LEN)).reshape(B, G, 1, SLC_Q_CHUNK, n_tok)
        s = jnp.einsum('bqgrd,bgqkd->bgrqk', qc, kg).astype(jnp.float32) * scale
        p = masked_softmax(s, tok <= qpos[:, None])
        return jnp.einsum('bgrqk,bgqkd->bqgrd', p.astype(vg.dtype), vg)

    o_slc = from_blocks(lax.map(slc_chunk, (to_blocks(q, SLC_Q_CHUNK), sel_c, jnp.arange(n_chunk))))

    k_w = rope(rms_norm(win_k_in.reshape(B, S, G, hd), k_gain), pos)
    pad = ((0, 0), (WIN, 0), (0, 0), (0, 0))
    kp = jnp.pad(k_w, pad)
    vp = jnp.pad(win_v_in.reshape(B, S, G, hd), pad)

    def win_block(args):
        qb, i = args
        start = i * Q_BLOCK
        kk = lax.dynamic_slice_in_dim(kp, start, Q_BLOCK + WIN, axis=1)
        vv = lax.dynamic_slice_in_dim(vp, start, Q_BLOCK + WIN, axis=1)
        qpos = start + jnp.arange(Q_BLOCK)
        kpos = start - WIN + jnp.arange(Q_BLOCK + WIN)
        dist = qpos[:, None] - kpos[None, :]
        mask = (dist >= 0) & (dist < WIN) & (kpos[None, :] >= 0)
        s = jnp.einsum('bqgrd,bkgd->bgrqk', qb, kk).astype(jnp.float32) * scale
        p = masked_softmax(s, mask)
        return jnp.einsum('bgrqk,bkgd->bqgrd', p.astype(vv.dtype), vv)

    o_win = from_blocks(lax.map(win_block, (to_blocks(q, Q_BLOCK), jnp.arange(S // Q_BLOCK))))

    g = jax.nn.sigmoid(gate_logits.astype(jnp.float32)).reshape(B, S, N_BRANCH, G, R, 1).astype(q.dtype)
    o = g[:, :, 0] * o_cmp + g[:, :, 1] * o_slc + g[:, :, 2] * o_win
    return o.reshape(B, S, NSA_HEADS * hd)


def setup_inputs(seed: int = 0) -> dict:
    key = jax.random.key(seed)
    ks = jax.random.split(key, 28)
    L = DEPTH
    cmp_in = CMP_LEN * NSA_HD

    def nrm(k, shape, scale):
        return jax.random.normal(k, shape, jnp.float32) * scale

    def gain(k, n):
        return 1.0 + 0.05 * jax.random.normal(k, (L, n), jnp.float32)

    return {
        'x': nrm(ks[0], (BATCH, SEQ, D_MODEL), 1.0),
        'ffn1_norm': gain(ks[1], D_MODEL),
        'ffn1_wi': nrm(ks[2], (L, D_MODEL, 2 * D_FF), D_MODEL ** -0.5),
        'ffn1_wo': nrm(ks[3], (L, D_FF, D_MODEL), D_FF ** -0.5),
        'mix_norm': gain(ks[4], D_MODEL),
        'w_in': nrm(ks[5], (L, D_MODEL, N_IN), D_MODEL ** -0.5),
        'mla_q_norm': gain(ks[6], MLA_Q_LORA),
        'mla_w_uq': nrm(ks[7], (L, MLA_Q_LORA, MLA_HEADS * (MLA_NOPE + MLA_ROPE)), MLA_Q_LORA ** -0.5),
        'mla_kv_norm': gain(ks[8], MLA_KV_LORA),
        'mla_w_ukv': nrm(ks[9], (L, MLA_KV_LORA, MLA_HEADS * (MLA_NOPE + MLA_V)), MLA_KV_LORA ** -0.5),
        'mla_qk_gain_q': gain(ks[10], MLA_NOPE + MLA_ROPE),
        'mla_qk_gain_k': gain(ks[11], MLA_NOPE + MLA_ROPE),
        'nsa_q_gain': gain(ks[12], NSA_HD),
        'nsa_k_gain': gain(ks[13], NSA_HD),
        'cmp_pos_k': nrm(ks[14], (L, CMP_LEN, NSA_HD), 0.5),
        'cmp_pos_v': nrm(ks[15], (L, CMP_LEN, NSA_HD), 0.5),
        'cmp_wk1': nrm(ks[16], (L, cmp_in, CMP_HIDDEN), cmp_in ** -0.5),
        'cmp_wk2': nrm(ks[17], (L, CMP_HIDDEN, NSA_HD), CMP_HIDDEN ** -0.5),
        'cmp_wv1': nrm(ks[18], (L, cmp_in, CMP_HIDDEN), cmp_in ** -0.5),
        'cmp_wv2': nrm(ks[19], (L, CMP_HIDDEN, NSA_HD), CMP_HIDDEN ** -0.5),
        'proj_sb': nrm(ks[20], (L, SB_W, D_MODEL), SB_W ** -0.5),
        'proj_mla': nrm(ks[21], (L, MLA_HEADS * MLA_V, D_MODEL), (MLA_HEADS * MLA_V) ** -0.5),
        'proj_nsa': nrm(ks[22], (L, NSA_HEADS * NSA_HD, D_MODEL), (NSA_HEADS * NSA_HD) ** -0.5),
        'w_out': nrm(ks[23], (L, D_MODEL, D_MODEL), D_MODEL ** -0.5),
        'ffn2_norm': gain(ks[24], D_MODEL),
        'ffn2_wi': nrm(ks[25], (L, D_MODEL, 2 * D_FF), D_MODEL ** -0.5),
        'ffn2_wo': nrm(ks[26], (L, D_FF, D_MODEL), D_FF ** -0.5),
    }


def reference(x, ffn1_norm, ffn1_wi, ffn1_wo, mix_norm, w_in, mla_q_norm, mla_w_uq, mla_kv_norm,
              mla_w_ukv, mla_qk_gain_q, mla_qk_gain_k, nsa_q_gain, nsa_k_gain, cmp_pos_k, cmp_pos_v,
              cmp_wk1, cmp_wk2, cmp_wv1, cmp_wv2, proj_sb, proj_mla, proj_nsa, w_out,
              ffn2_norm, ffn2_wi, ffn2_wo):
    B, S, _ = x.shape
    pos = jnp.arange(S, dtype=jnp.int32)
    offsets = np.cumsum(IN_SIZES)[:-1].tolist()
    for l in range(DEPTH):
        x = x + 0.5 * swiglu_ffn(x, ffn1_norm[l], ffn1_wi[l], ffn1_wo[l])
        h = rms_norm(x, mix_norm[l])
        (sb_q, sb_k, sb_v, mla_cq, mla_ckv, mla_kr, nsa_q, ck, cv, sk, sv, wk, wv,
         nsa_gate, merge_gate) = jnp.split(h @ w_in[l], offsets, axis=-1)
        o_sb = stick_breaking_attention(sb_q.reshape(B, S, SB_HEADS, SB_HD),
                                        sb_k.reshape(B, S, SB_HEADS, SB_HD),
                                        sb_v.reshape(B, S, SB_HEADS, SB_HD)).reshape(B, S, SB_W)
        o_mla = mla_attention(mla_cq, mla_ckv, mla_kr, mla_q_norm[l], mla_w_uq[l], mla_kv_norm[l],
                              mla_w_ukv[l], mla_qk_gain_q[l], mla_qk_gain_k[l], pos)
        o_nsa = nsa_attention(nsa_q, ck, cv, sk, sv, wk, wv, nsa_gate, nsa_q_gain[l], nsa_k_gain[l],
                              cmp_pos_k[l], cmp_pos_v[l], cmp_wk1[l], cmp_wk2[l], cmp_wv1[l], cmp_wv2[l], pos)
        gates = jax.nn.sigmoid(merge_gate.astype(jnp.float32)).reshape(B, S, N_BRANCH, D_MODEL).astype(x.dtype)
        y = (gates[:, :, 0] * (o_sb @ proj_sb[l])
             + gates[:, :, 1] * (o_mla @ proj_mla[l])
             + gates[:, :, 2] * (o_nsa @ proj_nsa[l]))
        x = x + y @ w_out[l]
        x = x + 0.5 * swiglu_ffn(x, ffn2_norm[l], ffn2_wi[l], ffn2_wo[l])
    return x
```

```python
import numpy as np
import concourse.bass as bass
import concourse.mybir as mybir

F32 = mybir.dt.float32
BF16 = mybir.dt.bfloat16
U8 = mybir.dt.uint8
I32 = mybir.dt.int32
AF = mybir.ActivationFunctionType
ALU = mybir.AluOpType
AX = mybir.AxisListType

ENGS = ['pe', 'act', 'dve', 'pool', 'sp']
DSIZE = {F32: 4, BF16: 2, U8: 1, I32: 4}
SAME_ENG_SYNC = True
SAME_ENG_GAP = 1
DMA_K = {'sp': 12, 'pool': 12, 'act': 6}


class Buf:
    __slots__ = ('name', 'ap', 'state', 'space', 'off', 'size')

    def __init__(self, name, ap, space='sb', off=0, size=0):
        self.name = name
        self.ap = ap
        self.state = {}
        self.space = space
        self.off = off
        self.size = size

    def __getitem__(self, k):
        return self.ap[k]


class Op:
    __slots__ = ('eng', 'fn', 'idx', 'eidx', 'waits', 'inc', 'tick', 'is_dma', 'dsem', 'dval', 'vc', 'vcd', 'pewr')


class Prog:
    def __init__(self, nc, arena_bytes=200 * 1024):
        self.nc = nc
        self.ops = []
        self.eops = {e: [] for e in ENGS}
        self.known = {e: {d: -1 for d in ENGS} for e in ENGS}
        self.known_dma = {e: {} for e in ENGS}
        self.ndma = {e: 0 for e in ENGS}
        self.dma_hist = {e: [] for e in ENGS}
        self.arena_bytes = arena_bytes
        self.arena = nc.alloc_sbuf_tensor('arena', [128, arena_bytes], U8).ap()
        self.psum = [nc.alloc_psum_tensor('psb%d' % i, [128, 512], F32).ap() for i in range(8)]
        self.sb_top = 0
        self.freed = []
        self.live = {}
        self.all_dma = []
        self.spacer = {}

    def sb_at(self, name, off, shape, dtype, parts=128):
        n = int(np.prod(shape)) * DSIZE[dtype]
        assert off % 4 == 0
        assert off + n <= self.arena_bytes, (name, off, n, self.arena_bytes)
        ap = self.arena[0:parts, off:off + n].bitcast(dtype)
        if len(shape) > 1:
            names = ' '.join('d%d' % i for i in range(len(shape)))
            kw = {'d%d' % i: shape[i] for i in range(len(shape))}
            ap = ap.rearrange('p (%s) -> p %s' % (names, names), **kw)
        b = Buf(name, ap, 'sb', off, n)
        inh_w = []
        for (fo, fs, ops_) in self.freed:
            if fo < off + n and off < fo + fs:
                inh_w.extend(ops_)
        if inh_w:
            b.state[None] = [None, list(inh_w), list(inh_w)]
        return b

    def free(self, b):
        s = []
        for k, st in b.state.items():
            if st[0] is not None:
                s.append(st[0])
            s.extend(st[1])
            if len(st) > 2:
                s.extend(st[2])
        self.freed.append((b.off, b.size, self._compress(s)))
        if len(self.freed) > 400:
            self.freed = self.freed[-400:]

    def psb(self, bank, name=None):
        return Buf(name or ('ps%d' % bank), self.psum[bank], 'ps')

    def dram(self, name, ap):
        return Buf(name, ap, 'dram')

    def _compress(self, lst):
        best = {}
        out = []
        for i in set(lst):
            o = self.ops[i]
            if o.is_dma:
                out.append(i)
            else:
                if o.eng not in best or best[o.eng] < i:
                    best[o.eng] = i
        out.extend(best.values())
        return out

    def _deps_for(self, region, is_write, opidx, eng=None):
        if isinstance(region, tuple):
            buf, key = region
        else:
            buf, key = region, None
        st = buf.state
        deps = []
        psum = (buf.space == 'ps')
        if key is None:
            keys = list(st.keys())
        else:
            keys = [k for k in (key, None) if k in st]
        for k in keys:
            e = st[k]
            if e[0] is not None:
                deps.append(e[0])
            if is_write:
                deps.extend(e[1])
            elif psum:
                deps.extend(r for r in e[1] if self.ops[r].eng != eng)
            if len(e) > 2:
                deps.extend(e[2])
        return deps

    def _update(self, region, is_write, opidx):
        if isinstance(region, tuple):
            buf, key = region
        else:
            buf, key = region, None
        st = buf.state
        if key is None:
            if is_write:
                buf.state = {None: [opidx, []]}
            else:
                if None not in st:
                    st[None] = [None, []]
                for k in st:
                    st[k][1].append(opidx)
                    if len(st[k][1]) > 24:
                        st[k][1] = self._compress(st[k][1])
        else:
            if is_write:
                st[key] = [opidx, []]
            else:
                if key not in st:
                    st[key] = [None, []]
                st[key][1].append(opidx)
                if len(st[key][1]) > 24:
                    st[key][1] = self._compress(st[key][1])

    def _mkop(self, eng, fn, dma):
        o = Op()
        o.eng = eng
        o.fn = fn
        o.idx = len(self.ops)
        o.eidx = len(self.eops[eng])
        o.is_dma = dma
        o.inc = dma
        o.tick = None
        o.pewr = False
        o.waits = []
        return o

    def op(self, eng, fn, reads=(), writes=(), dma=False, pewr=False, strict=False):
        deps = []
        for r in reads:
            deps.extend(self._deps_for(r, False, None, eng))
        for w in writes:
            deps.extend(self._deps_for(w, True, None, eng))
        if eng in self.spacer and SAME_ENG_SYNC and not strict:
            last = len(self.eops[eng]) - 1
            for d in set(deps):
                od = self.ops[d]
                if (not od.is_dma) and od.eng == eng and od.eidx == last:
                    sp = self._mkop(eng, self.spacer[eng], False)
                    sp.vc = dict(self.known[eng])
                    sp.vcd = dict(self.known_dma[eng])
                    self.ops.append(sp)
                    self.eops[eng].append(sp)
                    break
        o = self._mkop(eng, fn, dma)
        if dma:
            k = DMA_K[eng]
            i = self.ndma[eng]
            o.dsem = (eng, i % k)
            o.dval = 16 * (i // k + 1)
            if i >= k:
                deps.append(self.dma_hist[eng][i - k])
            self.ndma[eng] += 1
            self.dma_hist[eng].append(o.idx)
            self.all_dma.append(o.idx)
        kn = self.known[eng]
        kd = self.known_dma[eng]
        for d in sorted(set(deps)):
            od = self.ops[d]
            if od.is_dma:
                if kd.get(od.dsem, 0) >= od.dval:
                    continue
                o.waits.append(d)
                kd[od.dsem] = od.dval
                for e2, v in od.vc.items():
                    if kn[e2] < v:
                        kn[e2] = v
                for k2, v in od.vcd.items():
                    if kd.get(k2, 0) < v:
                        kd[k2] = v
            else:
                if od.eng == eng:
                    if eng == 'pe' or not SAME_ENG_SYNC:
                        continue
                    if eng in self.spacer and not strict:
                        continue
                if kn[od.eng] >= d:
                    continue
                o.waits.append(d)
                od.inc = True
                for e2, v in od.vc.items():
                    if kn[e2] < v:
                        kn[e2] = v
                if kn[od.eng] < d:
                    kn[od.eng] = d
                for k2, v in od.vcd.items():
                    if kd.get(k2, 0) < v:
                        kd[k2] = v
        o.vc = dict(kn)
        o.vcd = dict(kd)
        self.ops.append(o)
        self.eops[eng].append(o)
        for r in reads:
            self._update(r, False, o.idx)
        for w in writes:
            self._update(w, True, o.idx)
        return o

    def emit(self):
        nc = self.nc
        fin_deps = list(self.all_dma[-64:])
        o = Op()
        o.eng = 'sp'; o.fn = None; o.idx = len(self.ops); o.is_dma = False; o.inc = False
        o.tick = None; o.waits = []; o.pewr = False
        kd = self.known_dma['sp']
        for q in DMA_K:
            for d in self.dma_hist[q][-DMA_K[q]:]:
                od = self.ops[d]
                if kd.get(od.dsem, 0) < od.dval:
                    o.waits.append(d)
        o.vc = {}; o.vcd = {}
        self.ops.append(o)
        self.eops['sp'].append(o)

        for e in ENGS:
            t = 0
            for op in self.eops[e]:
                if not op.is_dma and op.inc:
                    t += 1
                    op.tick = t
        import contextlib
        with contextlib.ExitStack() as es:
            esem = {e: es.enter_context(nc.semaphore('s_' + e)) for e in ENGS}
            dsem = {}
            for q, k in DMA_K.items():
                for j in range(k):
                    dsem[(q, j)] = es.enter_context(nc.semaphore('d_%s%d' % (q, j)))
            block = es.enter_context(nc.Block())
            ops = self.ops

            def run(e, h):
                for op in self.eops[e]:
                    for d in op.waits:
                        od = ops[d]
                        if od.is_dma:
                            h.wait_ge(dsem[od.dsem], od.dval)
                        else:
                            h.wait_ge(esem[od.eng], od.tick)
                    if op.fn is None:
                        continue
                    ins = op.fn(h)
                    if op.is_dma:
                        ins.then_inc(dsem[op.dsem], 16)
                    elif op.inc:
                        ins.then_inc(esem[e], 1)

            @block.tensor
            def _(h):
                run('pe', h)

            @block.scalar
            def _(h):
                run('act', h)

            @block.vector
            def _(h):
                run('dve', h)

            @block.gpsimd
            def _(h):
                run('pool', h)

            @block.sync
            def _(h):
                run('sp', h)

    def stats(self):
        s = {e: len(self.eops[e]) for e in ENGS}
        w = {e: sum(len(o.waits) for o in self.eops[e]) for e in ENGS}
        inc = {e: sum(1 for o in self.eops[e] if o.inc) for e in ENGS}
        return s, w, inc
import math, os
from concourse.bass_utils import run_bass_kernel_spmd

S = 2048
D = 1024
DFF = 2816
NT = 16
NL = 2
EPS = 1e-6
ARENA = 207 * 1024
NEG = -30000.0
N_IN = 6328
NPV = 352
PV_GFF1, PV_GMIX, PV_GFF2, PV_QN, PV_KVN, PV_GQ, PV_GK, PV_NQ, PV_NK = 0, 8, 16, 24, 26, 27, 123, 219, 283
O_SBQ, O_SBK, O_SBV, O_CQ, O_CKV, O_KR, O_NQ = 0, 512, 1024, 1536, 1792, 1920, 1952
O_CK, O_CV, O_SK, O_SV, O_WK, O_WV, O_NG, O_MG = 2464, 2592, 2720, 2848, 2976, 3104, 3232, 3256
CF_MC, CF_MS, CF_NC, CF_NS, CF_CC, CF_CS, CF_KEEP, CF_ADD, NCF = 0, 256, 512, 1024, 1536, 1568, 1600, 2112, 2624
CB_ID, CB_NSTRICT, CB_NINCL, CB_NFAR, CB_TRI, CB_NONES, CB_ONES, CB_CMPB, CB_E, CB_OV, NCB = 0, 128, 256, 384, 512, 640, 768, 896, 2944, 4992, 5056


class Alloc:
    def __init__(self, P, base, limit):
        self.P, self.base, self.limit, self.top, self.bufs = P, base, limit, base, []

    def get(self, name, shape, dtype, parts=128):
        n = int(np.prod(shape)) * DSIZE[dtype]
        off = (self.top + 31) // 32 * 32
        assert off + n <= self.limit, ('SBUF phase overflow', name, off, n, self.limit)
        b = self.P.sb_at(name, off, shape, dtype, parts)
        self.top = off + n
        self.peak = max(getattr(self, 'peak', 0), self.top)
        if os.environ.get('MEMDBG'):
            print('ALLOC %-10s off=%6d n=%6d top=%6d limit=%6d slack=%6d' % (name, off, n, self.top, self.limit, self.limit - self.top))
        self.bufs.append(b)
        return b

    def mark(self):
        return (self.top, len(self.bufs))

    def release(self, mark=None):
        top, nb = mark if mark is not None else (self.base, 0)
        for b in self.bufs[nb:]:
            self.P.free(b)
        self.bufs = self.bufs[:nb]
        self.top = top


class K:
    def __init__(self, nc, depth=NL, debug=None, stages=None):
        self.nc = nc
        self.depth = depth
        self.debug = debug
        self.stages = stages
        P = self.P = Prog(nc, arena_bytes=ARENA)
        dt = lambda name, shape, kind="ExternalInput": nc.dram_tensor(name, shape, F32, kind=kind).ap()
        self.x = dt('x', [S, D])
        self.out = dt('out', [S, D], "ExternalOutput")
        W = self.W = {}
        for name, shape in [('ffn1_wi', [NL, D, 2 * DFF]), ('ffn1_wo', [NL, DFF, D]), ('w_in', [NL, D, N_IN]),
                            ('mla_w_uq', [NL, 256, 768]), ('mla_w_ukv', [NL, 128, 1024]),
                            ('cmp_wk1', [NL, 2048, 128]), ('cmp_wk2', [NL, 128, 64]), ('cmp_wv1', [NL, 2048, 128]),
                            ('cmp_wv2', [NL, 128, 64]), ('proj_sb', [NL, 512, D]), ('proj_mla', [NL, 512, D]),
                            ('proj_nsa', [NL, 512, D]), ('w_out', [NL, D, D]), ('ffn2_wi', [NL, D, 2 * DFF]),
                            ('ffn2_wo', [NL, DFF, D]), ('pvec', [NL, 128, NPV]), ('pekv', [NL, 128, 64]),
                            ('cf', [128, NCF]), ('cb', [128, NCB])]:
            W[name] = dt(name, shape)
        if debug:
            self.dbg = dt('dbg', list(debug), "ExternalOutput")
        self.X = P.sb_at('X', 0, [NT, D], F32)
        cbase = NT * D * 4
        self.ident = P.sb_at('ident', cbase, [128], BF16)
        self.pv = P.sb_at('pv', cbase + 256, [NPV], F32)
        self.rstd = P.sb_at('rstd', cbase + 256 + NPV * 4, [NT], F32)
        self.small = P.sb_at('small', cbase + 256 + NPV * 4 + 64, [64], F32)
        self.A = Alloc(P, cbase + 4096, ARENA)
        self.ps = [P.psb(i) for i in range(8)]
        self.cnt = 0
        sm = self.small
        P.op('dve', lambda e: e.memset(sm[:, :], 0.0), (), [sm])
        if os.environ.get('SPACER'): P.spacer['dve'] = lambda e: e.memset(sm[:, 32:40], 0.0)
        if os.environ.get('SPACER'): P.spacer['act'] = lambda e: e.activation(out=sm[:, 40:48], in_=sm[:, 48:56], func=AF.Copy)

    def mm(self, out, lhsT, rhs, start, stop, R, Wr, skip=False):
        self.P.op('pe', lambda e: e.matmul(out, lhsT=lhsT, rhs=rhs, start=start, stop=stop, skip_group_check=skip), R, Wr)

    def tr(self, out, in_, ident, R, Wr):
        self.P.op('pe', lambda e: e.transpose(out=out, in_=in_, identity=ident), R, Wr)

    def act(self, out, in_, func, R, Wr, bias=None, scale=None, accum=None):
        kw = {}
        if bias is not None:
            kw['bias'] = bias
        if scale is not None:
            kw['scale'] = scale
        if accum is not None:
            kw['accum_out'] = accum
        strict = (bias is not None and not isinstance(bias, (int, float))) or (scale is not None and not isinstance(scale, (int, float)))
        self.P.op('act', lambda e: e.activation(out=out, in_=in_, func=func, **kw), R, Wr, strict=strict)

    def tt(self, eng, out, in0, in1, op, R, Wr):
        self.P.op(eng, lambda e: e.tensor_tensor(out=out, in0=in0, in1=in1, op=op), R, Wr)

    def ts(self, eng, out, in0, s1, s2, op0, op1, R, Wr, strict=False):
        strict = strict or not isinstance(s1, (int, float)) or not (s2 is None or isinstance(s2, (int, float)))
        if op1 is None:
            self.P.op(eng, lambda e: e.tensor_scalar(out=out, in0=in0, scalar1=s1, scalar2=None, op0=op0), R, Wr, strict=strict)
        else:
            self.P.op(eng, lambda e: e.tensor_scalar(out=out, in0=in0, scalar1=s1, scalar2=s2, op0=op0, op1=op1), R, Wr, strict=strict)

    def stt(self, eng, out, in0, scalar, in1, op0, op1, R, Wr):
        strict = not isinstance(scalar, (int, float))
        self.P.op(eng, lambda e: e.scalar_tensor_tensor(out=out, in0=in0, scalar=scalar, in1=in1, op0=op0, op1=op1), R, Wr, strict=strict)

    def cp(self, eng, out, in_, R, Wr):
        if eng == 'act':
            self.P.op('act', lambda e: e.copy(out=out, in_=in_), R, Wr)
        else:
            self.P.op(eng, lambda e: e.tensor_copy(out=out, in_=in_), R, Wr)

    def dma(self, q, out, in_, R, Wr):
        self.P.op(q, lambda e: e.dma_start(out=out, in_=in_), R, Wr, dma=True)

    def memset(self, eng, ap, val, Wr):
        self.P.op(eng, lambda e: e.memset(ap, val), (), Wr)

    def recip(self, out, in_, R, Wr):
        self.P.op('dve', lambda e: e.reciprocal(out=out, in_=in_), R, Wr)

    def load_x(self):
        xv = self.x.rearrange('(i p) d -> p i d', p=128)
        for c in range(4):
            self.dma('sp', self.X[:, 4 * c:4 * c + 4, :], xv[:, 4 * c:4 * c + 4, :], (), [(self.X, i) for i in range(4 * c, 4 * c + 4)])
        self.dma('pool', self.ident[:], self.W['cb'][:, CB_ID:CB_ID + 128], (), [self.ident])

    def store_x(self):
        ov = self.out.rearrange('(i p) d -> p i d', p=128)
        od = self.P.dram('out', self.out)
        for c in range(4):
            self.dma('sp', ov[:, 4 * c:4 * c + 4, :], self.X[:, 4 * c:4 * c + 4, :], [(self.X, i) for i in range(4 * c, 4 * c + 4)], [(od, c)])

    def load_pv(self, l):
        self.dma('sp', self.pv[:], self.W['pvec'][l], (), [self.pv])

    def norm_tile(self, i, gcol, hT, tcol, hkey, tmp, bank, save_rstd=True, use_saved=False):
        X = self.X
        ss, junk, xn = tmp
        k = self.cnt
        self.cnt += 1
        xn_ = xn[k % 2]
        rs = self.rstd[:, i:i + 1]
        if not use_saved:
            s_ = ss[:, (k % 2) * 2:(k % 2) * 2 + 1]
            sd = ss[:, (k % 2) * 2 + 1:(k % 2) * 2 + 2]
            sk = (ss, k % 2)
            self.memset('dve', s_, 0.0, [sk])
            self.act(junk[:], X[:, i, :], AF.Square, [(X, i), sk], [junk, sk], accum=s_)
            self.act(sd, s_, AF.Sqrt, [sk], [sk], scale=1.0 / D, bias=EPS)
            self.recip(rs, sd, [sk], [(self.rstd, i)])
        self.act(xn_[:], X[:, i, :], AF.Copy, [(X, i), (self.rstd, i)], [xn_], scale=rs)
        pb = self.ps[bank]
        pbv = pb.ap.bitcast(BF16)
        for c in range(8):
            self.tr(pbv[:, c * 128:(c + 1) * 128], xn_[:, c * 128:(c + 1) * 128], self.ident[:], [xn_, self.ident], [pb])
        self.tt('dve', hT[:, 0:8, tcol:tcol + 128], pbv[:, :].rearrange('p (c t) -> p c t', c=8),
                self.pv[:, gcol:gcol + 8].unsqueeze(2).broadcast_to([128, 8, 128]), ALU.mult, [pb, self.pv], [(hT, hkey)])

    def norm_tmp(self):
        A = self.A
        ss = A.get('ss', [4], F32)
        junk = A.get('junk', [D], BF16)
        xn = [A.get('xn%d' % j, [D], BF16) for j in range(2)]
        return (ss, junk, xn)

    def ffn(self, l, which):
        A, P, X = self.A, self.P, self.X
        A.release()
        wi = self.W['ffn%d_wi' % which][l].rearrange('(c p) f -> p c f', p=128)
        wo = self.W['ffn%d_wo' % which][l].rearrange('(j p) d -> p j d', p=128)
        gcol = PV_GFF1 if which == 1 else PV_GFF2
        wo_sb = A.get('wo_sb', [22, D], BF16)
        hTs = [A.get('hT%d' % j, [8, 1024], BF16) for j in range(2)]
        uT = A.get('uT', [22, 1024], BF16)
        wib = [A.get('wib%d' % j, [8, 256], BF16) for j in range(2)]
        tmp = self.norm_tmp()
        sl = [A.get('sl%d' % j, [512], F32) for j in range(2)]
        it = 0
        for ti in range(8):
            self.norm_tile(ti, gcol, hTs[0], ti * 128, ti, tmp, 4 + (ti % 2))
        for hf in range(2):
            hT = hTs[hf]
            for j in range(22):
                if hf == 0 and j % 2 == 1 and j // 2 < 8:
                    ti = j // 2
                    self.norm_tile(8 + ti, gcol, hTs[1], ti * 128, ti, tmp, 4 + (ti % 2))
                wb = wib[j % 2]
                self.dma('pool', wb[:, :, 0:128], wi[:, :, j * 128:(j + 1) * 128], (), [(wb, 0)])
                self.dma('pool', wb[:, :, 128:256], wi[:, :, DFF + j * 128:DFF + (j + 1) * 128], (), [(wb, 1)])
                if hf == 0 and j % 2 == 0:
                    self.dma('pool', wo_sb[:, j:j + 2, :], wo[:, j:j + 2, :], (), [(wo_sb, j), (wo_sb, j + 1)])
                for tg in range(2):
                    pa, pbk = self.ps[(it % 2) * 2], self.ps[(it % 2) * 2 + 1]
                    s_ = sl[it % 2]
                    it += 1
                    hk = [(hT, 4 * tg + a) for a in range(4)]
                    for k in range(8):
                        self.mm(pa[:, :], wb[:, k, 0:128], hT[:, k, tg * 512:(tg + 1) * 512], k == 0, k == 7, hk + [(wb, 0)], [pa])
                    for k in range(8):
                        self.mm(pbk[:, :], wb[:, k, 128:256], hT[:, k, tg * 512:(tg + 1) * 512], k == 0, k == 7, hk + [(wb, 1)], [pbk])
                    self.act(s_[:], pa[:, :], AF.Silu, [pa], [s_])
                    self.tt('dve', uT[:, j, tg * 512:(tg + 1) * 512], s_[:], pbk[:, :], ALU.mult, [s_, pbk], [(uT, (j, tg))])
            for ti in range(8):
                i = 8 * hf + ti
                for dh in range(2):
                    pb = self.ps[4 + (2 * ti + dh) % 4]
                    for j in range(22):
                        self.mm(pb[:, :], uT[:, j, ti * 128:(ti + 1) * 128], wo_sb[:, j, dh * 512:(dh + 1) * 512], j == 0, j == 21,
                                [(uT, (j, ti // 4)), (wo_sb, j)], [pb])
                    self.stt('dve', X[:, i, dh * 512:(dh + 1) * 512], pb[:, :], 0.5, X[:, i, dh * 512:(dh + 1) * 512], ALU.mult, ALU.add,
                             [pb, (X, i)], [(X, i)])
        A.release()

    def build(self):
        self.load_x()
        for l in range(self.depth):
            self.load_pv(l)
            st = self.stages or ('ffn1', 'mix', 'ffn2')
            if 'ffn1' in st:
                self.ffn(l, 1)
            if 'mix' in st:
                self.mixer(l)
            if 'ffn2' in st:
                self.ffn(l, 2)
        self.store_x()
        self.P.emit()


def host_consts():
    cf = np.zeros((128, NCF), np.float32)
    p = np.arange(128)[:, None]
    pos = (np.arange(NT)[None, :] * 128 + p).astype(np.float32)

    def ropetab(d, posv):
        half = d // 2
        inv = np.exp(np.float32(-math.log(10000.0)) * np.arange(half, dtype=np.float32) * np.float32(2.0 / d)).astype(np.float32)
        ang = (posv[..., None].astype(np.float32) * inv).astype(np.float32)
        return np.cos(ang).astype(np.float32), np.sin(ang).astype(np.float32)
    c, s = ropetab(32, pos)
    cf[:, CF_MC:CF_MC + 256] = c.reshape(128, 256)
    cf[:, CF_MS:CF_MS + 256] = s.reshape(128, 256)
    c, s = ropetab(64, pos)
    cf[:, CF_NC:CF_NC + 512] = c.reshape(128, 512)
    cf[:, CF_NS:CF_NS + 512] = s.reshape(128, 512)
    ends = (np.arange(128) * 16 + 31).astype(np.float32)
    c, s = ropetab(64, ends)
    cf[:, CF_CC:CF_CC + 32] = c
    cf[:, CF_CS:CF_CS + 32] = s
    blk = np.arange(32)[None, None, :]
    cur = (pos // 64).astype(np.int64)[:, :, None]
    forced = (blk == 0) | (blk == cur) | (blk == cur - 1)
    fut = blk > cur
    keep = (~forced) & (~fut)
    add = np.where(fut, -1.0, np.where(forced, 1e3, 0.0))
    cf[:, CF_KEEP:CF_KEEP + 512] = keep.astype(np.float32).reshape(128, 512)
    cf[:, CF_ADD:CF_ADD + 512] = add.astype(np.float32).reshape(128, 512)
    cb = np.zeros((128, NCB), np.float32)
    a = np.arange(128)[:, None]
    b = np.arange(128)[None, :]
    cb[:, CB_ID:CB_ID + 128] = (a == b)
    cb[:, CB_NSTRICT:CB_NSTRICT + 128] = np.where(a < b, 0.0, NEG)
    cb[:, CB_NINCL:CB_NINCL + 128] = np.where(a <= b, 0.0, NEG)
    cb[:, CB_NFAR:CB_NFAR + 128] = np.where(a > b, 0.0, NEG)
    cb[:, CB_TRI:CB_TRI + 128] = np.where(a >= b, -1.0, 0.0)
    cb[:, CB_NONES:CB_NONES + 128] = -1.0
    cb[:, CB_ONES:CB_ONES + 128] = 1.0
    n = np.arange(128)[:, None]
    t = np.arange(S)[None, :]
    cb[:, CB_CMPB:CB_CMPB + S] = np.where(16 * n + 31 <= t, 0.0, NEG)
    j = np.arange(128)[:, None]
    cb[:, CB_E:CB_E + S] = (j == (t // 64)) * 30000.0
    c0 = np.arange(128)[:, None] * 16
    s0 = np.arange(32)[None, :] * 64
    ov = np.clip(np.minimum(c0 + 32, s0 + 64) - np.maximum(c0, s0), 0, None) / 32.0
    cb[:, CB_OV:CB_OV + 32] = ov
    cb[:, CB_OV + 32] = 1.0
    return cf, cb


def host_pvec(inp):
    pv = np.zeros((NL, 128, NPV), np.float32)
    for l in range(NL):
        pv[l, :, PV_GFF1:PV_GFF1 + 8] = inp['ffn1_norm'][l].reshape(8, 128).T
        pv[l, :, PV_GMIX:PV_GMIX + 8] = inp['mix_norm'][l].reshape(8, 128).T
        pv[l, :, PV_GFF2:PV_GFF2 + 8] = inp['ffn2_norm'][l].reshape(8, 128).T
        pv[l, :, PV_QN:PV_QN + 2] = inp['mla_q_norm'][l].reshape(2, 128).T
        pv[l, :, PV_KVN:PV_KVN + 1] = inp['mla_kv_norm'][l].reshape(1, 128).T
        pv[l, :, PV_GQ:PV_GQ + 96] = inp['mla_qk_gain_q'][l][None, :]
        pv[l, :, PV_GK:PV_GK + 96] = inp['mla_qk_gain_k'][l][None, :]
        pv[l, :, PV_NQ:PV_NQ + 64] = inp['nsa_q_gain'][l][None, :]
        pv[l, :, PV_NK:PV_NK + 64] = inp['nsa_k_gain'][l][None, :]
    pe = np.zeros((NL, 128, 64), np.float32)
    for l in range(NL):
        pe[l, :, 0:32] = np.tile(inp['cmp_pos_k'][l].T, (2, 1))
        pe[l, :, 32:64] = np.tile(inp['cmp_pos_v'][l].T, (2, 1))
    return pv, pe


_CACHE = {}


def make_maps(inputs, ncores=8):
    f32 = lambda a: np.ascontiguousarray(np.asarray(a, dtype=np.float32))
    inp = {k: f32(v) for k, v in inputs.items()}
    cf, cb = host_consts()
    pv, pe = host_pvec(inp)
    shared = {k: inp[k] for k in ['ffn1_wi', 'ffn1_wo', 'w_in', 'mla_w_uq', 'mla_w_ukv', 'cmp_wk1', 'cmp_wk2', 'cmp_wv1', 'cmp_wv2',
                                  'proj_sb', 'proj_mla', 'proj_nsa', 'w_out', 'ffn2_wi', 'ffn2_wo']}
    shared.update({'pvec': pv, 'pekv': pe, 'cf': cf, 'cb': cb})
    maps = []
    for c in range(ncores):
        m = dict(shared)
        m['x'] = np.ascontiguousarray(inp['x'][c])
        maps.append(m)
    return maps


def kernel(**inputs):
    if 'nc' not in _CACHE:
        nc = bass.Bass("TRN2", target_bir_lowering=False)
        K(nc).build()
        _CACHE['nc'] = nc
    nc = _CACHE['nc']
    maps = make_maps(inputs, 8)
    res = run_bass_kernel_spmd(nc, maps, core_ids=list(range(8)))
    return np.stack([np.asarray(r['out'], dtype=np.float32) for r in res.results], axis=0)


def _alloc_top(self, name, shape, dtype, parts=128):
    n = int(np.prod(shape)) * DSIZE[dtype]
    off = (self.limit - n) // 32 * 32
    assert off >= self.top, ('SBUF phase overflow (top)', name, off, self.top)
    b = self.P.sb_at(name, off, shape, dtype, parts)
    if not hasattr(self, 'tops'):
        self.tops = []
    self.tops.append((b, self.limit))
    self.limit = off
    return b


def _free_top(self, b):
    tb_, prev = self.tops.pop()
    assert tb_ is b, 'top allocations must be freed LIFO'
    self.P.free(b)
    self.limit = prev


Alloc.get_top = _alloc_top
Alloc.free_top = _free_top


def bc4(ap, n=4):
    return ap.unsqueeze(1).broadcast_to([ap.shape[0], n, ap.shape[1]])


def v3(ap, c):
    return ap.rearrange('p (c t) -> p c t', c=c)


def load_w_in(self, l, c0, n, buf, key=None):
    wv = self.W['w_in'][l].rearrange('(c p) f -> p c f', p=128)
    self.dma('pool', buf[:, :, 0:n], wv[:, :, c0:c0 + n], (), [buf if key is None else (buf, key)])


K.load_w_in = load_w_in


def mixer(self, l):
    A, P = self.A, self.P
    A.release()
    br = self.branches if hasattr(self, 'branches') else ('nsa', 'sb', 'mla')
    self.hT_freed = False
    hT = A.get_top('hT', [8, S], BF16)
    mk0 = A.mark()
    tmp = self.norm_tmp()
    for i in range(NT):
        self.norm_tile(i, PV_GMIX, hT, i * 128, i, tmp, 6 + (i % 2))
    A.release(mk0)
    self.hT = hT
    self.hk = lambda t0, t1: [(hT, i) for i in range(t0, t1)]
    cm = self.cm = A.get('cm', [768], BF16)
    self.dma('pool', cm[:], self.W['cb'][:, CB_NSTRICT:CB_NSTRICT + 768], (), [cm])
    self.nstrict, self.nincl, self.nfar = cm[:, 0:128], cm[:, 128:256], cm[:, 256:384]
    self.tri, self.nones, self.ones = cm[:, 384:512], cm[:, 512:640], cm[:, 640:768]
    self.oT = {}
    if 'nsa' in br:
        self.oT['nsa'] = A.get('oT_nsa', [4, S], BF16)
        self.nsa(l)
    if 'sb' in br:
        self.oT['sb'] = A.get('oT_sb', [4, S], BF16)
        self.sb(l)
    if 'mla' in br:
        self.oT['mla'] = A.get('oT_mla', [4, S], BF16)
        self.mla(l)
    if self.debug:
        for bi, b in enumerate(('sb', 'mla', 'nsa')):
            if b in self.oT:
                self.dma('pool', self.dbg[bi], self.oT[b][:, :, :], [self.oT[b]], [(self.P.dram('dbg', self.dbg), bi)])
    if getattr(self, 'hT_freed', False) and not getattr(self, 'skip_merge', False):
        hT = self.hT = A.get_top('hT', [8, S], BF16)
        self.hk = lambda t0, t1: [(hT, i) for i in range(t0, t1)]
        mk1 = A.mark()
        tmp = self.norm_tmp()
        for i in range(NT):
            self.norm_tile(i, PV_GMIX, hT, i * 128, i, tmp, 6 + (i % 2), use_saved=True)
        A.release(mk1)
        self.hT_freed = False
    if not getattr(self, 'skip_merge', False):
        self.merge(l)
    if not self.hT_freed:
        A.free_top(hT)
    A.release()


K.mixer = mixer


def run_pipeline(blocks, nstage):
    n = len(blocks)
    for t in range(n + nstage - 1):
        for k in reversed(range(nstage)):
            b = t - k
            if 0 <= b < n:
                blocks[b][k]()


def sb(self, l):
    A, P, hT = self.A, self.P, self.hT
    m0 = A.mark()
    wq = [A.get('sbw%d' % j, [8, 384], BF16) for j in range(2)]
    qP = [[A.get('sbq%d%d' % (j, h), [S], BF16) for h in range(2)] for j in range(2)]
    kT = [A.get('sbk%d' % j, [S], BF16) for j in range(2)]
    vv = [A.get('sbv%d' % j, [NT, 128], BF16) for j in range(2)]
    e_t = [A.get('sbe%d' % j, [512], F32) for j in range(2)]
    sp_t = [A.get('sbs%d' % j, [512], BF16) for j in range(3)]
    w_t = [A.get('sbp%d' % j, [512], BF16) for j in range(3)]
    sacc = [A.get('sba%d' % j, [512], BF16) for j in range(2)]
    oT = self.oT['sb']
    ps = self.ps
    ident = self.ident
    for j in range(2):
        for h in range(2):
            self.memset('dve', qP[j][h][:], 0.0, [qP[j][h]])
    wv = self.W['w_in'][l].rearrange('(c p) f -> p c f', p=128)

    def proj_chunks(pr):
        w = wq[pr % 2]
        q_, k_, v_ = qP[pr % 2], kT[pr % 2], vv[pr % 2]
        chunks = []

        def c_dma():
            for j, c0 in enumerate((O_SBQ, O_SBK, O_SBV)):
                self.dma('pool', w[:, :, j * 128:(j + 1) * 128], wv[:, :, c0 + pr * 128:c0 + (pr + 1) * 128], (), [(w, j)])
        chunks.append(c_dma)
        for tg in range(4):
            for j in range(2):
                def c_qk(tg=tg, j=j):
                    tcs = slice(tg * 512, (tg + 1) * 512)
                    pb = ps[6 + (tg * 2 + j) % 2]
                    for k_i in range(8):
                        self.mm(pb[:, :], w[:, k_i, j * 128:(j + 1) * 128], hT[:, k_i, tcs], k_i == 0, k_i == 7,
                                self.hk(4 * tg, 4 * tg + 4) + [(w, j)], [pb])
                    if j == 0:
                        for h in range(2):
                            self.act(q_[h][64 * h:64 * h + 64, tcs], pb[64 * h:64 * h + 64, :], AF.Copy, [pb], [(q_[h], tg)], scale=0.125)
                    else:
                        self.cp('dve', k_[:, tcs], pb[:, :], [pb], [(k_, tg)])
                chunks.append(c_qk)
        for i4 in range(4):
            def c_v(i4=i4):
                pb = ps[6 + i4 % 2]
                for ii in range(4):
                    i = i4 * 4 + ii
                    for k_i in range(8):
                        self.mm(pb[:, ii * 128:(ii + 1) * 128], hT[:, k_i, i * 128:(i + 1) * 128], w[:, k_i, 256:384], k_i == 0 and ii == 0, k_i == 7,
                                [(hT, i), (w, 2)], [pb], skip=True)
                self.cp('dve', v_[:, i4 * 4:i4 * 4 + 4, :], v3(pb[:, :], 4), [pb], [(v_, i4)])
            chunks.append(c_v)
        return chunks

    for c_ in proj_chunks(0):
        c_()
    for pr in range(4):
        q_, k_, v_ = qP[pr % 2], kT[pr % 2], vv[pr % 2]
        blocks = []
        bi = 0
        gi = 0
        for hh in range(2):
            pp = slice(64 * hh, 64 * hh + 64)
            for qg in range(4):
                sa = sacc[gi % 2]
                pc = ps[4 + gi % 2]
                gi += 1
                top = 4 * qg + 3
                for kb in range(top, -1, -1):
                    a = kb - 4 * qg
                    diag = a >= 0
                    c0 = 128 * a if diag else 0
                    cs = slice(c0, 512)
                    qc = slice(qg * 512 + c0, (qg + 1) * 512)
                    kc = slice(kb * 128, (kb + 1) * 128)
                    pa, pbk = ps[bi % 2], ps[2 + bi % 2]
                    et, st, wt = e_t[bi % 2], sp_t[bi % 3], w_t[bi % 3]
                    bi += 1
                    qh = q_[hh]
                    rq = [(qh, qg), (k_, kb // 4)]

                    def s0(pa=pa, cs=cs, c0=c0, kc=kc, qc=qc, diag=diag, rq=rq, qh=qh):
                        self.mm(pa[:, cs], k_[:, kc], qh[:, qc], True, not diag, rq, [pa])
                        if diag:
                            self.mm(pa[:, c0:c0 + 128], ident[:], self.nstrict, False, True, [self.ident, self.cm], [pa])

                    def s1(pa=pa, cs=cs, et=et, st=st):
                        self.act(et[:, cs], pa[:, cs], AF.Exp, [pa], [et])
                        self.act(st[:, cs], et[:, cs], AF.Ln, [et], [st], bias=1.0)

                    def s2(pbk=pbk, cs=cs, c0=c0, kc=kc, qc=qc, diag=diag, rq=rq, qh=qh, st=st, sa=sa, kb=kb, top=top):
                        if kb == top:
                            self.memset('dve', sa[:], 0.0, [sa])
                        self.mm(pbk[:, cs], self.tri, st[:, cs], True, False, [st, self.cm], [pbk])
                        if kb < top:
                            self.mm(pbk[:, cs], self.nones, sa[:, cs], False, False, [sa, self.cm], [pbk])
                        self.mm(pbk[:, cs], k_[:, kc], qh[:, qc], False, not diag, rq, [pbk])
                        if diag:
                            self.mm(pbk[:, c0:c0 + 128], ident[:], self.nstrict, False, True, [self.ident, self.cm], [pbk])
                        if kb > 0:
                            self.tt('dve', sa[:, cs], sa[:, cs], st[:, cs], ALU.add, [sa, st], [sa])

                    def s3(pbk=pbk, cs=cs, wt=wt):
                        self.act(wt[:, cs], pbk[:, cs], AF.Exp, [pbk], [wt])

                    def s4(pc=pc, cs=cs, wt=wt, kb=kb, top=top, pp=pp, qg=qg, hh=hh):
                        self.mm(pc[:, cs], v_[:, kb, :], wt[:, cs], kb == top, kb == 0, [(v_, kb // 4), wt], [pc], skip=True)
                        if kb == 0:
                            self.cp('dve', oT[pp, pr, qg * 512:(qg + 1) * 512], pc[pp, :], [pc], [(oT, (pr, hh, qg))])

                    blocks.append([s0, s1, s2, s3, s4])
        if pr + 1 < 4:
            nop = lambda: None
            ch = proj_chunks(pr + 1)
            step = max(1, (len(blocks) - 8) // len(ch))
            for ci, c_ in enumerate(ch):
                blocks.insert(min(len(blocks), 2 + ci * (step + 1)), [c_, nop, nop, nop, nop])
        run_pipeline(blocks, 5)
    A.release(m0)


K.sb = sb


def headnorm_rope(self, src, nh, hd, rope0, half, gain_ap, cos_ap, sin_ap, dst, scale, tmpb, R, Wr, gain_full=None, gR=None):
    sq, qn, st, r1, r2 = tmpb['sq'], tmpb['qn'], tmpb['st'], tmpb['r1'], tmpb['r2']
    n = nh * hd
    npart = src.shape[0]
    sqv = sq[0:npart, 0:n].rearrange('p (h d) -> p h d', h=nh)
    qnv = qn[0:npart, 0:n].rearrange('p (h d) -> p h d', h=nh)
    self.act(sqv, src, AF.Square, R, [sq])
    ss = st[0:npart, 0:nh]
    sd = st[0:npart, nh:2 * nh]
    self.P.op('dve', lambda e: e.tensor_reduce(out=ss, in_=sqv, axis=AX.X, op=ALU.add), [sq], [(st, 0)])
    self.act(sd, ss, AF.Sqrt, [(st, 0)], [(st, 1)], scale=1.0 / hd, bias=EPS)
    self.recip(ss, sd, [(st, 1)], [(st, 0)])
    if scale != 1.0:
        self.ts('dve', ss, ss, float(scale), None, ALU.mult, None, [(st, 0)], [(st, 0)])
    self.tt('dve', qnv, src, ss.unsqueeze(2).broadcast_to([npart, nh, hd]), ALU.mult, list(R) + [(st, 0)], [qn])
    if gain_full is not None:
        self.tt('dve', qnv, qnv, gain_full, ALU.mult, [qn] + list(gR), [qn])
    else:
        self.tt('dve', qnv, qnv, gain_ap.unsqueeze(1).broadcast_to([npart, nh, hd]), ALU.mult, [qn, self.pv], [qn])
    x1 = qnv[:, :, rope0:rope0 + half]
    x2 = qnv[:, :, rope0 + half:rope0 + 2 * half]
    cb = cos_ap.unsqueeze(1).broadcast_to([npart, nh, half])
    sb_ = sin_ap.unsqueeze(1).broadcast_to([npart, nh, half])
    r1v = r1[0:npart, 0:nh * half].rearrange('p (h d) -> p h d', h=nh)
    r2v = r2[0:npart, 0:nh * half].rearrange('p (h d) -> p h d', h=nh)
    cR = [qn, self.cfm]
    self.tt('dve', r1v, x2, sb_, ALU.mult, cR, [r1])
    self.tt('dve', r2v, x1, sb_, ALU.mult, cR, [r2])
    if rope0 > 0:
        self.cp('act', dst[:, :, 0:rope0], qnv[:, :, 0:rope0], [qn], [(Wr[0], 'a')] if isinstance(Wr[0], Buf) else Wr)
    o1 = dst[:, :, rope0:rope0 + half]
    o2 = dst[:, :, rope0 + half:rope0 + 2 * half]
    self.tt('dve', x1, x1, cb, ALU.mult, cR, [qn])
    self.tt('dve', x2, x2, cb, ALU.mult, cR, [qn])
    self.tt('dve', o1, x1, r1v, ALU.subtract, [qn, r1], Wr)
    self.tt('dve', o2, x2, r2v, ALU.add, [qn, r2], Wr)


K.headnorm_rope = headnorm_rope


def hn_tmp(self, A, n, nh, half):
    return {'sq': A.get('hn_sq', [n], F32), 'qn': A.get('hn_qn', [n], F32), 'st': A.get('hn_st', [2 * nh], F32),
            'r1': A.get('hn_r1', [nh * half], F32), 'r2': A.get('hn_r2', [nh * half], F32)}


K.hn_tmp = hn_tmp


def mla(self, l):
    A, P, hT, ps, ident = self.A, self.P, self.hT, self.ps, self.ident
    m0 = A.mark()
    oT = self.oT['mla']
    cfm = self.cfm = A.get('cfm_mla', [512], F32)
    self.dma('sp', cfm[:], self.W['cf'][:, CF_MC:CF_MC + 512], (), [cfm])
    wuq = A.get('wuq', [2, 768], BF16)
    self.dma('pool', wuq[:, :, :], self.W['mla_w_uq'][l].rearrange('(c p) f -> p c f', p=128), (), [wuq])
    wukv = A.get('wukv', [1024], BF16)
    self.dma('pool', wukv[:], self.W['mla_w_ukv'][l], (), [wukv])
    cqT = A.get('cqT', [2, S], BF16)
    ckvT = A.get('ckvT', [S], BF16)
    krt = A.get('krt', [NT, 32], F32)
    ms1 = A.mark()
    wc = A.get('mla_wc', [8, 416], BF16)
    self.load_w_in(l, O_CQ, 416, wc)
    st = A.get('mla_st', [8], F32)
    junk = A.get('mla_junk', [256], BF16)
    xq = [A.get('mla_xq%d' % j, [384], BF16) for j in range(2)]
    for i in range(NT):
        pb = ps[6 + i % 2]
        for k in range(8):
            self.mm(pb[:, 0:416], hT[:, k, i * 128:(i + 1) * 128], wc[:, k, :], k == 0, k == 7, [(hT, i), wc], [pb])
        x_ = xq[i % 2]
        sk = (st, i % 2)
        o = (i % 2) * 4
        self.memset('dve', st[:, o:o + 2], 0.0, [sk])
        self.act(junk[:, 0:256], pb[:, 0:256], AF.Square, [pb, sk], [junk, sk], accum=st[:, o:o + 1])
        self.act(junk[:, 0:128], pb[:, 256:384], AF.Square, [pb, sk], [junk, sk], accum=st[:, o + 1:o + 2])
        self.act(st[:, o + 2:o + 3], st[:, o:o + 1], AF.Sqrt, [sk], [sk], scale=1.0 / 256, bias=EPS)
        self.act(st[:, o + 3:o + 4], st[:, o + 1:o + 2], AF.Sqrt, [sk], [sk], scale=1.0 / 128, bias=EPS)
        self.recip(st[:, o:o + 2], st[:, o + 2:o + 4], [sk], [sk])
        self.act(x_[:, 0:256], pb[:, 0:256], AF.Copy, [pb, sk], [x_], scale=st[:, o:o + 1])
        self.act(x_[:, 256:384], pb[:, 256:384], AF.Copy, [pb, sk], [x_], scale=st[:, o + 1:o + 2])
        self.cp('dve', krt[:, i, :], pb[:, 384:416], [pb], [(krt, i)])
        pt = ps[4 + i % 2]
        for c in range(3):
            self.mm(pt[:, c * 128:(c + 1) * 128], x_[:, c * 128:(c + 1) * 128], ident[:], c == 0, True, [x_, self.ident], [pt], skip=True)
        self.tt('dve', cqT[:, :, i * 128:(i + 1) * 128], v3(pt[:, 0:256], 2),
                self.pv[:, PV_QN:PV_QN + 2].unsqueeze(2).broadcast_to([128, 2, 128]), ALU.mult, [pt, self.pv], [(cqT, i)])
        self.ts('dve', ckvT[:, i * 128:(i + 1) * 128], pt[:, 256:384], self.pv[:, PV_KVN:PV_KVN + 1], None, ALU.mult, None, [pt, self.pv], [(ckvT, i)])
    A.release(ms1)
    A.free_top(self.hT)
    self.hT_freed = True
    m1 = A.mark()
    qks = [A.get('mla_qkT%d' % j, [4, S], BF16) for j in range(2)]
    vvs = [A.get('mla_v%d' % j, [NT, 128], BF16) for j in range(2)]
    for j in range(2):
        self.memset('dve', qks[j][:, :, :], 0.0, [qks[j]])
    sq = A.get('mq_sq', [768], F32)
    qn = A.get('mq_qn', [768], F32)
    st = A.get('mq_st', [16], F32)
    r1 = A.get('mq_r1', [128], F32)
    r2 = A.get('mq_r2', [128], F32)
    qst = A.get('mq_qst', [768], F32)
    g4 = A.get('mla_g4', [4, 96], F32)
    qr_ = A.get('mla_qr', [768], BF16)
    p_t = [A.get('mla_p%d' % j, [512], BF16) for j in range(3)]
    rec = [A.get('mla_rec%d' % j, [512], F32) for j in range(2)]
    gq = self.pv[:, PV_GQ:PV_GQ + 96]
    gk = self.pv[:, PV_GK:PV_GK + 96]
    self.ts('dve', g4[:, 0, :], gq, float(96 ** -0.5), None, ALU.mult, None, [self.pv], [g4])
    self.ts('dve', g4[:, 1, :], gq, float(96 ** -0.5), None, ALU.mult, None, [self.pv], [g4])
    self.cp('dve', g4[:, 2, :], gk, [self.pv], [g4])
    self.cp('dve', g4[:, 3, :], gk, [self.pv], [g4])
    NPS = 14
    nop = lambda: None
    v4 = lambda ap: ap.rearrange('p (t h d) -> p t h d', t=2, h=4)
    v8 = lambda ap: ap.rearrange('p (h d) -> p h d', h=8)

    def prep_stages(pr, i2):
        qk, vv = qks[pr % 2], vvs[pr % 2]
        pqs = [ps[6], ps[7]]
        tiles = (2 * i2, 2 * i2 + 1)
        qst4, qn4, qr4 = v4(qst[:, :]), v4(qn[:, :]), v4(qr_[:, :])
        cos = cfm[:, i2 * 32:(i2 + 1) * 32].rearrange('p (t d) -> p t d', t=2).unsqueeze(2).broadcast_to([128, 2, 4, 16])
        sin = cfm[:, 256 + i2 * 32:256 + (i2 + 1) * 32].rearrange('p (t d) -> p t d', t=2).unsqueeze(2).broadcast_to([128, 2, 4, 16])
        r14 = r1[:, :].rearrange('p (t h d) -> p t h d', t=2, h=4)
        r24 = r2[:, :].rearrange('p (t h d) -> p t h d', t=2, h=4)
        ss, sd = st[:, 0:8], st[:, 8:16]
        x1, x2 = qn4[:, :, :, 64:80], qn4[:, :, :, 80:96]

        def p0():
            for t_, i in enumerate(tiles):
                pq = pqs[t_]
                for c in range(2):
                    self.mm(pq[:, 0:192], cqT[:, c, i * 128:(i + 1) * 128], wuq[:, c, pr * 192:(pr + 1) * 192], c == 0, c == 1, [(cqT, i), wuq], [pq])
                self.mm(pq[:, 256:512], ckvT[:, i * 128:(i + 1) * 128], wukv[:, pr * 256:(pr + 1) * 256], False, True, [(ckvT, i), wukv], [pq], skip=True)

        def p1():
            for t_, i in enumerate(tiles):
                pq = pqs[t_]
                kvv = v3(pq[:, 256:512], 2)
                self.cp('act', qst4[:, t_, 0:2, :], v3(pq[:, 0:192], 2), [pq], [(qst, (t_, 0))])
                self.cp('act', qst4[:, t_, 2:4, 0:64], kvv[:, :, 0:64], [pq], [(qst, (t_, 1))])
                self.cp('act', qst4[:, t_, 2:4, 64:96], krt[:, i, :].unsqueeze(1).broadcast_to([128, 2, 32]), [(krt, i)], [(qst, (t_, 2))])

        def p2():
            self.act(sq[:, :], qst[:, :], AF.Square, [qst], [sq])
            for t_, i in enumerate(tiles):
                kvv = v3(pqs[t_][:, 256:512], 2)
                self.cp('dve', vv[:, i, :].rearrange('p (h d) -> p h d', h=2), kvv[:, :, 64:128], [pqs[t_]], [(vv, i)])

        def p3():
            self.P.op('dve', lambda e: e.tensor_reduce(out=ss, in_=v8(sq[:, :]), axis=AX.X, op=ALU.add), [sq], [(st, 0)])

        def p4():
            self.act(sd, ss, AF.Ln, [(st, 0)], [(st, 1)], scale=1.0 / 96, bias=EPS)
            self.act(ss, sd, AF.Exp, [(st, 1)], [(st, 0)], scale=-0.5)

        def p5():
            self.tt('dve', v8(qn[:, :]), v8(qst[:, :]), ss.unsqueeze(2).broadcast_to([128, 8, 96]), ALU.mult, [qst, (st, 0)], [qn])
            self.tt('dve', qn4, qn4, g4[:, :, :].unsqueeze(1).broadcast_to([128, 2, 4, 96]), ALU.mult, [qn, g4], [qn])
            self.cp('dve', qr4[:, :, :, 0:64], qn4[:, :, :, 0:64], [qn], [qr_])
            cR = [qn, cfm]
            self.tt('dve', r14, x2, sin, ALU.mult, cR, [r1])
            self.tt('dve', r24, x1, sin, ALU.mult, cR, [r2])
            self.tt('dve', x1, x1, cos, ALU.mult, cR, [qn])
            self.tt('dve', x2, x2, cos, ALU.mult, cR, [qn])
            self.tt('dve', qr4[:, :, :, 64:80], x1, r14, ALU.subtract, [qn, r1], [qr_])
            self.tt('dve', qr4[:, :, :, 80:96], x2, r24, ALU.add, [qn, r2], [qr_])

        pts = [ps[4], ps[5]]

        def p6():
            for t_, i in enumerate(tiles):
                pt = pts[t_]
                for j in range(4):
                    self.mm(pt[0:96, j * 128:(j + 1) * 128], qr4[:, t_, j, :], ident[:], j == 0, True, [qr_, self.ident], [pt], skip=True)

        def p7():
            for t_, i in enumerate(tiles):
                self.cp('act', qk[0:96, :, i * 128:(i + 1) * 128], v3(pts[t_][0:96, :], 4), [pts[t_]], [(qk, i)])

        return [p0, p1, p2, p3, p4, p5] + [nop] * (NPS - 8) + [p6, p7]

    SP = 10
    NCH = NT // 2
    blocks = []
    for i2 in range(NCH):
        blocks.append(prep_stages(0, i2))
        for _ in range(SP - 1):
            blocks.append([nop] * NPS)
    run_pipeline(blocks, NPS)
    for pr in range(4):
        qk, vv = qks[pr % 2], vvs[pr % 2]
        blocks = []
        bi = 0
        gi = 0
        for hh in range(2):
            pp = slice(64 * hh, 64 * hh + 64)
            for qg in range(4):
                pc, pd = ps[2], ps[3]
                rc = rec[gi % 2]
                gi += 1
                top = 4 * qg + 3
                for kb in range(0, top + 1):
                    a = kb - 4 * qg
                    diag = a >= 0
                    c0 = 128 * a if diag else 0
                    cs = slice(c0, 512)
                    qc = slice(qg * 512 + c0, (qg + 1) * 512)
                    kc = slice(kb * 128, (kb + 1) * 128)
                    pa = ps[bi % 2]
                    pt_ = p_t[bi % 3]
                    bi += 1
                    rq = [(qk, i_) for i_ in range(4 * qg, 4 * qg + 4)] + [(qk, kb)]

                    def s0(pa=pa, cs=cs, c0=c0, kc=kc, qc=qc, diag=diag, rq=rq, hh=hh, qk=qk):
                        self.mm(pa[:, cs], qk[:, 2 + hh, kc], qk[:, hh, qc], True, not diag, rq, [pa])
                        if diag:
                            self.mm(pa[:, c0:c0 + 128], ident[:], self.nincl, False, True, [self.ident, self.cm], [pa])

                    def s1(pa=pa, cs=cs, pt_=pt_):
                        self.act(pt_[:, cs], pa[:, cs], AF.Exp, [pa], [pt_])

                    def s2(pc=pc, pd=pd, cs=cs, pt_=pt_, kb=kb, top=top, pp=pp, rc=rc, qg=qg, hh=hh, vv=vv, pr=pr):
                        self.mm(pc[:, cs], vv[:, kb, :], pt_[:, cs], kb == 0, kb == top, [(vv, kb), pt_], [pc])
                        self.mm(pd[:, cs], self.ones, pt_[:, cs], kb == 0, kb == top, [self.cm, pt_], [pd])
                        if kb == top:
                            self.recip(rc[pp, :], pd[pp, :], [pd], [rc])
                            self.tt('dve', oT[pp, pr, qg * 512:(qg + 1) * 512], pc[pp, :], rc[pp, :], ALU.mult, [pc, rc], [(oT, (pr, hh, qg))])

                    blocks.append([s0, s1, s2] + [nop] * (NPS - 3))
        if pr + 1 < 4:
            assert len(blocks) >= SP * NCH
            for i2 in range(NCH):
                blocks.insert(i2 * SP, prep_stages(pr + 1, i2))
        run_pipeline(blocks, NPS)
    A.release(m0)


K.mla = mla


def nsa(self, l):
    A, P, hT, ps, ident = self.A, self.P, self.hT, self.ps, self.ident
    m0 = A.mark()
    oT = self.oT['nsa']
    qT = A.get('nq', [4, S], BF16)
    kTs = A.get('nks', [2, S], BF16)
    kTw = A.get('nkw', [2, S], BF16)
    vs = A.get('nvs', [NT, 2, 65], BF16)
    vw = A.get('nvw', [NT, 2, 65], BF16)
    gts = A.get('ngt', [NT, 24], F32)
    kcT = A.get('nkc', [2, 128], BF16)
    cmpV = A.get('ncv', [2, 97], BF16)
    cfm = self.cfm = A.get('cfm_nsa', [1088], F32)
    self.dma('sp', cfm[:], self.W['cf'][:, CF_NC:CF_NC + 1088], (), [cfm])
    self.memset('dve', vs[:, :, :, :], 1.0, [vs])
    self.memset('dve', vw[:, :, :, :], 1.0, [vw])
    self.memset('dve', kTs[:, :, :], 0.0, [kTs])
    self.memset('dve', kTw[:, :, :], 0.0, [kTw])
    self.memset('dve', kcT[:, :, :], 0.0, [kcT])
    gq = self.pv[:, PV_NQ:PV_NQ + 64]
    gk = self.pv[:, PV_NK:PV_NK + 64]
    m1 = A.mark()
    NB = 2
    bufA = [A.get('npA%d' % j, [768], F32) for j in range(NB)]
    bufB = [A.get('npB%d' % j, [768], F32) for j in range(NB)]
    sts = [A.get('npst%d' % j, [24], F32) for j in range(NB)]
    qkrs = [A.get('npqk%d' % j, [12, 64], BF16) for j in range(NB)]
    g12 = A.get('npg12', [12, 64], F32)
    self.ts('dve', g12[:, 0:8, :], gq.unsqueeze(1).broadcast_to([128, 8, 64]), 0.125, None, ALU.mult, None, [self.pv], [g12])
    self.cp('dve', g12[:, 8:12, :], gk.unsqueeze(1).broadcast_to([128, 4, 64]), [self.pv], [g12])
    wN = A.get_top('nw', [8, 1304], BF16)
    self.load_w_in(l, O_NQ, 512, wN, 0)
    wv_ = self.W['w_in'][l].rearrange('(c p) f -> p c f', p=128)
    self.dma('pool', wN[:, :, 512:1304], wv_[:, :, O_CK:O_CK + 792], (), [(wN, 1)])
    v12 = lambda ap: ap.rearrange('p (h d) -> p h d', h=12)
    nop = lambda: None
    NPN = 10
    SPN = 5

    def prep_stages(i):
        tc_ = slice(i * 128, (i + 1) * 128)
        pq, pk, pg, pt = ps[6 + i % 2], ps[4 + i % 2], ps[2 + i % 2], ps[i % 2]
        bA, bB, st, qkr = bufA[i % NB], bufB[i % NB], sts[i % NB], qkrs[i % NB]
        A12, B12 = v12(bA[:, :]), v12(bB[:, :])
        ss, sd = st[:, 0:12], st[:, 12:24]
        cos = cfm[:, i * 32:(i + 1) * 32].unsqueeze(1).broadcast_to([128, 12, 32])
        sin = cfm[:, 512 + i * 32:512 + (i + 1) * 32].unsqueeze(1).broadcast_to([128, 12, 32])
        r1 = bA[:, 0:384].rearrange('p (h d) -> p h d', h=12)
        r2 = bA[:, 384:768].rearrange('p (h d) -> p h d', h=12)
        x1, x2 = B12[:, :, 0:32], B12[:, :, 32:64]

        def p0():
            for k_i in range(8):
                self.mm(pq[:, :], hT[:, k_i, tc_], wN[:, k_i, 0:512], k_i == 0, k_i == 7, [(hT, i), (wN, 0)], [pq])
            for k_i in range(8):
                self.mm(pk[:, :], hT[:, k_i, tc_], wN[:, k_i, 768:1280], k_i == 0, k_i == 7, [(hT, i), (wN, 1)], [pk])
            for k_i in range(8):
                self.mm(pg[:, 0:24], hT[:, k_i, tc_], wN[:, k_i, 1280:1304], k_i == 0, k_i == 7, [(hT, i), (wN, 1)], [pg])

        def p1():
            self.cp('act', A12[:, 0:8, :], v3(pq[:, :], 8), [pq], [(bA, 0)])
            self.cp('act', A12[:, 8:10, :], v3(pk[:, 0:128], 2), [pk], [(bA, 1)])
            self.cp('act', A12[:, 10:12, :], v3(pk[:, 256:384], 2), [pk], [(bA, 2)])
            self.act(gts[:, i, :], pg[:, 0:24], AF.Tanh, [pg], [(gts, i)], scale=0.5)

        def p2():
            self.act(bB[:, :], bA[:, :], AF.Square, [bA], [bB])
            self.ts('dve', gts[:, i, :], gts[:, i, :], 0.5, 0.5, ALU.mult, ALU.add, [(gts, i)], [(gts, i)])
            self.cp('dve', vs[:, i, :, 0:64], v3(pk[:, 128:256], 2), [pk], [(vs, i)])
            self.cp('dve', vw[:, i, :, 0:64], v3(pk[:, 384:512], 2), [pk], [(vw, i)])

        def p3():
            self.P.op('dve', lambda e: e.tensor_reduce(out=ss, in_=B12, axis=AX.X, op=ALU.add), [bB], [(st, 0)])

        def p4():
            self.act(sd, ss, AF.Ln, [(st, 0)], [(st, 1)], scale=1.0 / 64, bias=EPS)
            self.act(ss, sd, AF.Exp, [(st, 1)], [(st, 0)], scale=-0.5)

        def p5():
            self.tt('dve', B12, A12, ss.unsqueeze(2).broadcast_to([128, 12, 64]), ALU.mult, [bA, (st, 0)], [bB])
            self.tt('dve', B12, B12, g12[:, :, :], ALU.mult, [bB, g12], [bB])
            cR = [bB, cfm]
            self.tt('dve', r1, x2, sin, ALU.mult, cR, [(bA, 'r1')])
            self.tt('dve', r2, x1, sin, ALU.mult, cR, [(bA, 'r2')])
            self.tt('dve', x1, x1, cos, ALU.mult, cR, [bB])
            self.tt('dve', x2, x2, cos, ALU.mult, cR, [bB])
            self.tt('dve', qkr[:, :, 0:32], x1, r1, ALU.subtract, [bB, (bA, 'r1')], [qkr])
            self.tt('dve', qkr[:, :, 32:64], x2, r2, ALU.add, [bB, (bA, 'r2')], [qkr])

        def p8():
            for r in range(4):
                for g in range(2):
                    self.mm(pt[64 * g:64 * g + 64, r * 128:(r + 1) * 128], qkr[:, 4 * g + r, :], ident[:], r == 0, True, [qkr, self.ident], [pt], skip=True)
            for b in range(2):
                for g in range(2):
                    self.mm(pq[64 * g:64 * g + 64, b * 128:(b + 1) * 128], qkr[:, 8 + 2 * b + g, :], ident[:], b == 0, True, [qkr, self.ident], [pq], skip=True)

        def p9():
            self.cp('act', qT[:, :, tc_], v3(pt[:, :], 4), [pt], [(qT, i)])
            for g in range(2):
                gs = slice(64 * g, 64 * g + 64)
                self.cp('act', kTs[gs, g, tc_], pq[gs, 0:128], [pq], [(kTs, i)])
                self.cp('act', kTw[gs, g, tc_], pq[gs, 128:256], [pq], [(kTw, i)])

        return [p0, p1, p2, p3, p4, p5, nop, nop, p8, p9]

    blocks = []
    for i in range(NT):
        blocks.append(prep_stages(i))
        for _ in range(SPN - 1):
            blocks.append([nop] * NPN)
    run_pipeline(blocks, NPN)
    A.release(m1)
    m1 = A.mark()
    ckT = A.get('nck', [S], BF16)
    cvT = A.get('ncvT', [S], BF16)
    tb = self.hn_tmp(A, 64, 1, 32)
    for tg in range(4):
        for j, dst in ((0, ckT), (1, cvT)):
            pb = ps[(tg * 2 + j) % 2]
            for k_i in range(8):
                self.mm(pb[:, :], wN[:, k_i, 512 + j * 128:640 + j * 128], hT[:, k_i, tg * 512:(tg + 1) * 512], k_i == 0, k_i == 7,
                        self.hk(4 * tg, 4 * tg + 4) + [(wN, 1)], [pb])
            self.cp('act' if j == 0 else 'dve', dst[:, tg * 512:(tg + 1) * 512], pb[:, :], [pb], [(dst, tg)])
    A.free_top(wN)
    w1 = [A.get('nw1%d' % j, [32, 128], BF16) for j in range(2)]
    w2 = [A.get('nw2%d' % j, [64], BF16) for j in range(2)]
    peT = A.get('npe', [64], BF16)
    gx = [A.get('ngx%d' % j, [128], F32) for j in range(4)]
    hid = A.get('nhid', [128], BF16)
    kc = A.get('nkcs', [64], BF16)
    for j, nm in enumerate(('cmp_wk1', 'cmp_wv1')):
        src = self.W[nm][l].rearrange('(l d) h -> d l h', d=64)
        self.dma('pool', w1[j][0:64, :, :], src, (), [(w1[j], 0)])
        self.dma('pool', w1[j][64:128, :, :], src, (), [(w1[j], 1)])
    for j, nm in enumerate(('cmp_wk2', 'cmp_wv2')):
        self.dma('pool', w2[j][:], self.W[nm][l], (), [w2[j]])
    self.dma('pool', peT[:], self.W['pekv'][l], (), [peT])
    self.dma('pool', cmpV[0:127, 0, 64:97], self.W['cb'][0:127, CB_OV:CB_OV + 33], (), [(cmpV, 'ov0')])
    self.dma('pool', cmpV[0:127, 1, 64:97], self.W['cb'][0:127, CB_OV:CB_OV + 33], (), [(cmpV, 'ov1')])
    cnt = 0
    for j, cT in ((0, ckT), (1, cvT)):
        for g in range(2):
            pg_ = slice(64 * g, 64 * g + 64)
            ph, po = ps[cnt % 2], ps[2 + cnt % 2]
            cnt += 1
            for l_ in range(32):
                self.mm(ph[:, 0:127], w1[j][pg_, l_, :], cT[pg_, l_:l_ + 16 * 126 + 1:16], l_ == 0, False, [cT, (w1[j], g)], [ph])
            for l_ in range(32):
                self.mm(ph[:, 0:127], w1[j][pg_, l_, :], peT[pg_, j * 32 + l_:j * 32 + l_ + 1].broadcast_to([64, 127]), False, l_ == 31,
                        [peT, (w1[j], g)], [ph])
            x, x2, u, th = [b_[:, 0:127] for b_ in gx]
            self.cp('act', x, ph[:, 0:127], [ph], [gx[0]])
            self.tt('dve', x2, x, x, ALU.mult, [gx[0]], [gx[1]])
            self.ts('dve', x2, x2, 0.044715, 1.0, ALU.mult, ALU.add, [gx[1]], [gx[1]])
            self.tt('dve', u, x, x2, ALU.mult, [gx[0], gx[1]], [gx[2]])
            self.act(th, u, AF.Tanh, [gx[2]], [gx[3]], scale=0.7978845608028654)
            self.ts('dve', th, th, 0.5, 0.5, ALU.mult, ALU.add, [gx[3]], [gx[3]])
            self.tt('dve', hid[:, 0:127], x, th, ALU.mult, [gx[0], gx[3]], [hid])
            self.mm(po[0:127, 0:64], hid[:, 0:127], w2[j][:], True, True, [hid, w2[j]], [po])
            if j == 0:
                self.headnorm_rope(po[0:127, 0:64].unsqueeze(1), 1, 64, 0, 32, gk[0:127, :], cfm[0:127, 1024:1056], cfm[0:127, 1056:1088],
                                   kc[0:127, :].unsqueeze(1), 1.0, tb, [po], [kc])
                pt = ps[4 + g]
                self.mm(pt[pg_, 0:127], kc[0:127, :], ident[0:127, 0:127], True, True, [kc, self.ident], [pt])
                self.cp('dve', kcT[pg_, g, 0:127], pt[pg_, 0:127], [pt], [(kcT, g)])
            else:
                self.cp('dve', cmpV[0:127, g, 0:64], po[0:127, 0:64], [po], [(cmpV, g)])
    A.release(m1)
    ct = A.get('nct', [1024], F32)
    self.dma('sp', ct[:], self.W['cf'][:, CF_KEEP:CF_KEEP + 1024], (), [ct])
    cmpb = A.get('ncb', [S], BF16)
    self.dma('pool', cmpb[:], self.W['cb'][:, CB_CMPB:CB_CMPB + S], (), [cmpb])
    E = A.get('nE', [S], BF16)
    self.dma('pool', E[:], self.W['cb'][:, CB_E:CB_E + S], (), [E])
    selt = [A.get('nsel%d' % j, [128], BF16) for j in range(4)]
    for j in range(4):
        self.memset('dve', selt[j][:], 0.0, [selt[j]])
    pTb = [A.get('npT%d' % j, [512], BF16) for j in range(3)]
    oacc = [A.get('noa%d' % j, [4, 64], F32) for j in range(2)]
    otmp = A.get('notmp', [4, 64], F32)
    ob = [A.get('nob%d' % j, [8, 64], BF16) for j in range(2)]
    sm = [A.get('nsm%d' % j, [160], F32) for j in range(2)]
    selb = [A.get('nselb%d' % j, [32], BF16) for j in range(4)]
    cnt = [0]
    NST = 14
    nop = lambda: None

    def nxt():
        b = cnt[0]
        cnt[0] += 1
        return ps[b % 3], pTb[b % 3]

    def ctx(idx):
        i, g = idx // 2, idx % 2
        s_ = sm[idx % 2]
        return dict(i=i, g=g, tc_=slice(i * 128, (i + 1) * 128), s_=s_, oa=oacc[idx % 2], sb_=selb[idx % 4], st_=selt[idx % 4],
                    ob_=ob[i % 2], den=s_[:, 0:4], rec=s_[:, 4:8], coef=s_[:, 8:12], imp=s_[:, 16:48], sc=s_[:, 48:80],
                    sc2=s_[:, 80:112], m8a=s_[:, 112:120], m8b=s_[:, 120:128], sk=[s_])

    def cmp_block(idx):
        c = ctx(idx)
        i, g, tc_, s_, sk = c['i'], c['g'], c['tc_'], c['s_'], c['sk']
        pb, pT = nxt()
        po = ps[3]

        def s0():
            self.mm(v3(pb[0:127, :], 4), kcT[:, g, 0:127], qT[:, :, tc_], True, False, [(kcT, g), (qT, i)], [pb])
            self.mm(v3(pb[0:127, :], 4), ident[0:127, 0:127], bc4(cmpb[0:127, tc_]), False, True, [self.ident, cmpb], [pb])

        def s1():
            self.act(pT[0:127, :], pb[0:127, :], AF.Exp, [pb], [pT])

        def s2():
            den, rec, coef, imp, sc, sc2, m8a, m8b = (c[n] for n in ('den', 'rec', 'coef', 'imp', 'sc', 'sc2', 'm8a', 'm8b'))
            oa, sb_, st_ = c['oa'], c['sb_'], c['st_']
            for cc in range(4):
                self.mm(po[:, cc * 97:(cc + 1) * 97], pT[0:127, cc * 128:(cc + 1) * 128], cmpV[0:127, g, :], cc == 0, True, [pT, cmpV], [po], skip=True)
            pov = v3(po[:, 0:388], 4)
            self.ts('dve', den, pov[:, :, 96], 1e-30, None, ALU.max, None, [po], sk)
            self.recip(rec, den, sk, sk)
            self.ts('dve', imp, pov[:, 0, 64:96], rec[:, 0:1], None, ALU.mult, None, [po] + sk, sk)
            for cc in range(1, 4):
                self.stt('dve', imp, pov[:, cc, 64:96], rec[:, cc:cc + 1], imp, ALU.mult, ALU.add, [po] + sk, sk)
            self.tt('dve', coef, rec, gts[:, i, 4 * g:4 * g + 4], ALU.mult, sk + [(gts, i)], sk)
            self.tt('dve', oa[:, :, :], pov[:, :, 0:64], coef.unsqueeze(2).broadcast_to([128, 4, 64]), ALU.mult, [po] + sk, [oa])
            self.tt('dve', sc, imp, ct[:, i * 32:(i + 1) * 32], ALU.mult, sk + [ct], sk)
            self.tt('dve', sc, sc, ct[:, 512 + i * 32:512 + (i + 1) * 32], ALU.add, sk + [ct], sk)
            self.P.op('dve', lambda e: e.max(out=m8a, in_=sc), sk, sk, strict=True)
            self.P.op('dve', lambda e: e.match_replace(out=sc2, in_to_replace=m8a, in_values=sc, imm_value=-1e30), sk, sk, strict=True)
            self.P.op('dve', lambda e: e.max(out=m8b, in_=sc2), sk, sk, strict=True)
            self.ts('dve', sb_[:, :], sc, m8b[:, 7:8], 1.0, ALU.is_ge, ALU.subtract, sk, [sb_])

        def s12():
            self.mm(ps[6][0:32, 0:128], c['sb_'][:, :], ident[:], True, True, [c['sb_'], self.ident], [ps[6]])

        def s13():
            self.cp('act', c['st_'][0:32, :], ps[6][0:32, 0:128], [ps[6]], [c['st_']])

        return [s0, s1, s2] + [nop] * (NST - 5) + [s12, s13]

    def att_block(idx, kind, kb, first, last):
        c = ctx(idx)
        i, g, tc_, s_, sk = c['i'], c['g'], c['tc_'], c['s_'], c['sk']
        pb, pT = nxt()
        kc_ = slice(kb * 128, (kb + 1) * 128)
        kT_, v_, pacc, goff = (kTs, vs, ps[4], 8) if kind == 'slc' else (kTw, vw, ps[5], 16)
        dg = (kb == i)
        far = (kind == 'win' and kb == i - 4)

        def s0():
            nb = (1 if kind == 'slc' else 0) + (1 if dg else 0) + (1 if far else 0)
            self.mm(v3(pb[:, :], 4), kT_[:, g, kc_], qT[:, :, tc_], True, nb == 0, [(kT_, kb), (qT, i)], [pb])
            if kind == 'slc':
                nb -= 1
                self.mm(v3(pb[:, :], 4), E[:, kc_], bc4(c['st_'][:, :]), False, nb == 0, [E, c['st_']], [pb])
            if dg:
                nb -= 1
                self.mm(v3(pb[:, :], 4), ident[:], bc4(self.nincl), False, nb == 0, [self.ident, self.cm], [pb])
            if far:
                nb -= 1
                self.mm(v3(pb[:, :], 4), ident[:], bc4(self.nfar), False, nb == 0, [self.ident, self.cm], [pb])

        def s1():
            self.act(pT[:, :], pb[:, :], AF.Exp, [pb], [pT])

        def s2():
            for cc in range(4):
                self.mm(pacc[:, cc * 65:(cc + 1) * 65], pT[:, cc * 128:(cc + 1) * 128], v_[:, kb, g, :], first and cc == 0, last,
                        [pT, (v_, kb)], [pacc], skip=True)
            if not last:
                return
            rec, coef, oa, ob_ = c['rec'], c['coef'], c['oa'], c['ob_']
            pav = v3(pacc[:, 0:260], 4)
            self.recip(rec, pav[:, :, 64], [pacc], sk)
            self.tt('dve', coef, rec, gts[:, i, goff + 4 * g:goff + 4 * g + 4], ALU.mult, sk + [(gts, i)], sk)
            self.tt('dve', otmp[:, :, :], pav[:, :, 0:64], coef.unsqueeze(2).broadcast_to([128, 4, 64]), ALU.mult, [pacc] + sk, [otmp])
            if kind == 'win':
                self.tt('dve', oa[:, :, :], oa[:, :, :], otmp[:, :, :], ALU.add, [oa, otmp], [oa])
            else:
                self.tt('dve', ob_[:, 4 * g:4 * g + 4, :], oa[:, :, :], otmp[:, :, :], ALU.add, [oa, otmp], [(ob_, g)])
                if g == 1:
                    pm2 = ps[7]
                    for p_ in range(4):
                        self.mm(pm2[:, p_ * 128:(p_ + 1) * 128], ob_[:, 2 * p_:2 * p_ + 2, :].rearrange('p h d -> p (h d)'), ident[:], p_ == 0, True,
                                [ob_, self.ident], [pm2], skip=True)
                    self.cp('act', oT[:, :, tc_], v3(pm2[:, :], 4), [pm2], [(oT, i)])

        return [s0, s1, s2] + [nop] * (NST - 3)

    blocks = [cmp_block(0)]
    for idx in range(2 * NT):
        i = idx // 2
        if idx + 1 < 2 * NT:
            blocks.append(cmp_block(idx + 1))
        kb0 = max(0, i - 4)
        for kb in range(kb0, i + 1):
            blocks.append(att_block(idx, 'win', kb, kb == kb0, kb == i))
        for kb in range(0, i + 1):
            blocks.append(att_block(idx, 'slc', kb, kb == 0, kb == i))
    run_pipeline(blocks, NST)
    A.release(m0)


K.nsa = nsa


def merge(self, l):
    A, P, hT, ps, X = self.A, self.P, self.hT, self.ps, self.X
    m0 = A.mark()
    yT = A.get('yT', [8, S], BF16)
    wg = [A.get('mwg%d' % j, [8, 3, 128], BF16) for j in range(2)]
    pw = [A.get('mpw%d' % j, [4, 3, 128], BF16) for j in range(1)]
    sg = [A.get('msg%d' % j, [512], BF16) for j in range(3)]
    tt_ = [A.get('mtt%d' % j, [512], F32) for j in range(2)]
    wv = self.W['w_in'][l].rearrange('(c p) f -> p c f', p=128)
    pj = [self.W[n][l].rearrange('(k p) d -> p k d', p=128) for n in ('proj_sb', 'proj_mla', 'proj_nsa')]
    oTs = [self.oT.get(n) for n in ('sb', 'mla', 'nsa')]
    cnt = 0
    for c in range(8):
        wg_, pw_ = wg[c % 2], pw[0]
        for b in range(3):
            self.dma('pool', wg_[:, :, b, :], wv[:, :, O_MG + b * 1024 + c * 128:O_MG + b * 1024 + (c + 1) * 128], (), [(wg_, b)])
            self.dma('pool', pw_[:, :, b, :], pj[b][:, :, c * 128:(c + 1) * 128], (), [(pw_, b)])
        for tg in range(4):
            tcs = slice(tg * 512, (tg + 1) * 512)
            gb, pb = [], []
            for b in range(3):
                g_ = ps[cnt % 8]
                cnt += 1
                for k in range(8):
                    self.mm(g_[:, :], wg_[:, k, b, :], hT[:, k, tcs], k == 0, k == 7, self.hk(4 * tg, 4 * tg + 4) + [(wg_, b)], [g_])
                self.act(sg[b][:], g_[:, :], AF.Sigmoid, [g_], [sg[b]])
            for b in range(3):
                p_ = ps[cnt % 8]
                cnt += 1
                pb.append(p_)
                if oTs[b] is None:
                    continue
                for kk in range(4):
                    self.mm(p_[:, :], pw_[:, kk, b, :], oTs[b][:, kk, tcs], kk == 0, kk == 3, [oTs[b], (pw_, b)], [p_])
            act = [b for b in range(3) if oTs[b] is not None]
            t0, t1 = tt_
            self.tt('dve', t0[:], sg[act[0]][:], pb[act[0]][:, :], ALU.mult, [sg[act[0]], pb[act[0]]], [t0])
            for b in act[1:]:
                self.tt('dve', t1[:], sg[b][:], pb[b][:, :], ALU.mult, [sg[b], pb[b]], [t1])
                self.tt('dve', t0[:], t0[:], t1[:], ALU.add, [t0, t1], [t0])
            self.cp('dve', yT[:, c, tcs], t0[:], [t0], [(yT, (c, tg))])
    A.free_top(hT)
    self.hT_freed = True
    wo = A.get_top('wout', [8, D], BF16)
    wov = self.W['w_out'][l].rearrange('(k p) d -> p k d', p=128)
    for h2 in range(2):
        self.dma('pool', wo[:, 4 * h2:4 * h2 + 4, :], wov[:, 4 * h2:4 * h2 + 4, :], (), [(wo, h2)])
    for i in range(NT):
        for dh in range(2):
            p_ = ps[cnt % 8]
            cnt += 1
            for k in range(8):
                self.mm(p_[:, :], yT[:, k, i * 128:(i + 1) * 128], wo[:, k, dh * 512:(dh + 1) * 512], k == 0, k == 7,
                        [(yT, (k, i // 4)), (wo, k // 4)], [p_])
            self.tt('dve', X[:, i, dh * 512:(dh + 1) * 512], X[:, i, dh * 512:(dh + 1) * 512], p_[:, :], ALU.add, [p_, (X, i)], [(X, i)])
    A.free_top(wo)
    A.release(m0)


K.merge = merge
```

```python
import numpy as np
import concourse.bass as bass
import concourse.mybir as mybir

F32 = mybir.dt.float32
BF16 = mybir.dt.bfloat16
U8 = mybir.dt.uint8
I32 = mybir.dt.int32
AF = mybir.ActivationFunctionType
ALU = mybir.AluOpType
AX = mybir.AxisListType

ENGS = ['pe', 'act', 'dve', 'pool', 'sp']
DSIZE = {F32: 4, BF16: 2, U8: 1, I32: 4}
SAME_ENG_SYNC = True
SAME_ENG_GAP = 1
DMA_K = {'sp': 12, 'pool': 12, 'act': 6}


class Buf:
    __slots__ = ('name', 'ap', 'state', 'space', 'off', 'size')

    def __init__(self, name, ap, space='sb', off=0, size=0):
        self.name = name
        self.ap = ap
        self.state = {}
        self.space = space
        self.off = off
        self.size = size

    def __getitem__(self, k):
        return self.ap[k]


class Op:
    __slots__ = ('eng', 'fn', 'idx', 'eidx', 'waits', 'inc', 'tick', 'is_dma', 'dsem', 'dval', 'vc', 'vcd', 'pewr')


class Prog:
    def __init__(self, nc, arena_bytes=200 * 1024):
        self.nc = nc
        self.ops = []
        self.eops = {e: [] for e in ENGS}
        self.known = {e: {d: -1 for d in ENGS} for e in ENGS}
        self.known_dma = {e: {} for e in ENGS}
        self.ndma = {e: 0 for e in ENGS}
        self.dma_hist = {e: [] for e in ENGS}
        self.arena_bytes = arena_bytes
        self.arena = nc.alloc_sbuf_tensor('arena', [128, arena_bytes], U8).ap()
        self.psum = [nc.alloc_psum_tensor('psb%d' % i, [128, 512], F32).ap() for i in range(8)]
        self.sb_top = 0
        self.freed = []
        self.live = {}
        self.all_dma = []
        self.spacer = {}

    def sb_at(self, name, off, shape, dtype, parts=128):
        n = int(np.prod(shape)) * DSIZE[dtype]
        assert off % 4 == 0
        assert off + n <= self.arena_bytes, (name, off, n, self.arena_bytes)
        ap = self.arena[0:parts, off:off + n].bitcast(dtype)
        if len(shape) > 1:
            names = ' '.join('d%d' % i for i in range(len(shape)))
            kw = {'d%d' % i: shape[i] for i in range(len(shape))}
            ap = ap.rearrange('p (%s) -> p %s' % (names, names), **kw)
        b = Buf(name, ap, 'sb', off, n)
        inh_w = []
        for (fo, fs, ops_) in self.freed:
            if fo < off + n and off < fo + fs:
                inh_w.extend(ops_)
        if inh_w:
            b.state[None] = [None, list(inh_w), list(inh_w)]
        return b

    def free(self, b):
        s = []
        for k, st in b.state.items():
            if st[0] is not None:
                s.append(st[0])
            s.extend(st[1])
            if len(st) > 2:
                s.extend(st[2])
        self.freed.append((b.off, b.size, self._compress(s)))
        if len(self.freed) > 400:
            self.freed = self.freed[-400:]

    def psb(self, bank, name=None):
        return Buf(name or ('ps%d' % bank), self.psum[bank], 'ps')

    def dram(self, name, ap):
        return Buf(name, ap, 'dram')

    def _compress(self, lst):
        best = {}
        out = []
        for i in set(lst):
            o = self.ops[i]
            if o.is_dma:
                out.append(i)
            else:
                if o.eng not in best or best[o.eng] < i:
                    best[o.eng] = i
        out.extend(best.values())
        return out

    def _deps_for(self, region, is_write, opidx, eng=None):
        if isinstance(region, tuple):
            buf, key = region
        else:
            buf, key = region, None
        st = buf.state
        deps = []
        psum = (buf.space == 'ps')
        if key is None:
            keys = list(st.keys())
        else:
            keys = [k for k in (key, None) if k in st]
        for k in keys:
            e = st[k]
            if e[0] is not None:
                deps.append(e[0])
            if is_write:
                deps.extend(e[1])
            elif psum:
                deps.extend(r for r in e[1] if self.ops[r].eng != eng)
            if len(e) > 2:
                deps.extend(e[2])
        return deps

    def _update(self, region, is_write, opidx):
        if isinstance(region, tuple):
            buf, key = region
        else:
            buf, key = region, None
        st = buf.state
        if key is None:
            if is_write:
                buf.state = {None: [opidx, []]}
            else:
                if None not in st:
                    st[None] = [None, []]
                for k in st:
                    st[k][1].append(opidx)
                    if len(st[k][1]) > 24:
                        st[k][1] = self._compress(st[k][1])
        else:
            if is_write:
                st[key] = [opidx, []]
            else:
                if key not in st:
                    st[key] = [None, []]
                st[key][1].append(opidx)
                if len(st[key][1]) > 24:
                    st[key][1] = self._compress(st[key][1])

    def _mkop(self, eng, fn, dma):
        o = Op()
        o.eng = eng
        o.fn = fn
        o.idx = len(self.ops)
        o.eidx = len(self.eops[eng])
        o.is_dma = dma
        o.inc = dma
        o.tick = None
        o.pewr = False
        o.waits = []
        return o

    def op(self, eng, fn, reads=(), writes=(), dma=False, pewr=False, strict=False):
        deps = []
        for r in reads:
            deps.extend(self._deps_for(r, False, None, eng))
        for w in writes:
            deps.extend(self._deps_for(w, True, None, eng))
        if eng in self.spacer and SAME_ENG_SYNC and not strict:
            last = len(self.eops[eng]) - 1
            for d in set(deps):
                od = self.ops[d]
                if (not od.is_dma) and od.eng == eng and od.eidx == last:
                    sp = self._mkop(eng, self.spacer[eng], False)
                    sp.vc = dict(self.known[eng])
                    sp.vcd = dict(self.known_dma[eng])
                    self.ops.append(sp)
                    self.eops[eng].append(sp)
                    break
        o = self._mkop(eng, fn, dma)
        if dma:
            k = DMA_K[eng]
            i = self.ndma[eng]
            o.dsem = (eng, i % k)
            o.dval = 16 * (i // k + 1)
            if i >= k:
                deps.append(self.dma_hist[eng][i - k])
            self.ndma[eng] += 1
            self.dma_hist[eng].append(o.idx)
            self.all_dma.append(o.idx)
        kn = self.known[eng]
        kd = self.known_dma[eng]
        for d in sorted(set(deps)):
            od = self.ops[d]
            if od.is_dma:
                if kd.get(od.dsem, 0) >= od.dval:
                    continue
                o.waits.append(d)
                kd[od.dsem] = od.dval
                for e2, v in od.vc.items():
                    if kn[e2] < v:
                        kn[e2] = v
                for k2, v in od.vcd.items():
                    if kd.get(k2, 0) < v:
                        kd[k2] = v
            else:
                if od.eng == eng:
                    if eng == 'pe' or not SAME_ENG_SYNC:
                        continue
                    if eng in self.spacer and not strict:
                        continue
                if kn[od.eng] >= d:
                    continue
                o.waits.append(d)
                od.inc = True
                for e2, v in od.vc.items():
                    if kn[e2] < v:
                        kn[e2] = v
                if kn[od.eng] < d:
                    kn[od.eng] = d
                for k2, v in od.vcd.items():
                    if kd.get(k2, 0) < v:
                        kd[k2] = v
        o.vc = dict(kn)
        o.vcd = dict(kd)
        self.ops.append(o)
        self.eops[eng].append(o)
        for r in reads:
            self._update(r, False, o.idx)
        for w in writes:
            self._update(w, True, o.idx)
        return o

    def emit(self):
        nc = self.nc
        fin_deps = list(self.all_dma[-64:])
        o = Op()
        o.eng = 'sp'; o.fn = None; o.idx = len(self.ops); o.is_dma = False; o.inc = False
        o.tick = None; o.waits = []; o.pewr = False
        kd = self.known_dma['sp']
        for q in DMA_K:
            for d in self.dma_hist[q][-DMA_K[q]:]:
                od = self.ops[d]
                if kd.get(od.dsem, 0) < od.dval:
                    o.waits.append(d)
        o.vc = {}; o.vcd = {}
        self.ops.append(o)
        self.eops['sp'].append(o)

        for e in ENGS:
            t = 0
            for op in self.eops[e]:
                if not op.is_dma and op.inc:
                    t += 1
                    op.tick = t
        import contextlib
        with contextlib.ExitStack() as es:
            esem = {e: es.enter_context(nc.semaphore('s_' + e)) for e in ENGS}
            dsem = {}
            for q, k in DMA_K.items():
                for j in range(k):
                    dsem[(q, j)] = es.enter_context(nc.semaphore('d_%s%d' % (q, j)))
            block = es.enter_context(nc.Block())
            ops = self.ops

            def run(e, h):
                for op in self.eops[e]:
                    for d in op.waits:
                        od = ops[d]
                        if od.is_dma:
                            h.wait_ge(dsem[od.dsem], od.dval)
                        else:
                            h.wait_ge(esem[od.eng], od.tick)
                    if op.fn is None:
                        continue
                    ins = op.fn(h)
                    if op.is_dma:
                        ins.then_inc(dsem[op.dsem], 16)
                    elif op.inc:
                        ins.then_inc(esem[e], 1)

            @block.tensor
            def _(h):
                run('pe', h)

            @block.scalar
            def _(h):
                run('act', h)

            @block.vector
            def _(h):
                run('dve', h)

            @block.gpsimd
            def _(h):
                run('pool', h)

            @block.sync
            def _(h):
                run('sp', h)

    def stats(self):
        s = {e: len(self.eops[e]) for e in ENGS}
        w = {e: sum(len(o.waits) for o in self.eops[e]) for e in ENGS}
        inc = {e: sum(1 for o in self.eops[e] if o.inc) for e in ENGS}
        return s, w, inc
import math, os
from concourse.bass_utils import run_bass_kernel_spmd

S = 2048
D = 1024
DFF = 2816
NT = 16
NL = 2
EPS = 1e-6
ARENA = 207 * 1024
NEG = -30000.0
N_IN = 6328
NPV = 352
PV_GFF1, PV_GMIX, PV_GFF2, PV_QN, PV_KVN, PV_GQ, PV_GK, PV_NQ, PV_NK = 0, 8, 16, 24, 26, 27, 123, 219, 283
O_SBQ, O_SBK, O_SBV, O_CQ, O_CKV, O_KR, O_NQ = 0, 512, 1024, 1536, 1792, 1920, 1952
O_CK, O_CV, O_SK, O_SV, O_WK, O_WV, O_NG, O_MG = 2464, 2592, 2720, 2848, 2976, 3104, 3232, 3256
CF_MC, CF_MS, CF_NC, CF_NS, CF_CC, CF_CS, CF_KEEP, CF_ADD, NCF = 0, 256, 512, 1024, 1536, 1568, 1600, 2112, 2624
CB_ID, CB_NSTRICT, CB_NINCL, CB_NFAR, CB_TRI, CB_NONES, CB_ONES, CB_CMPB, CB_E, CB_OV, NCB = 0, 128, 256, 384, 512, 640, 768, 896, 2944, 4992, 5056


class Alloc:
    def __init__(self, P, base, limit):
        self.P, self.base, self.limit, self.top, self.bufs = P, base, limit, base, []

    def get(self, name, shape, dtype, parts=128):
        n = int(np.prod(shape)) * DSIZE[dtype]
        off = (self.top + 31) // 32 * 32
        assert off + n <= self.limit, ('SBUF phase overflow', name, off, n, self.limit)
        b = self.P.sb_at(name, off, shape, dtype, parts)
        self.top = off + n
        self.peak = max(getattr(self, 'peak', 0), self.top)
        if os.environ.get('MEMDBG'):
            print('ALLOC %-10s off=%6d n=%6d top=%6d limit=%6d slack=%6d' % (name, off, n, self.top, self.limit, self.limit - self.top))
        self.bufs.append(b)
        return b

    def mark(self):
        return (self.top, len(self.bufs))

    def release(self, mark=None):
        top, nb = mark if mark is not None else (self.base, 0)
        for b in self.bufs[nb:]:
            self.P.free(b)
        self.bufs = self.bufs[:nb]
        self.top = top


class K:
    def __init__(self, nc, depth=NL, debug=None, stages=None):
        self.nc = nc
        self.depth = depth
        self.debug = debug
        self.stages = stages
        P = self.P = Prog(nc, arena_bytes=ARENA)
        dt = lambda name, shape, kind="ExternalInput": nc.dram_tensor(name, shape, F32, kind=kind).ap()
        self.x = dt('x', [S, D])
        self.out = dt('out', [S, D], "ExternalOutput")
        W = self.W = {}
        for name, shape in [('ffn1_wi', [NL, D, 2 * DFF]), ('ffn1_wo', [NL, DFF, D]), ('w_in', [NL, D, N_IN]),
                            ('mla_w_uq', [NL, 256, 768]), ('mla_w_ukv', [NL, 128, 1024]),
                            ('cmp_wk1', [NL, 2048, 128]), ('cmp_wk2', [NL, 128, 64]), ('cmp_wv1', [NL, 2048, 128]),
                            ('cmp_wv2', [NL, 128, 64]), ('proj_sb', [NL, 512, D]), ('proj_mla', [NL, 512, D]),
                            ('proj_nsa', [NL, 512, D]), ('w_out', [NL, D, D]), ('ffn2_wi', [NL, D, 2 * DFF]),
                            ('ffn2_wo', [NL, DFF, D]), ('pvec', [NL, 128, NPV]), ('pekv', [NL, 128, 64]),
                            ('cf', [128, NCF]), ('cb', [128, NCB])]:
            W[name] = dt(name, shape)
        if debug:
            self.dbg = dt('dbg', list(debug), "ExternalOutput")
        self.X = P.sb_at('X', 0, [NT, D], F32)
        cbase = NT * D * 4
        self.ident = P.sb_at('ident', cbase, [128], BF16)
        self.pv = P.sb_at('pv', cbase + 256, [NPV], F32)
        self.rstd = P.sb_at('rstd', cbase + 256 + NPV * 4, [NT], F32)
        self.small = P.sb_at('small', cbase + 256 + NPV * 4 + 64, [64], F32)
        self.A = Alloc(P, cbase + 4096, ARENA)
        self.ps = [P.psb(i) for i in range(8)]
        self.cnt = 0
        sm = self.small
        P.op('dve', lambda e: e.memset(sm[:, :], 0.0), (), [sm])
        if os.environ.get('SPACER'): P.spacer['dve'] = lambda e: e.memset(sm[:, 32:40], 0.0)
        if os.environ.get('SPACER'): P.spacer['act'] = lambda e: e.activation(out=sm[:, 40:48], in_=sm[:, 48:56], func=AF.Copy)

    def mm(self, out, lhsT, rhs, start, stop, R, Wr, skip=False):
        self.P.op('pe', lambda e: e.matmul(out, lhsT=lhsT, rhs=rhs, start=start, stop=stop, skip_group_check=skip), R, Wr)

    def tr(self, out, in_, ident, R, Wr):
        self.P.op('pe', lambda e: e.transpose(out=out, in_=in_, identity=ident), R, Wr)

    def act(self, out, in_, func, R, Wr, bias=None, scale=None, accum=None):
        kw = {}
        if bias is not None:
            kw['bias'] = bias
        if scale is not None:
            kw['scale'] = scale
        if accum is not None:
            kw['accum_out'] = accum
        strict = (bias is not None and not isinstance(bias, (int, float))) or (scale is not None and not isinstance(scale, (int, float)))
        self.P.op('act', lambda e: e.activation(out=out, in_=in_, func=func, **kw), R, Wr, strict=strict)

    def tt(self, eng, out, in0, in1, op, R, Wr):
        self.P.op(eng, lambda e: e.tensor_tensor(out=out, in0=in0, in1=in1, op=op), R, Wr)

    def ts(self, eng, out, in0, s1, s2, op0, op1, R, Wr, strict=False):
        strict = strict or not isinstance(s1, (int, float)) or not (s2 is None or isinstance(s2, (int, float)))
        if op1 is None:
            self.P.op(eng, lambda e: e.tensor_scalar(out=out, in0=in0, scalar1=s1, scalar2=None, op0=op0), R, Wr, strict=strict)
        else:
            self.P.op(eng, lambda e: e.tensor_scalar(out=out, in0=in0, scalar1=s1, scalar2=s2, op0=op0, op1=op1), R, Wr, strict=strict)

    def stt(self, eng, out, in0, scalar, in1, op0, op1, R, Wr):
        strict = not isinstance(scalar, (int, float))
        self.P.op(eng, lambda e: e.scalar_tensor_tensor(out=out, in0=in0, scalar=scalar, in1=in1, op0=op0, op1=op1), R, Wr, strict=strict)

    def cp(self, eng, out, in_, R, Wr):
        if eng == 'act':
            self.P.op('act', lambda e: e.copy(out=out, in_=in_), R, Wr)
        else:
            self.P.op(eng, lambda e: e.tensor_copy(out=out, in_=in_), R, Wr)

    def dma(self, q, out, in_, R, Wr):
        self.P.op(q, lambda e: e.dma_start(out=out, in_=in_), R, Wr, dma=True)

    def memset(self, eng, ap, val, Wr):
        self.P.op(eng, lambda e: e.memset(ap, val), (), Wr)

    def recip(self, out, in_, R, Wr):
        self.P.op('dve', lambda e: e.reciprocal(out=out, in_=in_), R, Wr)

    def load_x(self):
        xv = self.x.rearrange('(i p) d -> p i d', p=128)
        for c in range(4):
            self.dma('sp', self.X[:, 4 * c:4 * c + 4, :], xv[:, 4 * c:4 * c + 4, :], (), [(self.X, i) for i in range(4 * c, 4 * c + 4)])
        self.dma('pool', self.ident[:], self.W['cb'][:, CB_ID:CB_ID + 128], (), [self.ident])

    def store_x(self):
        ov = self.out.rearrange('(i p) d -> p i d', p=128)
        od = self.P.dram('out', self.out)
        for c in range(4):
            self.dma('sp', ov[:, 4 * c:4 * c + 4, :], self.X[:, 4 * c:4 * c + 4, :], [(self.X, i) for i in range(4 * c, 4 * c + 4)], [(od, c)])

    def load_pv(self, l):
        self.dma('sp', self.pv[:], self.W['pvec'][l], (), [self.pv])

    def norm_tile(self, i, gcol, hT, tcol, hkey, tmp, bank, save_rstd=True, use_saved=False):
        X = self.X
        ss, junk, xn = tmp
        k = self.cnt
        self.cnt += 1
        xn_ = xn[k % 2]
        rs = self.rstd[:, i:i + 1]
        if not use_saved:
            s_ = ss[:, (k % 2) * 2:(k % 2) * 2 + 1]
            sd = ss[:, (k % 2) * 2 + 1:(k % 2) * 2 + 2]
            sk = (ss, k % 2)
            self.memset('dve', s_, 0.0, [sk])
            self.act(junk[:], X[:, i, :], AF.Square, [(X, i), sk], [junk, sk], accum=s_)
            self.act(sd, s_, AF.Sqrt, [sk], [sk], scale=1.0 / D, bias=EPS)
            self.recip(rs, sd, [sk], [(self.rstd, i)])
        self.act(xn_[:], X[:, i, :], AF.Copy, [(X, i), (self.rstd, i)], [xn_], scale=rs)
        pb = self.ps[bank]
        pbv = pb.ap.bitcast(BF16)
        for c in range(8):
            self.tr(pbv[:, c * 128:(c + 1) * 128], xn_[:, c * 128:(c + 1) * 128], self.ident[:], [xn_, self.ident], [pb])
        self.tt('dve', hT[:, 0:8, tcol:tcol + 128], pbv[:, :].rearrange('p (c t) -> p c t', c=8),
                self.pv[:, gcol:gcol + 8].unsqueeze(2).broadcast_to([128, 8, 128]), ALU.mult, [pb, self.pv], [(hT, hkey)])

    def norm_tmp(self):
        A = self.A
        ss = A.get('ss', [4], F32)
        junk = A.get('junk', [D], BF16)
        xn = [A.get('xn%d' % j, [D], BF16) for j in range(2)]
        return (ss, junk, xn)

    def ffn(self, l, which):
        A, P, X = self.A, self.P, self.X
        A.release()
        wi = self.W['ffn%d_wi' % which][l].rearrange('(c p) f -> p c f', p=128)
        wo = self.W['ffn%d_wo' % which][l].rearrange('(j p) d -> p j d', p=128)
        gcol = PV_GFF1 if which == 1 else PV_GFF2
        wo_sb = A.get('wo_sb', [22, D], BF16)
        hTs = [A.get('hT%d' % j, [8, 1024], BF16) for j in range(2)]
        uT = A.get('uT', [22, 1024], BF16)
        wib = [A.get('wib%d' % j, [8, 256], BF16) for j in range(2)]
        tmp = self.norm_tmp()
        sl = [A.get('sl%d' % j, [512], F32) for j in range(2)]
        it = 0
        for ti in range(8):
            self.norm_tile(ti, gcol, hTs[0], ti * 128, ti, tmp, 4 + (ti % 2))
        for hf in range(2):
            hT = hTs[hf]
            for j in range(22):
                if hf == 0 and j % 2 == 1 and j // 2 < 8:
                    ti = j // 2
                    self.norm_tile(8 + ti, gcol, hTs[1], ti * 128, ti, tmp, 4 + (ti % 2))
                wb = wib[j % 2]
                self.dma('pool', wb[:, :, 0:128], wi[:, :, j * 128:(j + 1) * 128], (), [(wb, 0)])
                self.dma('pool', wb[:, :, 128:256], wi[:, :, DFF + j * 128:DFF + (j + 1) * 128], (), [(wb, 1)])
                if hf == 0 and j % 2 == 0:
                    self.dma('pool', wo_sb[:, j:j + 2, :], wo[:, j:j + 2, :], (), [(wo_sb, j), (wo_sb, j + 1)])
                for tg in range(2):
                    pa, pbk = self.ps[(it % 2) * 2], self.ps[(it % 2) * 2 + 1]
                    s_ = sl[it % 2]
                    it += 1
                    hk = [(hT, 4 * tg + a) for a in range(4)]
                    for k in range(8):
                        self.mm(pa[:, :], wb[:, k, 0:128], hT[:, k, tg * 512:(tg + 1) * 512], k == 0, k == 7, hk + [(wb, 0)], [pa])
                    for k in range(8):
                        self.mm(pbk[:, :], wb[:, k, 128:256], hT[:, k, tg * 512:(tg + 1) * 512], k == 0, k == 7, hk + [(wb, 1)], [pbk])
                    self.act(s_[:], pa[:, :], AF.Silu, [pa], [s_])
                    self.tt('dve', uT[:, j, tg * 512:(tg + 1) * 512], s_[:], pbk[:, :], ALU.mult, [s_, pbk], [(uT, (j, tg))])
            for ti in range(8):
                i = 8 * hf + ti
                for dh in range(2):
                    pb = self.ps[4 + (2 * ti + dh) % 4]
                    for j in range(22):
                        self.mm(pb[:, :], uT[:, j, ti * 128:(ti + 1) * 128], wo_sb[:, j, dh * 512:(dh + 1) * 512], j == 0, j == 21,
                                [(uT, (j, ti // 4)), (wo_sb, j)], [pb])
                    self.stt('dve', X[:, i, dh * 512:(dh + 1) * 512], pb[:, :], 0.5, X[:, i, dh * 512:(dh + 1) * 512], ALU.mult, ALU.add,
                             [pb, (X, i)], [(X, i)])
        A.release()

    def build(self):
        self.load_x()
        for l in range(self.depth):
            self.load_pv(l)
            st = self.stages or ('ffn1', 'mix', 'ffn2')
            if 'ffn1' in st:
                self.ffn(l, 1)
            if 'mix' in st:
                self.mixer(l)
            if 'ffn2' in st:
                self.ffn(l, 2)
        self.store_x()
        self.P.emit()


def host_consts():
    cf = np.zeros((128, NCF), np.float32)
    p = np.arange(128)[:, None]
    pos = (np.arange(NT)[None, :] * 128 + p).astype(np.float32)

    def ropetab(d, posv):
        half = d // 2
        inv = np.exp(np.float32(-math.log(10000.0)) * np.arange(half, dtype=np.float32) * np.float32(2.0 / d)).astype(np.float32)
        ang = (posv[..., None].astype(np.float32) * inv).astype(np.float32)
        return np.cos(ang).astype(np.float32), np.sin(ang).astype(np.float32)
    c, s = ropetab(32, pos)
    cf[:, CF_MC:CF_MC + 256] = c.reshape(128, 256)
    cf[:, CF_MS:CF_MS + 256] = s.reshape(128, 256)
    c, s = ropetab(64, pos)
    cf[:, CF_NC:CF_NC + 512] = c.reshape(128, 512)
    cf[:, CF_NS:CF_NS + 512] = s.reshape(128, 512)
    ends = (np.arange(128) * 16 + 31).astype(np.float32)
    c, s = ropetab(64, ends)
    cf[:, CF_CC:CF_CC + 32] = c
    cf[:, CF_CS:CF_CS + 32] = s
    blk = np.arange(32)[None, None, :]
    cur = (pos // 64).astype(np.int64)[:, :, None]
    forced = (blk == 0) | (blk == cur) | (blk == cur - 1)
    fut = blk > cur
    keep = (~forced) & (~fut)
    add = np.where(fut, -1.0, np.where(forced, 1e3, 0.0))
    cf[:, CF_KEEP:CF_KEEP + 512] = keep.astype(np.float32).reshape(128, 512)
    cf[:, CF_ADD:CF_ADD + 512] = add.astype(np.float32).reshape(128, 512)
    cb = np.zeros((128, NCB), np.float32)
    a = np.arange(128)[:, None]
    b = np.arange(128)[None, :]
    cb[:, CB_ID:CB_ID + 128] = (a == b)
    cb[:, CB_NSTRICT:CB_NSTRICT + 128] = np.where(a < b, 0.0, NEG)
    cb[:, CB_NINCL:CB_NINCL + 128] = np.where(a <= b, 0.0, NEG)
    cb[:, CB_NFAR:CB_NFAR + 128] = np.where(a > b, 0.0, NEG)
    cb[:, CB_TRI:CB_TRI + 128] = np.where(a >= b, -1.0, 0.0)
    cb[:, CB_NONES:CB_NONES + 128] = -1.0
    cb[:, CB_ONES:CB_ONES + 128] = 1.0
    n = np.arange(128)[:, None]
    t = np.arange(S)[None, :]
    cb[:, CB_CMPB:CB_CMPB + S] = np.where(16 * n + 31 <= t, 0.0, NEG)
    j = np.arange(128)[:, None]
    cb[:, CB_E:CB_E + S] = (j == (t // 64)) * 30000.0
    c0 = np.arange(128)[:, None] * 16
    s0 = np.arange(32)[None, :] * 64
    ov = np.clip(np.minimum(c0 + 32, s0 + 64) - np.maximum(c0, s0), 0, None) / 32.0
    cb[:, CB_OV:CB_OV + 32] = ov
    cb[:, CB_OV + 32] = 1.0
    return cf, cb


def host_pvec(inp):
    pv = np.zeros((NL, 128, NPV), np.float32)
    for l in range(NL):
        pv[l, :, PV_GFF1:PV_GFF1 + 8] = inp['ffn1_norm'][l].reshape(8, 128).T
        pv[l, :, PV_GMIX:PV_GMIX + 8] = inp['mix_norm'][l].reshape(8, 128).T
        pv[l, :, PV_GFF2:PV_GFF2 + 8] = inp['ffn2_norm'][l].reshape(8, 128).T
        pv[l, :, PV_QN:PV_QN + 2] = inp['mla_q_norm'][l].reshape(2, 128).T
        pv[l, :, PV_KVN:PV_KVN + 1] = inp['mla_kv_norm'][l].reshape(1, 128).T
        pv[l, :, PV_GQ:PV_GQ + 96] = inp['mla_qk_gain_q'][l][None, :]
        pv[l, :, PV_GK:PV_GK + 96] = inp['mla_qk_gain_k'][l][None, :]
        pv[l, :, PV_NQ:PV_NQ + 64] = inp['nsa_q_gain'][l][None, :]
        pv[l, :, PV_NK:PV_NK + 64] = inp['nsa_k_gain'][l][None, :]
    pe = np.zeros((NL, 128, 64), np.float32)
    for l in range(NL):
        pe[l, :, 0:32] = np.tile(inp['cmp_pos_k'][l].T, (2, 1))
        pe[l, :, 32:64] = np.tile(inp['cmp_pos_v'][l].T, (2, 1))
    return pv, pe


_CACHE = {}


def make_maps(inputs, ncores=8):
    f32 = lambda a: np.ascontiguousarray(np.asarray(a, dtype=np.float32))
    inp = {k: f32(v) for k, v in inputs.items()}
    cf, cb = host_consts()
    pv, pe = host_pvec(inp)
    shared = {k: inp[k] for k in ['ffn1_wi', 'ffn1_wo', 'w_in', 'mla_w_uq', 'mla_w_ukv', 'cmp_wk1', 'cmp_wk2', 'cmp_wv1', 'cmp_wv2',
                                  'proj_sb', 'proj_mla', 'proj_nsa', 'w_out', 'ffn2_wi', 'ffn2_wo']}
    shared.update({'pvec': pv, 'pekv': pe, 'cf': cf, 'cb': cb})
    maps = []
    for c in range(ncores):
        m = dict(shared)
        m['x'] = np.ascontiguousarray(inp['x'][c])
        maps.append(m)
    return maps


def kernel(**inputs):
    if 'nc' not in _CACHE:
        nc = bass.Bass("TRN2", target_bir_lowering=False)
        K(nc).build()
        _CACHE['nc'] = nc
    nc = _CACHE['nc']
    maps = make_maps(inputs, 8)
    res = run_bass_kernel_spmd(nc, maps, core_ids=list(range(8)))
    return np.stack([np.asarray(r['out'], dtype=np.float32) for r in res.results], axis=0)


def _alloc_top(self, name, shape, dtype, parts=128):
    n = int(np.prod(shape)) * DSIZE[dtype]
    off = (self.limit - n) // 32 * 32
    assert off >= self.top, ('SBUF phase overflow (top)', name, off, self.top)
    b = self.P.sb_at(name, off, shape, dtype, parts)
    if not hasattr(self, 'tops'):
        self.tops = []
    self.tops.append((b, self.limit))
    self.limit = off
    return b


def _free_top(self, b):
    tb_, prev = self.tops.pop()
    assert tb_ is b, 'top allocations must be freed LIFO'
    self.P.free(b)
    self.limit = prev


Alloc.get_top = _alloc_top
Alloc.free_top = _free_top


def bc4(ap, n=4):
    return ap.unsqueeze(1).broadcast_to([ap.shape[0], n, ap.shape[1]])


def v3(ap, c):
    return ap.rearrange('p (c t) -> p c t', c=c)


def load_w_in(self, l, c0, n, buf, key=None):
    wv = self.W['w_in'][l].rearrange('(c p) f -> p c f', p=128)
    self.dma('pool', buf[:, :, 0:n], wv[:, :, c0:c0 + n], (), [buf if key is None else (buf, key)])


K.load_w_in = load_w_in


def mixer(self, l):
    A, P = self.A, self.P
    A.release()
    br = self.branches if hasattr(self, 'branches') else ('nsa', 'sb', 'mla')
    self.hT_freed = False
    hT = A.get_top('hT', [8, S], BF16)
    mk0 = A.mark()
    tmp = self.norm_tmp()
    for i in range(NT):
        self.norm_tile(i, PV_GMIX, hT, i * 128, i, tmp, 6 + (i % 2))
    A.release(mk0)
    self.hT = hT
    self.hk = lambda t0, t1: [(hT, i) for i in range(t0, t1)]
    cm = self.cm = A.get('cm', [768], BF16)
    self.dma('pool', cm[:], self.W['cb'][:, CB_NSTRICT:CB_NSTRICT + 768], (), [cm])
    self.nstrict, self.nincl, self.nfar = cm[:, 0:128], cm[:, 128:256], cm[:, 256:384]
    self.tri, self.nones, self.ones = cm[:, 384:512], cm[:, 512:640], cm[:, 640:768]
    self.oT = {}
    if 'nsa' in br:
        self.oT['nsa'] = A.get('oT_nsa', [4, S], BF16)
        self.nsa(l)
    if 'sb' in br:
        self.oT['sb'] = A.get('oT_sb', [4, S], BF16)
        self.sb(l)
    if 'mla' in br:
        self.oT['mla'] = A.get('oT_mla', [4, S], BF16)
        self.mla(l)
    if self.debug:
        for bi, b in enumerate(('sb', 'mla', 'nsa')):
            if b in self.oT:
                self.dma('pool', self.dbg[bi], self.oT[b][:, :, :], [self.oT[b]], [(self.P.dram('dbg', self.dbg), bi)])
    if getattr(self, 'hT_freed', False) and not getattr(self, 'skip_merge', False):
        hT = self.hT = A.get_top('hT', [8, S], BF16)
        self.hk = lambda t0, t1: [(hT, i) for i in range(t0, t1)]
        mk1 = A.mark()
        tmp = self.norm_tmp()
        for i in range(NT):
            self.norm_tile(i, PV_GMIX, hT, i * 128, i, tmp, 6 + (i % 2), use_saved=True)
        A.release(mk1)
        self.hT_freed = False
    if not getattr(self, 'skip_merge', False):
        self.merge(l)
    if not self.hT_freed:
        A.free_top(hT)
    A.release()


K.mixer = mixer


def run_pipeline(blocks, nstage):
    n = len(blocks)
    for t in range(n + nstage - 1):
        for k in reversed(range(nstage)):
            b = t - k
            if 0 <= b < n:
                blocks[b][k]()


def sb(self, l):
    A, P, hT = self.A, self.P, self.hT
    m0 = A.mark()
    wq = [A.get('sbw%d' % j, [8, 384], BF16) for j in range(2)]
    qP = [[A.get('sbq%d%d' % (j, h), [S], BF16) for h in range(2)] for j in range(2)]
    kT = [A.get('sbk%d' % j, [S], BF16) for j in range(2)]
    vv = [A.get('sbv%d' % j, [NT, 128], BF16) for j in range(2)]
    e_t = [A.get('sbe%d' % j, [512], F32) for j in range(2)]
    sp_t = [A.get('sbs%d' % j, [512], BF16) for j in range(3)]
    w_t = [A.get('sbp%d' % j, [512], BF16) for j in range(3)]
    sacc = [A.get('sba%d' % j, [512], BF16) for j in range(2)]
    oT = self.oT['sb']
    ps = self.ps
    ident = self.ident
    for j in range(2):
        for h in range(2):
            self.memset('dve', qP[j][h][:], 0.0, [qP[j][h]])
    wv = self.W['w_in'][l].rearrange('(c p) f -> p c f', p=128)

    def proj_chunks(pr):
        w = wq[pr % 2]
        q_, k_, v_ = qP[pr % 2], kT[pr % 2], vv[pr % 2]
        chunks = []

        def c_dma():
            for j, c0 in enumerate((O_SBQ, O_SBK, O_SBV)):
                self.dma('pool', w[:, :, j * 128:(j + 1) * 128], wv[:, :, c0 + pr * 128:c0 + (pr + 1) * 128], (), [(w, j)])
        chunks.append(c_dma)
        for tg in range(4):
            for j in range(2):
                def c_qk(tg=tg, j=j):
                    tcs = slice(tg * 512, (tg + 1) * 512)
                    pb = ps[6 + (tg * 2 + j) % 2]
                    for k_i in range(8):
                        self.mm(pb[:, :], w[:, k_i, j * 128:(j + 1) * 128], hT[:, k_i, tcs], k_i == 0, k_i == 7,
                                self.hk(4 * tg, 4 * tg + 4) + [(w, j)], [pb])
                    if j == 0:
                        for h in range(2):
                            self.act(q_[h][64 * h:64 * h + 64, tcs], pb[64 * h:64 * h + 64, :], AF.Copy, [pb], [(q_[h], tg)], scale=0.125)
                    else:
                        self.cp('dve', k_[:, tcs], pb[:, :], [pb], [(k_, tg)])
                chunks.append(c_qk)
        for i4 in range(4):
            def c_v(i4=i4):
                pb = ps[6 + i4 % 2]
                for ii in range(4):
                    i = i4 * 4 + ii
                    for k_i in range(8):
                        self.mm(pb[:, ii * 128:(ii + 1) * 128], hT[:, k_i, i * 128:(i + 1) * 128], w[:, k_i, 256:384], k_i == 0 and ii == 0, k_i == 7,
                                [(hT, i), (w, 2)], [pb], skip=True)
                self.cp('dve', v_[:, i4 * 4:i4 * 4 + 4, :], v3(pb[:, :], 4), [pb], [(v_, i4)])
            chunks.append(c_v)
        return chunks

    for c_ in proj_chunks(0):
        c_()
    for pr in range(4):
        q_, k_, v_ = qP[pr % 2], kT[pr % 2], vv[pr % 2]
        blocks = []
        bi = 0
        gi = 0
        for hh in range(2):
            pp = slice(64 * hh, 64 * hh + 64)
            for qg in range(4):
                sa = sacc[gi % 2]
                pc = ps[4 + gi % 2]
                gi += 1
                top = 4 * qg + 3
                for kb in range(top, -1, -1):
                    a = kb - 4 * qg
                    diag = a >= 0
                    c0 = 128 * a if diag else 0
                    cs = slice(c0, 512)
                    qc = slice(qg * 512 + c0, (qg + 1) * 512)
                    kc = slice(kb * 128, (kb + 1) * 128)
                    pa, pbk = ps[bi % 2], ps[2 + bi % 2]
                    et, st, wt = e_t[bi % 2], sp_t[bi % 3], w_t[bi % 3]
                    bi += 1
                    qh = q_[hh]
                    rq = [(qh, qg), (k_, kb // 4)]

                    def s0(pa=pa, cs=cs, c0=c0, kc=kc, qc=qc, diag=diag, rq=rq, qh=qh):
                        self.mm(pa[:, cs], k_[:, kc], qh[:, qc], True, not diag, rq, [pa])
                        if diag:
                            self.mm(pa[:, c0:c0 + 128], ident[:], self.nstrict, False, True, [self.ident, self.cm], [pa])

                    def s1(pa=pa, cs=cs, et=et, st=st):
                        self.act(et[:, cs], pa[:, cs], AF.Exp, [pa], [et])
                        self.act(st[:, cs], et[:, cs], AF.Ln, [et], [st], bias=1.0)

                    def s2(pbk=pbk, cs=cs, c0=c0, kc=kc, qc=qc, diag=diag, rq=rq, qh=qh, st=st, sa=sa, kb=kb, top=top):
                        if kb == top:
                            self.memset('dve', sa[:], 0.0, [sa])
                        self.mm(pbk[:, cs], self.tri, st[:, cs], True, False, [st, self.cm], [pbk])
                        if kb < top:
                            self.mm(pbk[:, cs], self.nones, sa[:, cs], False, False, [sa, self.cm], [pbk])
                        self.mm(pbk[:, cs], k_[:, kc], qh[:, qc], False, not diag, rq, [pbk])
                        if diag:
                            self.mm(pbk[:, c0:c0 + 128], ident[:], self.nstrict, False, True, [self.ident, self.cm], [pbk])
                        if kb > 0:
                            self.tt('dve', sa[:, cs], sa[:, cs], st[:, cs], ALU.add, [sa, st], [sa])

                    def s3(pbk=pbk, cs=cs, wt=wt):
                        self.act(wt[:, cs], pbk[:, cs], AF.Exp, [pbk], [wt])

                    def s4(pc=pc, cs=cs, wt=wt, kb=kb, top=top, pp=pp, qg=qg, hh=hh):
                        self.mm(pc[:, cs], v_[:, kb, :], wt[:, cs], kb == top, kb == 0, [(v_, kb // 4), wt], [pc], skip=True)
                        if kb == 0:
                            self.cp('dve', oT[pp, pr, qg * 512:(qg + 1) * 512], pc[pp, :], [pc], [(oT, (pr, hh, qg))])

                    blocks.append([s0, s1, s2, s3, s4])
        if pr + 1 < 4:
            nop = lambda: None
            ch = proj_chunks(pr + 1)
            step = max(1, (len(blocks) - 8) // len(ch))
            for ci, c_ in enumerate(ch):
                blocks.insert(min(len(blocks), 2 + ci * (step + 1)), [c_, nop, nop, nop, nop])
        run_pipeline(blocks, 5)
    A.release(m0)


K.sb = sb


def headnorm_rope(self, src, nh, hd, rope0, half, gain_ap, cos_ap, sin_ap, dst, scale, tmpb, R, Wr, gain_full=None, gR=None):
    sq, qn, st, r1, r2 = tmpb['sq'], tmpb['qn'], tmpb['st'], tmpb['r1'], tmpb['r2']
    n = nh * hd
    npart = src.shape[0]
    sqv = sq[0:npart, 0:n].rearrange('p (h d) -> p h d', h=nh)
    qnv = qn[0:npart, 0:n].rearrange('p (h d) -> p h d', h=nh)
    self.act(sqv, src, AF.Square, R, [sq])
    ss = st[0:npart, 0:nh]
    sd = st[0:npart, nh:2 * nh]
    self.P.op('dve', lambda e: e.tensor_reduce(out=ss, in_=sqv, axis=AX.X, op=ALU.add), [sq], [(st, 0)])
    self.act(sd, ss, AF.Sqrt, [(st, 0)], [(st, 1)], scale=1.0 / hd, bias=EPS)
    self.recip(ss, sd, [(st, 1)], [(st, 0)])
    if scale != 1.0:
        self.ts('dve', ss, ss, float(scale), None, ALU.mult, None, [(st, 0)], [(st, 0)])
    self.tt('dve', qnv, src, ss.unsqueeze(2).broadcast_to([npart, nh, hd]), ALU.mult, list(R) + [(st, 0)], [qn])
    if gain_full is not None:
        self.tt('dve', qnv, qnv, gain_full, ALU.mult, [qn] + list(gR), [qn])
    else:
        self.tt('dve', qnv, qnv, gain_ap.unsqueeze(1).broadcast_to([npart, nh, hd]), ALU.mult, [qn, self.pv], [qn])
    x1 = qnv[:, :, rope0:rope0 + half]
    x2 = qnv[:, :, rope0 + half:rope0 + 2 * half]
    cb = cos_ap.unsqueeze(1).broadcast_to([npart, nh, half])
    sb_ = sin_ap.unsqueeze(1).broadcast_to([npart, nh, half])
    r1v = r1[0:npart, 0:nh * half].rearrange('p (h d) -> p h d', h=nh)
    r2v = r2[0:npart, 0:nh * half].rearrange('p (h d) -> p h d', h=nh)
    cR = [qn, self.cfm]
    self.tt('dve', r1v, x2, sb_, ALU.mult, cR, [r1])
    self.tt('dve', r2v, x1, sb_, ALU.mult, cR, [r2])
    if rope0 > 0:
        self.cp('act', dst[:, :, 0:rope0], qnv[:, :, 0:rope0], [qn], [(Wr[0], 'a')] if isinstance(Wr[0], Buf) else Wr)
    o1 = dst[:, :, rope0:rope0 + half]
    o2 = dst[:, :, rope0 + half:rope0 + 2 * half]
    self.tt('dve', x1, x1, cb, ALU.mult, cR, [qn])
    self.tt('dve', x2, x2, cb, ALU.mult, cR, [qn])
    self.tt('dve', o1, x1, r1v, ALU.subtract, [qn, r1], Wr)
    self.tt('dve', o2, x2, r2v, ALU.add, [qn, r2], Wr)


K.headnorm_rope = headnorm_rope


def hn_tmp(self, A, n, nh, half):
    return {'sq': A.get('hn_sq', [n], F32), 'qn': A.get('hn_qn', [n], F32), 'st': A.get('hn_st', [2 * nh], F32),
            'r1': A.get('hn_r1', [nh * half], F32), 'r2': A.get('hn_r2', [nh * half], F32)}


K.hn_tmp = hn_tmp


def mla(self, l):
    A, P, hT, ps, ident = self.A, self.P, self.hT, self.ps, self.ident
    m0 = A.mark()
    oT = self.oT['mla']
    cfm = self.cfm = A.get('cfm_mla', [512], F32)
    self.dma('sp', cfm[:], self.W['cf'][:, CF_MC:CF_MC + 512], (), [cfm])
    wuq = A.get('wuq', [2, 768], BF16)
    self.dma('pool', wuq[:, :, :], self.W['mla_w_uq'][l].rearrange('(c p) f -> p c f', p=128), (), [wuq])
    wukv = A.get('wukv', [1024], BF16)
    self.dma('pool', wukv[:], self.W['mla_w_ukv'][l], (), [wukv])
    cqT = A.get('cqT', [2, S], BF16)
    ckvT = A.get('ckvT', [S], BF16)
    krt = A.get('krt', [NT, 32], F32)
    ms1 = A.mark()
    wc = A.get('mla_wc', [8, 416], BF16)
    self.load_w_in(l, O_CQ, 416, wc)
    st = A.get('mla_st', [8], F32)
    junk = A.get('mla_junk', [256], BF16)
    xq = [A.get('mla_xq%d' % j, [384], BF16) for j in range(2)]
    for i in range(NT):
        pb = ps[6 + i % 2]
        for k in range(8):
            self.mm(pb[:, 0:416], hT[:, k, i * 128:(i + 1) * 128], wc[:, k, :], k == 0, k == 7, [(hT, i), wc], [pb])
        x_ = xq[i % 2]
        sk = (st, i % 2)
        o = (i % 2) * 4
        self.memset('dve', st[:, o:o + 2], 0.0, [sk])
        self.act(junk[:, 0:256], pb[:, 0:256], AF.Square, [pb, sk], [junk, sk], accum=st[:, o:o + 1])
        self.act(junk[:, 0:128], pb[:, 256:384], AF.Square, [pb, sk], [junk, sk], accum=st[:, o + 1:o + 2])
        self.act(st[:, o + 2:o + 3], st[:, o:o + 1], AF.Sqrt, [sk], [sk], scale=1.0 / 256, bias=EPS)
        self.act(st[:, o + 3:o + 4], st[:, o + 1:o + 2], AF.Sqrt, [sk], [sk], scale=1.0 / 128, bias=EPS)
        self.recip(st[:, o:o + 2], st[:, o + 2:o + 4], [sk], [sk])
        self.act(x_[:, 0:256], pb[:, 0:256], AF.Copy, [pb, sk], [x_], scale=st[:, o:o + 1])
        self.act(x_[:, 256:384], pb[:, 256:384], AF.Copy, [pb, sk], [x_], scale=st[:, o + 1:o + 2])
        self.cp('dve', krt[:, i, :], pb[:, 384:416], [pb], [(krt, i)])
        pt = ps[4 + i % 2]
        for c in range(3):
            self.mm(pt[:, c * 128:(c + 1) * 128], x_[:, c * 128:(c + 1) * 128], ident[:], c == 0, True, [x_, self.ident], [pt], skip=True)
        self.tt('dve', cqT[:, :, i * 128:(i + 1) * 128], v3(pt[:, 0:256], 2),
                self.pv[:, PV_QN:PV_QN + 2].unsqueeze(2).broadcast_to([128, 2, 128]), ALU.mult, [pt, self.pv], [(cqT, i)])
        self.ts('dve', ckvT[:, i * 128:(i + 1) * 128], pt[:, 256:384], self.pv[:, PV_KVN:PV_KVN + 1], None, ALU.mult, None, [pt, self.pv], [(ckvT, i)])
    A.release(ms1)
    A.free_top(self.hT)
    self.hT_freed = True
    m1 = A.mark()
    qks = [A.get('mla_qkT%d' % j, [4, S], BF16) for j in range(2)]
    vvs = [A.get('mla_v%d' % j, [NT, 128], BF16) for j in range(2)]
    for j in range(2):
        self.memset('dve', qks[j][:, :, :], 0.0, [qks[j]])
    sq = A.get('mq_sq', [768], F32)
    qn = A.get('mq_qn', [768], F32)
    st = A.get('mq_st', [16], F32)
    r1 = A.get('mq_r1', [128], F32)
    r2 = A.get('mq_r2', [128], F32)
    qst = A.get('mq_qst', [768], F32)
    g4 = A.get('mla_g4', [4, 96], F32)
    qr_ = A.get('mla_qr', [768], BF16)
    p_t = [A.get('mla_p%d' % j, [512], BF16) for j in range(3)]
    rec = [A.get('mla_rec%d' % j, [512], F32) for j in range(2)]
    gq = self.pv[:, PV_GQ:PV_GQ + 96]
    gk = self.pv[:, PV_GK:PV_GK + 96]
    self.ts('dve', g4[:, 0, :], gq, float(96 ** -0.5), None, ALU.mult, None, [self.pv], [g4])
    self.ts('dve', g4[:, 1, :], gq, float(96 ** -0.5), None, ALU.mult, None, [self.pv], [g4])
    self.cp('dve', g4[:, 2, :], gk, [self.pv], [g4])
    self.cp('dve', g4[:, 3, :], gk, [self.pv], [g4])
    NPS = 14
    nop = lambda: None
    v4 = lambda ap: ap.rearrange('p (t h d) -> p t h d', t=2, h=4)
    v8 = lambda ap: ap.rearrange('p (h d) -> p h d', h=8)

    def prep_stages(pr, i2):
        qk, vv = qks[pr % 2], vvs[pr % 2]
        pqs = [ps[6], ps[7]]
        tiles = (2 * i2, 2 * i2 + 1)
        qst4, qn4, qr4 = v4(qst[:, :]), v4(qn[:, :]), v4(qr_[:, :])
        cos = cfm[:, i2 * 32:(i2 + 1) * 32].rearrange('p (t d) -> p t d', t=2).unsqueeze(2).broadcast_to([128, 2, 4, 16])
        sin = cfm[:, 256 + i2 * 32:256 + (i2 + 1) * 32].rearrange('p (t d) -> p t d', t=2).unsqueeze(2).broadcast_to([128, 2, 4, 16])
        r14 = r1[:, :].rearrange('p (t h d) -> p t h d', t=2, h=4)
        r24 = r2[:, :].rearrange('p (t h d) -> p t h d', t=2, h=4)
        ss, sd = st[:, 0:8], st[:, 8:16]
        x1, x2 = qn4[:, :, :, 64:80], qn4[:, :, :, 80:96]

        def p0():
            for t_, i in enumerate(tiles):
                pq = pqs[t_]
                for c in range(2):
                    self.mm(pq[:, 0:192], cqT[:, c, i * 128:(i + 1) * 128], wuq[:, c, pr * 192:(pr + 1) * 192], c == 0, c == 1, [(cqT, i), wuq], [pq])
                self.mm(pq[:, 256:512], ckvT[:, i * 128:(i + 1) * 128], wukv[:, pr * 256:(pr + 1) * 256], False, True, [(ckvT, i), wukv], [pq], skip=True)

        def p1():
            for t_, i in enumerate(tiles):
                pq = pqs[t_]
                kvv = v3(pq[:, 256:512], 2)
                self.cp('act', qst4[:, t_, 0:2, :], v3(pq[:, 0:192], 2), [pq], [(qst, (t_, 0))])
                self.cp('act', qst4[:, t_, 2:4, 0:64], kvv[:, :, 0:64], [pq], [(qst, (t_, 1))])
                self.cp('act', qst4[:, t_, 2:4, 64:96], krt[:, i, :].unsqueeze(1).broadcast_to([128, 2, 32]), [(krt, i)], [(qst, (t_, 2))])

        def p2():
            self.act(sq[:, :], qst[:, :], AF.Square, [qst], [sq])
            for t_, i in enumerate(tiles):
                kvv = v3(pqs[t_][:, 256:512], 2)
                self.cp('dve', vv[:, i, :].rearrange('p (h d) -> p h d', h=2), kvv[:, :, 64:128], [pqs[t_]], [(vv, i)])

        def p3():
            self.P.op('dve', lambda e: e.tensor_reduce(out=ss, in_=v8(sq[:, :]), axis=AX.X, op=ALU.add), [sq], [(st, 0)])

        def p4():
            self.act(sd, ss, AF.Ln, [(st, 0)], [(st, 1)], scale=1.0 / 96, bias=EPS)
            self.act(ss, sd, AF.Exp, [(st, 1)], [(st, 0)], scale=-0.5)

        def p5():
            self.tt('dve', v8(qn[:, :]), v8(qst[:, :]), ss.unsqueeze(2).broadcast_to([128, 8, 96]), ALU.mult, [qst, (st, 0)], [qn])
            self.tt('dve', qn4, qn4, g4[:, :, :].unsqueeze(1).broadcast_to([128, 2, 4, 96]), ALU.mult, [qn, g4], [qn])
            self.cp('dve', qr4[:, :, :, 0:64], qn4[:, :, :, 0:64], [qn], [qr_])
            cR = [qn, cfm]
            self.tt('dve', r14, x2, sin, ALU.mult, cR, [r1])
            self.tt('dve', r24, x1, sin, ALU.mult, cR, [r2])
            self.tt('dve', x1, x1, cos, ALU.mult, cR, [qn])
            self.tt('dve', x2, x2, cos, ALU.mult, cR, [qn])
            self.tt('dve', qr4[:, :, :, 64:80], x1, r14, ALU.subtract, [qn, r1], [qr_])
            self.tt('dve', qr4[:, :, :, 80:96], x2, r24, ALU.add, [qn, r2], [qr_])

        pts = [ps[4], ps[5]]

        def p6():
            for t_, i in enumerate(tiles):
                pt = pts[t_]
                for j in range(4):
                    self.mm(pt[0:96, j * 128:(j + 1) * 128], qr4[:, t_, j, :], ident[:], j == 0, True, [qr_, self.ident], [pt], skip=True)

        def p7():
            for t_, i in enumerate(tiles):
                self.cp('act', qk[0:96, :, i * 128:(i + 1) * 128], v3(pts[t_][0:96, :], 4), [pts[t_]], [(qk, i)])

        return [p0, p1, p2, p3, p4, p5] + [nop] * (NPS - 8) + [p6, p7]

    SP = 10
    NCH = NT // 2
    blocks = []
    for i2 in range(NCH):
        blocks.append(prep_stages(0, i2))
        for _ in range(SP - 1):
            blocks.append([nop] * NPS)
    run_pipeline(blocks, NPS)
    for pr in range(4):
        qk, vv = qks[pr % 2], vvs[pr % 2]
        blocks = []
        bi = 0
        gi = 0
        for hh in range(2):
            pp = slice(64 * hh, 64 * hh + 64)
            for qg in range(4):
                pc, pd = ps[2], ps[3]
                rc = rec[gi % 2]
                gi += 1
                top = 4 * qg + 3
                for kb in range(0, top + 1):
                    a = kb - 4 * qg
                    diag = a >= 0
                    c0 = 128 * a if diag else 0
                    cs = slice(c0, 512)
                    qc = slice(qg * 512 + c0, (qg + 1) * 512)
                    kc = slice(kb * 128, (kb + 1) * 128)
                    pa = ps[bi % 2]
                    pt_ = p_t[bi % 3]
                    bi += 1
                    rq = [(qk, i_) for i_ in range(4 * qg, 4 * qg + 4)] + [(qk, kb)]

                    def s0(pa=pa, cs=cs, c0=c0, kc=kc, qc=qc, diag=diag, rq=rq, hh=hh, qk=qk):
                        self.mm(pa[:, cs], qk[:, 2 + hh, kc], qk[:, hh, qc], True, not diag, rq, [pa])
                        if diag:
                            self.mm(pa[:, c0:c0 + 128], ident[:], self.nincl, False, True, [self.ident, self.cm], [pa])

                    def s1(pa=pa, cs=cs, pt_=pt_):
                        self.act(pt_[:, cs], pa[:, cs], AF.Exp, [pa], [pt_])

                    def s2(pc=pc, pd=pd, cs=cs, pt_=pt_, kb=kb, top=top, pp=pp, rc=rc, qg=qg, hh=hh, vv=vv, pr=pr):
                        self.mm(pc[:, cs], vv[:, kb, :], pt_[:, cs], kb == 0, kb == top, [(vv, kb), pt_], [pc])
                        self.mm(pd[:, cs], self.ones, pt_[:, cs], kb == 0, kb == top, [self.cm, pt_], [pd])
                        if kb == top:
                            self.act(rc[pp, :], pd[pp, :], AF.Ln, [pd], [rc])
                            self.act(rc[pp, :], rc[pp, :], AF.Exp, [rc], [rc], scale=-1.0)
                            self.tt('dve', oT[pp, pr, qg * 512:(qg + 1) * 512], pc[pp, :], rc[pp, :], ALU.mult, [pc, rc], [(oT, (pr, hh, qg))])

                    blocks.append([s0, s1, s2] + [nop] * (NPS - 3))
        if pr + 1 < 4:
            assert len(blocks) >= SP * NCH
            for i2 in range(NCH):
                blocks.insert(i2 * SP, prep_stages(pr + 1, i2))
        run_pipeline(blocks, NPS)
    A.release(m0)


K.mla = mla


def nsa(self, l):
    A, P, hT, ps, ident = self.A, self.P, self.hT, self.ps, self.ident
    m0 = A.mark()
    oT = self.oT['nsa']
    qT = A.get('nq', [4, S], BF16)
    kTs = A.get('nks', [2, S], BF16)
    kTw = A.get('nkw', [2, S], BF16)
    vs = A.get('nvs', [NT, 2, 65], BF16)
    vw = A.get('nvw', [NT, 2, 65], BF16)
    gts = A.get('ngt', [NT, 24], F32)
    kcT = A.get('nkc', [2, 128], BF16)
    cmpV = A.get('ncv', [2, 97], BF16)
    cfm = self.cfm = A.get('cfm_nsa', [1088], F32)
    self.dma('sp', cfm[:], self.W['cf'][:, CF_NC:CF_NC + 1088], (), [cfm])
    self.memset('dve', vs[:, :, :, :], 1.0, [vs])
    self.memset('dve', vw[:, :, :, :], 1.0, [vw])
    self.memset('dve', kTs[:, :, :], 0.0, [kTs])
    self.memset('dve', kTw[:, :, :], 0.0, [kTw])
    self.memset('dve', kcT[:, :, :], 0.0, [kcT])
    gq = self.pv[:, PV_NQ:PV_NQ + 64]
    gk = self.pv[:, PV_NK:PV_NK + 64]
    m1 = A.mark()
    NB = 2
    bufA = [A.get('npA%d' % j, [768], F32) for j in range(NB)]
    bufB = [A.get('npB%d' % j, [768], F32) for j in range(NB)]
    sts = [A.get('npst%d' % j, [24], F32) for j in range(NB)]
    qkrs = [A.get('npqk%d' % j, [12, 64], BF16) for j in range(NB)]
    g12 = A.get('npg12', [12, 64], F32)
    self.ts('dve', g12[:, 0:8, :], gq.unsqueeze(1).broadcast_to([128, 8, 64]), 0.125, None, ALU.mult, None, [self.pv], [g12])
    self.cp('dve', g12[:, 8:12, :], gk.unsqueeze(1).broadcast_to([128, 4, 64]), [self.pv], [g12])
    wN = A.get_top('nw', [8, 1304], BF16)
    self.load_w_in(l, O_NQ, 512, wN, 0)
    wv_ = self.W['w_in'][l].rearrange('(c p) f -> p c f', p=128)
    self.dma('pool', wN[:, :, 512:1304], wv_[:, :, O_CK:O_CK + 792], (), [(wN, 1)])
    v12 = lambda ap: ap.rearrange('p (h d) -> p h d', h=12)
    nop = lambda: None
    NPN = 10
    SPN = 5

    def prep_stages(i):
        tc_ = slice(i * 128, (i + 1) * 128)
        pq, pk, pg, pt = ps[6 + i % 2], ps[4 + i % 2], ps[2 + i % 2], ps[i % 2]
        bA, bB, st, qkr = bufA[i % NB], bufB[i % NB], sts[i % NB], qkrs[i % NB]
        A12, B12 = v12(bA[:, :]), v12(bB[:, :])
        ss, sd = st[:, 0:12], st[:, 12:24]
        cos = cfm[:, i * 32:(i + 1) * 32].unsqueeze(1).broadcast_to([128, 12, 32])
        sin = cfm[:, 512 + i * 32:512 + (i + 1) * 32].unsqueeze(1).broadcast_to([128, 12, 32])
        r1 = bA[:, 0:384].rearrange('p (h d) -> p h d', h=12)
        r2 = bA[:, 384:768].rearrange('p (h d) -> p h d', h=12)
        x1, x2 = B12[:, :, 0:32], B12[:, :, 32:64]

        def p0():
            for k_i in range(8):
                self.mm(pq[:, :], hT[:, k_i, tc_], wN[:, k_i, 0:512], k_i == 0, k_i == 7, [(hT, i), (wN, 0)], [pq])
            for k_i in range(8):
                self.mm(pk[:, :], hT[:, k_i, tc_], wN[:, k_i, 768:1280], k_i == 0, k_i == 7, [(hT, i), (wN, 1)], [pk])
            for k_i in range(8):
                self.mm(pg[:, 0:24], hT[:, k_i, tc_], wN[:, k_i, 1280:1304], k_i == 0, k_i == 7, [(hT, i), (wN, 1)], [pg])

        def p1():
            self.cp('act', A12[:, 0:8, :], v3(pq[:, :], 8), [pq], [(bA, 0)])
            self.cp('act', A12[:, 8:10, :], v3(pk[:, 0:128], 2), [pk], [(bA, 1)])
            self.cp('act', A12[:, 10:12, :], v3(pk[:, 256:384], 2), [pk], [(bA, 2)])
            self.act(gts[:, i, :], pg[:, 0:24], AF.Tanh, [pg], [(gts, i)], scale=0.5)

        def p2():
            self.act(bB[:, :], bA[:, :], AF.Square, [bA], [bB])
            self.ts('dve', gts[:, i, :], gts[:, i, :], 0.5, 0.5, ALU.mult, ALU.add, [(gts, i)], [(gts, i)])
            self.cp('dve', vs[:, i, :, 0:64], v3(pk[:, 128:256], 2), [pk], [(vs, i)])
            self.cp('dve', vw[:, i, :, 0:64], v3(pk[:, 384:512], 2), [pk], [(vw, i)])

        def p3():
            self.P.op('dve', lambda e: e.tensor_reduce(out=ss, in_=B12, axis=AX.X, op=ALU.add), [bB], [(st, 0)])

        def p4():
            self.act(sd, ss, AF.Ln, [(st, 0)], [(st, 1)], scale=1.0 / 64, bias=EPS)
            self.act(ss, sd, AF.Exp, [(st, 1)], [(st, 0)], scale=-0.5)

        def p5():
            self.tt('dve', B12, A12, ss.unsqueeze(2).broadcast_to([128, 12, 64]), ALU.mult, [bA, (st, 0)], [bB])
            self.tt('dve', B12, B12, g12[:, :, :], ALU.mult, [bB, g12], [bB])
            cR = [bB, cfm]
            self.tt('dve', r1, x2, sin, ALU.mult, cR, [(bA, 'r1')])
            self.tt('dve', r2, x1, sin, ALU.mult, cR, [(bA, 'r2')])
            self.tt('dve', x1, x1, cos, ALU.mult, cR, [bB])
            self.tt('dve', x2, x2, cos, ALU.mult, cR, [bB])
            self.tt('dve', qkr[:, :, 0:32], x1, r1, ALU.subtract, [bB, (bA, 'r1')], [qkr])
            self.tt('dve', qkr[:, :, 32:64], x2, r2, ALU.add, [bB, (bA, 'r2')], [qkr])

        def p8():
            for r in range(4):
                for g in range(2):
                    self.mm(pt[64 * g:64 * g + 64, r * 128:(r + 1) * 128], qkr[:, 4 * g + r, :], ident[:], r == 0, True, [qkr, self.ident], [pt], skip=True)
            for b in range(2):
                for g in range(2):
                    self.mm(pq[64 * g:64 * g + 64, b * 128:(b + 1) * 128], qkr[:, 8 + 2 * b + g, :], ident[:], b == 0, True, [qkr, self.ident], [pq], skip=True)

        def p9():
            self.cp('act', qT[:, :, tc_], v3(pt[:, :], 4), [pt], [(qT, i)])
            for g in range(2):
                gs = slice(64 * g, 64 * g + 64)
                self.cp('act', kTs[gs, g, tc_], pq[gs, 0:128], [pq], [(kTs, i)])
                self.cp('act', kTw[gs, g, tc_], pq[gs, 128:256], [pq], [(kTw, i)])

        return [p0, p1, p2, p3, p4, p5, nop, nop, p8, p9]

    blocks = []
    for i in range(NT):
        blocks.append(prep_stages(i))
        for _ in range(SPN - 1):
            blocks.append([nop] * NPN)
    run_pipeline(blocks, NPN)
    A.release(m1)
    m1 = A.mark()
    ckT = A.get('nck', [S], BF16)
    cvT = A.get('ncvT', [S], BF16)
    tb = self.hn_tmp(A, 64, 1, 32)
    for tg in range(4):
        for j, dst in ((0, ckT), (1, cvT)):
            pb = ps[(tg * 2 + j) % 2]
            for k_i in range(8):
                self.mm(pb[:, :], wN[:, k_i, 512 + j * 128:640 + j * 128], hT[:, k_i, tg * 512:(tg + 1) * 512], k_i == 0, k_i == 7,
                        self.hk(4 * tg, 4 * tg + 4) + [(wN, 1)], [pb])
            self.cp('act' if j == 0 else 'dve', dst[:, tg * 512:(tg + 1) * 512], pb[:, :], [pb], [(dst, tg)])
    A.free_top(wN)
    w1 = [A.get('nw1%d' % j, [32, 128], BF16) for j in range(2)]
    w2 = [A.get('nw2%d' % j, [64], BF16) for j in range(2)]
    peT = A.get('npe', [64], BF16)
    gx = [A.get('ngx%d' % j, [128], F32) for j in range(4)]
    hid = A.get('nhid', [128], BF16)
    kc = A.get('nkcs', [64], BF16)
    for j, nm in enumerate(('cmp_wk1', 'cmp_wv1')):
        src = self.W[nm][l].rearrange('(l d) h -> d l h', d=64)
        self.dma('pool', w1[j][0:64, :, :], src, (), [(w1[j], 0)])
        self.dma('pool', w1[j][64:128, :, :], src, (), [(w1[j], 1)])
    for j, nm in enumerate(('cmp_wk2', 'cmp_wv2')):
        self.dma('pool', w2[j][:], self.W[nm][l], (), [w2[j]])
    self.dma('pool', peT[:], self.W['pekv'][l], (), [peT])
    self.dma('pool', cmpV[0:127, 0, 64:97], self.W['cb'][0:127, CB_OV:CB_OV + 33], (), [(cmpV, 'ov0')])
    self.dma('pool', cmpV[0:127, 1, 64:97], self.W['cb'][0:127, CB_OV:CB_OV + 33], (), [(cmpV, 'ov1')])
    cnt = 0
    for j, cT in ((0, ckT), (1, cvT)):
        for g in range(2):
            pg_ = slice(64 * g, 64 * g + 64)
            ph, po = ps[cnt % 2], ps[2 + cnt % 2]
            cnt += 1
            for l_ in range(32):
                self.mm(ph[:, 0:127], w1[j][pg_, l_, :], cT[pg_, l_:l_ + 16 * 126 + 1:16], l_ == 0, False, [cT, (w1[j], g)], [ph])
            for l_ in range(32):
                self.mm(ph[:, 0:127], w1[j][pg_, l_, :], peT[pg_, j * 32 + l_:j * 32 + l_ + 1].broadcast_to([64, 127]), False, l_ == 31,
                        [peT, (w1[j], g)], [ph])
            x, x2, u, th = [b_[:, 0:127] for b_ in gx]
            self.cp('act', x, ph[:, 0:127], [ph], [gx[0]])
            self.tt('dve', x2, x, x, ALU.mult, [gx[0]], [gx[1]])
            self.ts('dve', x2, x2, 0.044715, 1.0, ALU.mult, ALU.add, [gx[1]], [gx[1]])
            self.tt('dve', u, x, x2, ALU.mult, [gx[0], gx[1]], [gx[2]])
            self.act(th, u, AF.Tanh, [gx[2]], [gx[3]], scale=0.7978845608028654)
            self.ts('dve', th, th, 0.5, 0.5, ALU.mult, ALU.add, [gx[3]], [gx[3]])
            self.tt('dve', hid[:, 0:127], x, th, ALU.mult, [gx[0], gx[3]], [hid])
            self.mm(po[0:127, 0:64], hid[:, 0:127], w2[j][:], True, True, [hid, w2[j]], [po])
            if j == 0:
                self.headnorm_rope(po[0:127, 0:64].unsqueeze(1), 1, 64, 0, 32, gk[0:127, :], cfm[0:127, 1024:1056], cfm[0:127, 1056:1088],
                                   kc[0:127, :].unsqueeze(1), 1.0, tb, [po], [kc])
                pt = ps[4 + g]
                self.mm(pt[pg_, 0:127], kc[0:127, :], ident[0:127, 0:127], True, True, [kc, self.ident], [pt])
                self.cp('dve', kcT[pg_, g, 0:127], pt[pg_, 0:127], [pt], [(kcT, g)])
            else:
                self.cp('dve', cmpV[0:127, g, 0:64], po[0:127, 0:64], [po], [(cmpV, g)])
    A.release(m1)
    ct = A.get('nct', [1024], F32)
    self.dma('sp', ct[:], self.W['cf'][:, CF_KEEP:CF_KEEP + 1024], (), [ct])
    cmpb = A.get('ncb', [S], BF16)
    self.dma('pool', cmpb[:], self.W['cb'][:, CB_CMPB:CB_CMPB + S], (), [cmpb])
    E = A.get('nE', [S], BF16)
    self.dma('pool', E[:], self.W['cb'][:, CB_E:CB_E + S], (), [E])
    selt = [A.get('nsel%d' % j, [128], BF16) for j in range(4)]
    for j in range(4):
        self.memset('dve', selt[j][:], 0.0, [selt[j]])
    pTb = [A.get('npT%d' % j, [512], BF16) for j in range(3)]
    oacc = [A.get('noa%d' % j, [4, 64], F32) for j in range(2)]
    otmp = A.get('notmp', [4, 64], F32)
    ob = [A.get('nob%d' % j, [8, 64], BF16) for j in range(2)]
    sm = [A.get('nsm%d' % j, [160], F32) for j in range(2)]
    selb = [A.get('nselb%d' % j, [32], BF16) for j in range(4)]
    cnt = [0]
    NST = 14
    nop = lambda: None

    def nxt():
        b = cnt[0]
        cnt[0] += 1
        return ps[b % 3], pTb[b % 3]

    def ctx(idx):
        i, g = idx // 2, idx % 2
        s_ = sm[idx % 2]
        return dict(i=i, g=g, tc_=slice(i * 128, (i + 1) * 128), s_=s_, oa=oacc[idx % 2], sb_=selb[idx % 4], st_=selt[idx % 4],
                    ob_=ob[i % 2], den=s_[:, 0:4], rec=s_[:, 4:8], coef=s_[:, 8:12], imp=s_[:, 16:48], sc=s_[:, 48:80],
                    sc2=s_[:, 80:112], m8a=s_[:, 112:120], m8b=s_[:, 120:128], sk=[s_])

    def cmp_block(idx):
        c = ctx(idx)
        i, g, tc_, s_, sk = c['i'], c['g'], c['tc_'], c['s_'], c['sk']
        pb, pT = nxt()
        po = ps[3]

        def s0():
            self.mm(v3(pb[0:127, :], 4), kcT[:, g, 0:127], qT[:, :, tc_], True, False, [(kcT, g), (qT, i)], [pb])
            self.mm(v3(pb[0:127, :], 4), ident[0:127, 0:127], bc4(cmpb[0:127, tc_]), False, True, [self.ident, cmpb], [pb])

        def s1():
            self.act(pT[0:127, :], pb[0:127, :], AF.Exp, [pb], [pT])

        def s2():
            den, rec, coef, imp, sc, sc2, m8a, m8b = (c[n] for n in ('den', 'rec', 'coef', 'imp', 'sc', 'sc2', 'm8a', 'm8b'))
            oa, sb_, st_ = c['oa'], c['sb_'], c['st_']
            for cc in range(4):
                self.mm(po[:, cc * 97:(cc + 1) * 97], pT[0:127, cc * 128:(cc + 1) * 128], cmpV[0:127, g, :], cc == 0, True, [pT, cmpV], [po], skip=True)
            pov = v3(po[:, 0:388], 4)
            self.ts('dve', den, pov[:, :, 96], 1e-30, None, ALU.max, None, [po], sk)
            self.recip(rec, den, sk, sk)
            self.ts('dve', imp, pov[:, 0, 64:96], rec[:, 0:1], None, ALU.mult, None, [po] + sk, sk)
            for cc in range(1, 4):
                self.stt('dve', imp, pov[:, cc, 64:96], rec[:, cc:cc + 1], imp, ALU.mult, ALU.add, [po] + sk, sk)
            self.tt('dve', coef, rec, gts[:, i, 4 * g:4 * g + 4], ALU.mult, sk + [(gts, i)], sk)
            self.tt('dve', oa[:, :, :], pov[:, :, 0:64], coef.unsqueeze(2).broadcast_to([128, 4, 64]), ALU.mult, [po] + sk, [oa])
            self.tt('dve', sc, imp, ct[:, i * 32:(i + 1) * 32], ALU.mult, sk + [ct], sk)
            self.tt('dve', sc, sc, ct[:, 512 + i * 32:512 + (i + 1) * 32], ALU.add, sk + [ct], sk)
            self.P.op('dve', lambda e: e.max(out=m8a, in_=sc), sk, sk, strict=True)
            self.P.op('dve', lambda e: e.match_replace(out=sc2, in_to_replace=m8a, in_values=sc, imm_value=-1e30), sk, sk, strict=True)
            self.P.op('dve', lambda e: e.max(out=m8b, in_=sc2), sk, sk, strict=True)
            self.ts('dve', sb_[:, :], sc, m8b[:, 7:8], 1.0, ALU.is_ge, ALU.subtract, sk, [sb_])

        def s12():
            self.mm(ps[6][0:32, 0:128], c['sb_'][:, :], ident[:], True, True, [c['sb_'], self.ident], [ps[6]])

        def s13():
            self.cp('act', c['st_'][0:32, :], ps[6][0:32, 0:128], [ps[6]], [c['st_']])

        return [s0, s1, s2] + [nop] * (NST - 5) + [s12, s13]

    def att_block(idx, kind, kb, first, last):
        c = ctx(idx)
        i, g, tc_, s_, sk = c['i'], c['g'], c['tc_'], c['s_'], c['sk']
        pb, pT = nxt()
        kc_ = slice(kb * 128, (kb + 1) * 128)
        kT_, v_, pacc, goff = (kTs, vs, ps[4], 8) if kind == 'slc' else (kTw, vw, ps[5], 16)
        dg = (kb == i)
        far = (kind == 'win' and kb == i - 4)

        def s0():
            nb = (1 if kind == 'slc' else 0) + (1 if dg else 0) + (1 if far else 0)
            self.mm(v3(pb[:, :], 4), kT_[:, g, kc_], qT[:, :, tc_], True, nb == 0, [(kT_, kb), (qT, i)], [pb])
            if kind == 'slc':
                nb -= 1
                self.mm(v3(pb[:, :], 4), E[:, kc_], bc4(c['st_'][:, :]), False, nb == 0, [E, c['st_']], [pb])
            if dg:
                nb -= 1
                self.mm(v3(pb[:, :], 4), ident[:], bc4(self.nincl), False, nb == 0, [self.ident, self.cm], [pb])
            if far:
                nb -= 1
                self.mm(v3(pb[:, :], 4), ident[:], bc4(self.nfar), False, nb == 0, [self.ident, self.cm], [pb])

        def s1():
            self.act(pT[:, :], pb[:, :], AF.Exp, [pb], [pT])

        def s2():
            for cc in range(4):
                self.mm(pacc[:, cc * 65:(cc + 1) * 65], pT[:, cc * 128:(cc + 1) * 128], v_[:, kb, g, :], first and cc == 0, last,
                        [pT, (v_, kb)], [pacc], skip=True)
            if not last:
                return
            rec, coef, oa, ob_ = c['rec'], c['coef'], c['oa'], c['ob_']
            pav = v3(pacc[:, 0:260], 4)
            self.recip(rec, pav[:, :, 64], [pacc], sk)
            self.tt('dve', coef, rec, gts[:, i, goff + 4 * g:goff + 4 * g + 4], ALU.mult, sk + [(gts, i)], sk)
            self.tt('dve', otmp[:, :, :], pav[:, :, 0:64], coef.unsqueeze(2).broadcast_to([128, 4, 64]), ALU.mult, [pacc] + sk, [otmp])
            if kind == 'win':
                self.tt('dve', oa[:, :, :], oa[:, :, :], otmp[:, :, :], ALU.add, [oa, otmp], [oa])
            else:
                self.tt('dve', ob_[:, 4 * g:4 * g + 4, :], oa[:, :, :], otmp[:, :, :], ALU.add, [oa, otmp], [(ob_, g)])
                if g == 1:
                    pm2 = ps[7]
                    for p_ in range(4):
                        self.mm(pm2[:, p_ * 128:(p_ + 1) * 128], ob_[:, 2 * p_:2 * p_ + 2, :].rearrange('p h d -> p (h d)'), ident[:], p_ == 0, True,
                                [ob_, self.ident], [pm2], skip=True)
                    self.cp('act', oT[:, :, tc_], v3(pm2[:, :], 4), [pm2], [(oT, i)])

        return [s0, s1, s2] + [nop] * (NST - 3)

    blocks = [cmp_block(0)]
    for idx in range(2 * NT):
        i = idx // 2
        if idx + 1 < 2 * NT:
            blocks.append(cmp_block(idx + 1))
        kb0 = max(0, i - 4)
        for kb in range(kb0, i + 1):
            blocks.append(att_block(idx, 'win', kb, kb == kb0, kb == i))
        for kb in range(0, i + 1):
            blocks.append(att_block(idx, 'slc', kb, kb == 0, kb == i))
    run_pipeline(blocks, NST)
    A.release(m0)


K.nsa = nsa


def merge(self, l):
    A, P, hT, ps, X = self.A, self.P, self.hT, self.ps, self.X
    m0 = A.mark()
    yT = A.get('yT', [8, S], BF16)
    wg = [A.get('mwg%d' % j, [8, 3, 128], BF16) for j in range(2)]
    pw = [A.get('mpw%d' % j, [4, 3, 128], BF16) for j in range(1)]
    sg = [A.get('msg%d' % j, [512], BF16) for j in range(3)]
    tt_ = [A.get('mtt%d' % j, [512], F32) for j in range(2)]
    wv = self.W['w_in'][l].rearrange('(c p) f -> p c f', p=128)
    pj = [self.W[n][l].rearrange('(k p) d -> p k d', p=128) for n in ('proj_sb', 'proj_mla', 'proj_nsa')]
    oTs = [self.oT.get(n) for n in ('sb', 'mla', 'nsa')]
    cnt = 0
    for c in range(8):
        wg_, pw_ = wg[c % 2], pw[0]
        for b in range(3):
            self.dma('pool', wg_[:, :, b, :], wv[:, :, O_MG + b * 1024 + c * 128:O_MG + b * 1024 + (c + 1) * 128], (), [(wg_, b)])
            self.dma('pool', pw_[:, :, b, :], pj[b][:, :, c * 128:(c + 1) * 128], (), [(pw_, b)])
        for tg in range(4):
            tcs = slice(tg * 512, (tg + 1) * 512)
            gb, pb = [], []
            for b in range(3):
                g_ = ps[cnt % 8]
                cnt += 1
                for k in range(8):
                    self.mm(g_[:, :], wg_[:, k, b, :], hT[:, k, tcs], k == 0, k == 7, self.hk(4 * tg, 4 * tg + 4) + [(wg_, b)], [g_])
                self.act(sg[b][:], g_[:, :], AF.Sigmoid, [g_], [sg[b]])
            for b in range(3):
                p_ = ps[cnt % 8]
                cnt += 1
                pb.append(p_)
                if oTs[b] is None:
                    continue
                for kk in range(4):
                    self.mm(p_[:, :], pw_[:, kk, b, :], oTs[b][:, kk, tcs], kk == 0, kk == 3, [oTs[b], (pw_, b)], [p_])
            act = [b for b in range(3) if oTs[b] is not None]
            t0, t1 = tt_
            self.tt('dve', t0[:], sg[act[0]][:], pb[act[0]][:, :], ALU.mult, [sg[act[0]], pb[act[0]]], [t0])
            for b in act[1:]:
                self.tt('dve', t1[:], sg[b][:], pb[b][:, :], ALU.mult, [sg[b], pb[b]], [t1])
                self.tt('dve', t0[:], t0[:], t1[:], ALU.add, [t0, t1], [t0])
            self.cp('dve', yT[:, c, tcs], t0[:], [t0], [(yT, (c, tg))])
    A.free_top(hT)
    self.hT_freed = True
    wo = A.get_top('wout', [8, D], BF16)
    wov = self.W['w_out'][l].rearrange('(k p) d -> p k d', p=128)
    for h2 in range(2):
        self.dma('pool', wo[:, 4 * h2:4 * h2 + 4, :], wov[:, 4 * h2:4 * h2 + 4, :], (), [(wo, h2)])
    for i in range(NT):
        for dh in range(2):
            p_ = ps[cnt % 8]
            cnt += 1
            for k in range(8):
                self.mm(p_[:, :], yT[:, k, i * 128:(i + 1) * 128], wo[:, k, dh * 512:(dh + 1) * 512], k == 0, k == 7,
                        [(yT, (k, i // 4)), (wo, k // 4)], [p_])
            self.tt('dve', X[:, i, dh * 512:(dh + 1) * 512], X[:, i, dh * 512:(dh + 1) * 512], p_[:, :], ALU.add, [p_, (X, i)], [(X, i)])
    A.free_top(wo)
    A.release(m0)


K.merge = merge
```

```python
import numpy as np
import concourse.bass as bass
import concourse.mybir as mybir

F32 = mybir.dt.float32
BF16 = mybir.dt.bfloat16
U8 = mybir.dt.uint8
I32 = mybir.dt.int32
AF = mybir.ActivationFunctionType
ALU = mybir.AluOpType
AX = mybir.AxisListType

ENGS = ['pe', 'act', 'dve', 'pool', 'sp']
DSIZE = {F32: 4, BF16: 2, U8: 1, I32: 4}
SAME_ENG_SYNC = True
SAME_ENG_GAP = 1
DMA_K = {'sp': 12, 'pool': 12, 'act': 6}


class Buf:
    __slots__ = ('name', 'ap', 'state', 'space', 'off', 'size')

    def __init__(self, name, ap, space='sb', off=0, size=0):
        self.name = name
        self.ap = ap
        self.state = {}
        self.space = space
        self.off = off
        self.size = size

    def __getitem__(self, k):
        return self.ap[k]


class Op:
    __slots__ = ('eng', 'fn', 'idx', 'eidx', 'waits', 'inc', 'tick', 'is_dma', 'dsem', 'dval', 'vc', 'vcd', 'pewr')


class Prog:
    def __init__(self, nc, arena_bytes=200 * 1024):
        self.nc = nc
        self.ops = []
        self.eops = {e: [] for e in ENGS}
        self.known = {e: {d: -1 for d in ENGS} for e in ENGS}
        self.known_dma = {e: {} for e in ENGS}
        self.ndma = {e: 0 for e in ENGS}
        self.dma_hist = {e: [] for e in ENGS}
        self.arena_bytes = arena_bytes
        self.arena = nc.alloc_sbuf_tensor('arena', [128, arena_bytes], U8).ap()
        self.psum = [nc.alloc_psum_tensor('psb%d' % i, [128, 512], F32).ap() for i in range(8)]
        self.sb_top = 0
        self.freed = []
        self.live = {}
        self.all_dma = []
        self.spacer = {}

    def sb_at(self, name, off, shape, dtype, parts=128):
        n = int(np.prod(shape)) * DSIZE[dtype]
        assert off % 4 == 0
        assert off + n <= self.arena_bytes, (name, off, n, self.arena_bytes)
        ap = self.arena[0:parts, off:off + n].bitcast(dtype)
        if len(shape) > 1:
            names = ' '.join('d%d' % i for i in range(len(shape)))
            kw = {'d%d' % i: shape[i] for i in range(len(shape))}
            ap = ap.rearrange('p (%s) -> p %s' % (names, names), **kw)
        b = Buf(name, ap, 'sb', off, n)
        inh_w = []
        for (fo, fs, ops_) in self.freed:
            if fo < off + n and off < fo + fs:
                inh_w.extend(ops_)
        if inh_w:
            b.state[None] = [None, list(inh_w), list(inh_w)]
        return b

    def free(self, b):
        s = []
        for k, st in b.state.items():
            if st[0] is not None:
                s.append(st[0])
            s.extend(st[1])
            if len(st) > 2:
                s.extend(st[2])
        self.freed.append((b.off, b.size, self._compress(s)))
        if len(self.freed) > 400:
            self.freed = self.freed[-400:]

    def psb(self, bank, name=None):
        return Buf(name or ('ps%d' % bank), self.psum[bank], 'ps')

    def dram(self, name, ap):
        return Buf(name, ap, 'dram')

    def _compress(self, lst):
        best = {}
        out = []
        for i in set(lst):
            o = self.ops[i]
            if o.is_dma:
                out.append(i)
            else:
                if o.eng not in best or best[o.eng] < i:
                    best[o.eng] = i
        out.extend(best.values())
        return out

    def _deps_for(self, region, is_write, opidx, eng=None):
        if isinstance(region, tuple):
            buf, key = region
        else:
            buf, key = region, None
        st = buf.state
        deps = []
        psum = (buf.space == 'ps')
        if key is None:
            keys = list(st.keys())
        else:
            keys = [k for k in (key, None) if k in st]
        for k in keys:
            e = st[k]
            if e[0] is not None:
                deps.append(e[0])
            if is_write:
                deps.extend(e[1])
            elif psum:
                deps.extend(r for r in e[1] if self.ops[r].eng != eng)
            if len(e) > 2:
                deps.extend(e[2])
        return deps

    def _update(self, region, is_write, opidx):
        if isinstance(region, tuple):
            buf, key = region
        else:
            buf, key = region, None
        st = buf.state
        if key is None:
            if is_write:
                buf.state = {None: [opidx, []]}
            else:
                if None not in st:
                    st[None] = [None, []]
                for k in st:
                    st[k][1].append(opidx)
                    if len(st[k][1]) > 24:
                        st[k][1] = self._compress(st[k][1])
        else:
            if is_write:
                st[key] = [opidx, []]
            else:
                if key not in st:
                    st[key] = [None, []]
                st[key][1].append(opidx)
                if len(st[key][1]) > 24:
                    st[key][1] = self._compress(st[key][1])

    def _mkop(self, eng, fn, dma):
        o = Op()
        o.eng = eng
        o.fn = fn
        o.idx = len(self.ops)
        o.eidx = len(self.eops[eng])
        o.is_dma = dma
        o.inc = dma
        o.tick = None
        o.pewr = False
        o.waits = []
        return o

    def op(self, eng, fn, reads=(), writes=(), dma=False, pewr=False, strict=False):
        deps = []
        for r in reads:
            deps.extend(self._deps_for(r, False, None, eng))
        for w in writes:
            deps.extend(self._deps_for(w, True, None, eng))
        if eng in self.spacer and SAME_ENG_SYNC and not strict:
            last = len(self.eops[eng]) - 1
            for d in set(deps):
                od = self.ops[d]
                if (not od.is_dma) and od.eng == eng and od.eidx == last:
                    sp = self._mkop(eng, self.spacer[eng], False)
                    sp.vc = dict(self.known[eng])
                    sp.vcd = dict(self.known_dma[eng])
                    self.ops.append(sp)
                    self.eops[eng].append(sp)
                    break
        o = self._mkop(eng, fn, dma)
        if dma:
            k = DMA_K[eng]
            i = self.ndma[eng]
            o.dsem = (eng, i % k)
            o.dval = 16 * (i // k + 1)
            if i >= k:
                deps.append(self.dma_hist[eng][i - k])
            self.ndma[eng] += 1
            self.dma_hist[eng].append(o.idx)
            self.all_dma.append(o.idx)
        kn = self.known[eng]
        kd = self.known_dma[eng]
        for d in sorted(set(deps)):
            od = self.ops[d]
            if od.is_dma:
                if kd.get(od.dsem, 0) >= od.dval:
                    continue
                o.waits.append(d)
                kd[od.dsem] = od.dval
                for e2, v in od.vc.items():
                    if kn[e2] < v:
                        kn[e2] = v
                for k2, v in od.vcd.items():
                    if kd.get(k2, 0) < v:
                        kd[k2] = v
            else:
                if od.eng == eng:
                    if eng == 'pe' or not SAME_ENG_SYNC:
                        continue
                    if eng in self.spacer and not strict:
                        continue
                if kn[od.eng] >= d:
                    continue
                o.waits.append(d)
                od.inc = True
                for e2, v in od.vc.items():
                    if kn[e2] < v:
                        kn[e2] = v
                if kn[od.eng] < d:
                    kn[od.eng] = d
                for k2, v in od.vcd.items():
                    if kd.get(k2, 0) < v:
                        kd[k2] = v
        o.vc = dict(kn)
        o.vcd = dict(kd)
        self.ops.append(o)
        self.eops[eng].append(o)
        for r in reads:
            self._update(r, False, o.idx)
        for w in writes:
            self._update(w, True, o.idx)
        return o

    def emit(self):
        nc = self.nc
        fin_deps = list(self.all_dma[-64:])
        o = Op()
        o.eng = 'sp'; o.fn = None; o.idx = len(self.ops); o.is_dma = False; o.inc = False
        o.tick = None; o.waits = []; o.pewr = False
        kd = self.known_dma['sp']
        for q in DMA_K:
            for d in self.dma_hist[q][-DMA_K[q]:]:
                od = self.ops[d]
                if kd.get(od.dsem, 0) < od.dval:
                    o.waits.append(d)
        o.vc = {}; o.vcd = {}
        self.ops.append(o)
        self.eops['sp'].append(o)

        for e in ENGS:
            t = 0
            for op in self.eops[e]:
                if not op.is_dma and op.inc:
                    t += 1
                    op.tick = t
        import contextlib
        with contextlib.ExitStack() as es:
            esem = {e: es.enter_context(nc.semaphore('s_' + e)) for e in ENGS}
            dsem = {}
            for q, k in DMA_K.items():
                for j in range(k):
                    dsem[(q, j)] = es.enter_context(nc.semaphore('d_%s%d' % (q, j)))
            block = es.enter_context(nc.Block())
            ops = self.ops

            def run(e, h):
                for op in self.eops[e]:
                    for d in op.waits:
                        od = ops[d]
                        if od.is_dma:
                            h.wait_ge(dsem[od.dsem], od.dval)
                        else:
                            h.wait_ge(esem[od.eng], od.tick)
                    if op.fn is None:
                        continue
                    ins = op.fn(h)
                    if op.is_dma:
                        ins.then_inc(dsem[op.dsem], 16)
                    elif op.inc:
                        ins.then_inc(esem[e], 1)

            @block.tensor
            def _(h):
                run('pe', h)

            @block.scalar
            def _(h):
                run('act', h)

            @block.vector
            def _(h):
                run('dve', h)

            @block.gpsimd
            def _(h):
                run('pool', h)

            @block.sync
            def _(h):
                run('sp', h)

    def stats(self):
        s = {e: len(self.eops[e]) for e in ENGS}
        w = {e: sum(len(o.waits) for o in self.eops[e]) for e in ENGS}
        inc = {e: sum(1 for o in self.eops[e] if o.inc) for e in ENGS}
        return s, w, inc
import math, os
from concourse.bass_utils import run_bass_kernel_spmd

S = 2048
D = 1024
DFF = 2816
NT = 16
NL = 2
EPS = 1e-6
ARENA = 207 * 1024
NEG = -30000.0
N_IN = 6328
NPV = 352
PV_GFF1, PV_GMIX, PV_GFF2, PV_QN, PV_KVN, PV_GQ, PV_GK, PV_NQ, PV_NK = 0, 8, 16, 24, 26, 27, 123, 219, 283
O_SBQ, O_SBK, O_SBV, O_CQ, O_CKV, O_KR, O_NQ = 0, 512, 1024, 1536, 1792, 1920, 1952
O_CK, O_CV, O_SK, O_SV, O_WK, O_WV, O_NG, O_MG = 2464, 2592, 2720, 2848, 2976, 3104, 3232, 3256
CF_MC, CF_MS, CF_NC, CF_NS, CF_CC, CF_CS, CF_KEEP, CF_ADD, NCF = 0, 256, 512, 1024, 1536, 1568, 1600, 2112, 2624
CB_ID, CB_NSTRICT, CB_NINCL, CB_NFAR, CB_TRI, CB_NONES, CB_ONES, CB_CMPB, CB_E, CB_OV, NCB = 0, 128, 256, 384, 512, 640, 768, 896, 2944, 4992, 5056


class Alloc:
    def __init__(self, P, base, limit):
        self.P, self.base, self.limit, self.top, self.bufs = P, base, limit, base, []

    def get(self, name, shape, dtype, parts=128):
        n = int(np.prod(shape)) * DSIZE[dtype]
        off = (self.top + 31) // 32 * 32
        assert off + n <= self.limit, ('SBUF phase overflow', name, off, n, self.limit)
        b = self.P.sb_at(name, off, shape, dtype, parts)
        self.top = off + n
        self.peak = max(getattr(self, 'peak', 0), self.top)
        if os.environ.get('MEMDBG'):
            print('ALLOC %-10s off=%6d n=%6d top=%6d limit=%6d slack=%6d' % (name, off, n, self.top, self.limit, self.limit - self.top))
        self.bufs.append(b)
        return b

    def mark(self):
        return (self.top, len(self.bufs))

    def release(self, mark=None):
        top, nb = mark if mark is not None else (self.base, 0)
        for b in self.bufs[nb:]:
            self.P.free(b)
        self.bufs = self.bufs[:nb]
        self.top = top


class K:
    def __init__(self, nc, depth=NL, debug=None, stages=None):
        self.nc = nc
        self.depth = depth
        self.debug = debug
        self.stages = stages
        P = self.P = Prog(nc, arena_bytes=ARENA)
        dt = lambda name, shape, kind="ExternalInput": nc.dram_tensor(name, shape, F32, kind=kind).ap()
        self.x = dt('x', [S, D])
        self.out = dt('out', [S, D], "ExternalOutput")
        W = self.W = {}
        for name, shape in [('ffn1_wi', [NL, D, 2 * DFF]), ('ffn1_wo', [NL, DFF, D]), ('w_in', [NL, D, N_IN]),
                            ('mla_w_uq', [NL, 256, 768]), ('mla_w_ukv', [NL, 128, 1024]),
                            ('cmp_wk1', [NL, 2048, 128]), ('cmp_wk2', [NL, 128, 64]), ('cmp_wv1', [NL, 2048, 128]),
                            ('cmp_wv2', [NL, 128, 64]), ('proj_sb', [NL, 512, D]), ('proj_mla', [NL, 512, D]),
                            ('proj_nsa', [NL, 512, D]), ('w_out', [NL, D, D]), ('ffn2_wi', [NL, D, 2 * DFF]),
                            ('ffn2_wo', [NL, DFF, D]), ('pvec', [NL, 128, NPV]), ('pekv', [NL, 128, 64]),
                            ('cf', [128, NCF]), ('cb', [128, NCB])]:
            W[name] = dt(name, shape)
        if debug:
            self.dbg = dt('dbg', list(debug), "ExternalOutput")
        self.X = P.sb_at('X', 0, [NT, D], F32)
        cbase = NT * D * 4
        self.ident = P.sb_at('ident', cbase, [128], BF16)
        self.pv = P.sb_at('pv', cbase + 256, [NPV], F32)
        self.rstd = P.sb_at('rstd', cbase + 256 + NPV * 4, [NT], F32)
        self.small = P.sb_at('small', cbase + 256 + NPV * 4 + 64, [64], F32)
        self.A = Alloc(P, cbase + 4096, ARENA)
        self.ps = [P.psb(i) for i in range(8)]
        self.cnt = 0
        sm = self.small
        P.op('dve', lambda e: e.memset(sm[:, :], 0.0), (), [sm])
        if os.environ.get('SPACER'): P.spacer['dve'] = lambda e: e.memset(sm[:, 32:40], 0.0)
        if os.environ.get('SPACER'): P.spacer['act'] = lambda e: e.activation(out=sm[:, 40:48], in_=sm[:, 48:56], func=AF.Copy)

    def mm(self, out, lhsT, rhs, start, stop, R, Wr, skip=False):
        self.P.op('pe', lambda e: e.matmul(out, lhsT=lhsT, rhs=rhs, start=start, stop=stop, skip_group_check=skip), R, Wr)

    def tr(self, out, in_, ident, R, Wr):
        self.P.op('pe', lambda e: e.transpose(out=out, in_=in_, identity=ident), R, Wr)

    def act(self, out, in_, func, R, Wr, bias=None, scale=None, accum=None):
        kw = {}
        if bias is not None:
            kw['bias'] = bias
        if scale is not None:
            kw['scale'] = scale
        if accum is not None:
            kw['accum_out'] = accum
        strict = (bias is not None and not isinstance(bias, (int, float))) or (scale is not None and not isinstance(scale, (int, float)))
        self.P.op('act', lambda e: e.activation(out=out, in_=in_, func=func, **kw), R, Wr, strict=strict)

    def tt(self, eng, out, in0, in1, op, R, Wr):
        self.P.op(eng, lambda e: e.tensor_tensor(out=out, in0=in0, in1=in1, op=op), R, Wr)

    def ts(self, eng, out, in0, s1, s2, op0, op1, R, Wr, strict=False):
        strict = strict or not isinstance(s1, (int, float)) or not (s2 is None or isinstance(s2, (int, float)))
        if op1 is None:
            self.P.op(eng, lambda e: e.tensor_scalar(out=out, in0=in0, scalar1=s1, scalar2=None, op0=op0), R, Wr, strict=strict)
        else:
            self.P.op(eng, lambda e: e.tensor_scalar(out=out, in0=in0, scalar1=s1, scalar2=s2, op0=op0, op1=op1), R, Wr, strict=strict)

    def stt(self, eng, out, in0, scalar, in1, op0, op1, R, Wr):
        strict = not isinstance(scalar, (int, float))
        self.P.op(eng, lambda e: e.scalar_tensor_tensor(out=out, in0=in0, scalar=scalar, in1=in1, op0=op0, op1=op1), R, Wr, strict=strict)

    def cp(self, eng, out, in_, R, Wr):
        if eng == 'act':
            self.P.op('act', lambda e: e.copy(out=out, in_=in_), R, Wr)
        else:
            self.P.op(eng, lambda e: e.tensor_copy(out=out, in_=in_), R, Wr)

    def dma(self, q, out, in_, R, Wr):
        self.P.op(q, lambda e: e.dma_start(out=out, in_=in_), R, Wr, dma=True)

    def memset(self, eng, ap, val, Wr):
        self.P.op(eng, lambda e: e.memset(ap, val), (), Wr)

    def recip(self, out, in_, R, Wr):
        self.P.op('dve', lambda e: e.reciprocal(out=out, in_=in_), R, Wr)

    def load_x(self):
        xv = self.x.rearrange('(i p) d -> p i d', p=128)
        for c in range(4):
            self.dma('sp', self.X[:, 4 * c:4 * c + 4, :], xv[:, 4 * c:4 * c + 4, :], (), [(self.X, i) for i in range(4 * c, 4 * c + 4)])
        self.dma('pool', self.ident[:], self.W['cb'][:, CB_ID:CB_ID + 128], (), [self.ident])

    def store_x(self):
        ov = self.out.rearrange('(i p) d -> p i d', p=128)
        od = self.P.dram('out', self.out)
        for c in range(4):
            self.dma('sp', ov[:, 4 * c:4 * c + 4, :], self.X[:, 4 * c:4 * c + 4, :], [(self.X, i) for i in range(4 * c, 4 * c + 4)], [(od, c)])

    def load_pv(self, l):
        self.dma('sp', self.pv[:], self.W['pvec'][l], (), [self.pv])

    def norm_tile(self, i, gcol, hT, tcol, hkey, tmp, bank, save_rstd=True, use_saved=False):
        X = self.X
        ss, junk, xn = tmp
        k = self.cnt
        self.cnt += 1
        xn_ = xn[k % 2]
        rs = self.rstd[:, i:i + 1]
        if not use_saved:
            s_ = ss[:, (k % 2) * 2:(k % 2) * 2 + 1]
            sd = ss[:, (k % 2) * 2 + 1:(k % 2) * 2 + 2]
            sk = (ss, k % 2)
            self.memset('dve', s_, 0.0, [sk])
            self.act(junk[:], X[:, i, :], AF.Square, [(X, i), sk], [junk, sk], accum=s_)
            self.act(sd, s_, AF.Sqrt, [sk], [sk], scale=1.0 / D, bias=EPS)
            self.recip(rs, sd, [sk], [(self.rstd, i)])
        self.act(xn_[:], X[:, i, :], AF.Copy, [(X, i), (self.rstd, i)], [xn_], scale=rs)
        pb = self.ps[bank]
        pbv = pb.ap.bitcast(BF16)
        for c in range(8):
            self.tr(pbv[:, c * 128:(c + 1) * 128], xn_[:, c * 128:(c + 1) * 128], self.ident[:], [xn_, self.ident], [pb])
        self.tt('dve', hT[:, 0:8, tcol:tcol + 128], pbv[:, :].rearrange('p (c t) -> p c t', c=8),
                self.pv[:, gcol:gcol + 8].unsqueeze(2).broadcast_to([128, 8, 128]), ALU.mult, [pb, self.pv], [(hT, hkey)])

    def norm_tmp(self):
        A = self.A
        ss = A.get('ss', [4], F32)
        junk = A.get('junk', [D], BF16)
        xn = [A.get('xn%d' % j, [D], BF16) for j in range(2)]
        return (ss, junk, xn)

    def ffn(self, l, which):
        A, P, X = self.A, self.P, self.X
        A.release()
        wi = self.W['ffn%d_wi' % which][l].rearrange('(c p) f -> p c f', p=128)
        wo = self.W['ffn%d_wo' % which][l].rearrange('(j p) d -> p j d', p=128)
        gcol = PV_GFF1 if which == 1 else PV_GFF2
        wo_sb = A.get('wo_sb', [22, D], BF16)
        hTs = [A.get_top('hT%d' % j, [8, 1024], BF16) for j in range(2)]
        uT = A.get('uT', [22, 1024], BF16)
        wib = [A.get('wib%d' % j, [8, 256], BF16) for j in range(2)]
        tmp = self.norm_tmp()
        sl = [A.get('sl%d' % j, [512], F32) for j in range(2)]
        it = 0
        for ti in range(8):
            self.norm_tile(ti, gcol, hTs[0], ti * 128, ti, tmp, 4 + (ti % 2))
        for hf in range(2):
            hT = hTs[hf]
            for j in range(22):
                if hf == 0 and j % 2 == 1 and j // 2 < 8:
                    ti = j // 2
                    self.norm_tile(8 + ti, gcol, hTs[1], ti * 128, ti, tmp, 4 + (ti % 2))
                wb = wib[j % 2]
                self.dma('pool', wb[:, :, 0:128], wi[:, :, j * 128:(j + 1) * 128], (), [(wb, 0)])
                self.dma('pool', wb[:, :, 128:256], wi[:, :, DFF + j * 128:DFF + (j + 1) * 128], (), [(wb, 1)])
                if hf == 0 and j % 2 == 0:
                    self.dma('pool', wo_sb[:, j:j + 2, :], wo[:, j:j + 2, :], (), [(wo_sb, j), (wo_sb, j + 1)])
                for tg in range(2):
                    pa, pbk = self.ps[(it % 2) * 2], self.ps[(it % 2) * 2 + 1]
                    s_ = sl[it % 2]
                    it += 1
                    hk = [(hT, 4 * tg + a) for a in range(4)]
                    for k in range(8):
                        self.mm(pa[:, :], wb[:, k, 0:128], hT[:, k, tg * 512:(tg + 1) * 512], k == 0, k == 7, hk + [(wb, 0)], [pa])
                    for k in range(8):
                        self.mm(pbk[:, :], wb[:, k, 128:256], hT[:, k, tg * 512:(tg + 1) * 512], k == 0, k == 7, hk + [(wb, 1)], [pbk])
                    self.act(s_[:], pa[:, :], AF.Silu, [pa], [s_])
                    self.tt('dve', uT[:, j, tg * 512:(tg + 1) * 512], s_[:], pbk[:, :], ALU.mult, [s_, pbk], [(uT, (j, tg))])
            for ti in range(8):
                i = 8 * hf + ti
                for dh in range(2):
                    pb = self.ps[4 + (2 * ti + dh) % 4]
                    for j in range(22):
                        self.mm(pb[:, :], uT[:, j, ti * 128:(ti + 1) * 128], wo_sb[:, j, dh * 512:(dh + 1) * 512], j == 0, j == 21,
                                [(uT, (j, ti // 4)), (wo_sb, j)], [pb])
                    self.stt('dve', X[:, i, dh * 512:(dh + 1) * 512], pb[:, :], 0.5, X[:, i, dh * 512:(dh + 1) * 512], ALU.mult, ALU.add,
                             [pb, (X, i)], [(X, i)])
        A.free_top(hTs[1])
        A.free_top(hTs[0])
        A.release()

    def build(self):
        self.load_x()
        for l in range(self.depth):
            self.load_pv(l)
            st = self.stages or ('ffn1', 'mix', 'ffn2')
            if 'ffn1' in st:
                self.ffn(l, 1)
            if 'mix' in st:
                self.mixer(l)
            if 'ffn2' in st:
                self.ffn(l, 2)
        self.store_x()
        self.P.emit()


def host_consts():
    cf = np.zeros((128, NCF), np.float32)
    p = np.arange(128)[:, None]
    pos = (np.arange(NT)[None, :] * 128 + p).astype(np.float32)

    def ropetab(d, posv):
        half = d // 2
        inv = np.exp(np.float32(-math.log(10000.0)) * np.arange(half, dtype=np.float32) * np.float32(2.0 / d)).astype(np.float32)
        ang = (posv[..., None].astype(np.float32) * inv).astype(np.float32)
        return np.cos(ang).astype(np.float32), np.sin(ang).astype(np.float32)
    c, s = ropetab(32, pos)
    cf[:, CF_MC:CF_MC + 256] = c.reshape(128, 256)
    cf[:, CF_MS:CF_MS + 256] = s.reshape(128, 256)
    c, s = ropetab(64, pos)
    cf[:, CF_NC:CF_NC + 512] = c.reshape(128, 512)
    cf[:, CF_NS:CF_NS + 512] = s.reshape(128, 512)
    ends = (np.arange(128) * 16 + 31).astype(np.float32)
    c, s = ropetab(64, ends)
    cf[:, CF_CC:CF_CC + 32] = c
    cf[:, CF_CS:CF_CS + 32] = s
    blk = np.arange(32)[None, None, :]
    cur = (pos // 64).astype(np.int64)[:, :, None]
    forced = (blk == 0) | (blk == cur) | (blk == cur - 1)
    fut = blk > cur
    keep = (~forced) & (~fut)
    add = np.where(fut, -1.0, np.where(forced, 1e3, 0.0))
    cf[:, CF_KEEP:CF_KEEP + 512] = keep.astype(np.float32).reshape(128, 512)
    cf[:, CF_ADD:CF_ADD + 512] = add.astype(np.float32).reshape(128, 512)
    cb = np.zeros((128, NCB), np.float32)
    a = np.arange(128)[:, None]
    b = np.arange(128)[None, :]
    cb[:, CB_ID:CB_ID + 128] = (a == b)
    cb[:, CB_NSTRICT:CB_NSTRICT + 128] = np.where(a < b, 0.0, NEG)
    cb[:, CB_NINCL:CB_NINCL + 128] = np.where(a <= b, 0.0, NEG)
    cb[:, CB_NFAR:CB_NFAR + 128] = np.where(a > b, 0.0, NEG)
    cb[:, CB_TRI:CB_TRI + 128] = np.where(a >= b, -1.0, 0.0)
    cb[:, CB_NONES:CB_NONES + 128] = -1.0
    cb[:, CB_ONES:CB_ONES + 128] = 1.0
    n = np.arange(128)[:, None]
    t = np.arange(S)[None, :]
    cb[:, CB_CMPB:CB_CMPB + S] = np.where(16 * n + 31 <= t, 0.0, NEG)
    j = np.arange(128)[:, None]
    cb[:, CB_E:CB_E + S] = (j == (t // 64)) * 30000.0
    c0 = np.arange(128)[:, None] * 16
    s0 = np.arange(32)[None, :] * 64
    ov = np.clip(np.minimum(c0 + 32, s0 + 64) - np.maximum(c0, s0), 0, None) / 32.0
    cb[:, CB_OV:CB_OV + 32] = ov
    cb[:, CB_OV + 32] = 1.0
    return cf, cb


def host_pvec(inp):
    pv = np.zeros((NL, 128, NPV), np.float32)
    for l in range(NL):
        pv[l, :, PV_GFF1:PV_GFF1 + 8] = inp['ffn1_norm'][l].reshape(8, 128).T
        pv[l, :, PV_GMIX:PV_GMIX + 8] = inp['mix_norm'][l].reshape(8, 128).T
        pv[l, :, PV_GFF2:PV_GFF2 + 8] = inp['ffn2_norm'][l].reshape(8, 128).T
        pv[l, :, PV_QN:PV_QN + 2] = inp['mla_q_norm'][l].reshape(2, 128).T
        pv[l, :, PV_KVN:PV_KVN + 1] = inp['mla_kv_norm'][l].reshape(1, 128).T
        pv[l, :, PV_GQ:PV_GQ + 96] = inp['mla_qk_gain_q'][l][None, :]
        pv[l, :, PV_GK:PV_GK + 96] = inp['mla_qk_gain_k'][l][None, :]
        pv[l, :, PV_NQ:PV_NQ + 64] = inp['nsa_q_gain'][l][None, :]
        pv[l, :, PV_NK:PV_NK + 64] = inp['nsa_k_gain'][l][None, :]
    pe = np.zeros((NL, 128, 64), np.float32)
    for l in range(NL):
        pe[l, :, 0:32] = np.tile(inp['cmp_pos_k'][l].T, (2, 1))
        pe[l, :, 32:64] = np.tile(inp['cmp_pos_v'][l].T, (2, 1))
    return pv, pe


_CACHE = {}


def make_maps(inputs, ncores=8):
    f32 = lambda a: np.ascontiguousarray(np.asarray(a, dtype=np.float32))
    inp = {k: f32(v) for k, v in inputs.items()}
    cf, cb = host_consts()
    pv, pe = host_pvec(inp)
    shared = {k: inp[k] for k in ['ffn1_wi', 'ffn1_wo', 'w_in', 'mla_w_uq', 'mla_w_ukv', 'cmp_wk1', 'cmp_wk2', 'cmp_wv1', 'cmp_wv2',
                                  'proj_sb', 'proj_mla', 'proj_nsa', 'w_out', 'ffn2_wi', 'ffn2_wo']}
    shared.update({'pvec': pv, 'pekv': pe, 'cf': cf, 'cb': cb})
    maps = []
    for c in range(ncores):
        m = dict(shared)
        m['x'] = np.ascontiguousarray(inp['x'][c])
        maps.append(m)
    return maps


def kernel(**inputs):
    if 'nc' not in _CACHE:
        nc = bass.Bass("TRN2", target_bir_lowering=False)
        K(nc).build()
        _CACHE['nc'] = nc
    nc = _CACHE['nc']
    maps = make_maps(inputs, 8)
    res = run_bass_kernel_spmd(nc, maps, core_ids=list(range(8)))
    return np.stack([np.asarray(r['out'], dtype=np.float32) for r in res.results], axis=0)


def _alloc_top(self, name, shape, dtype, parts=128):
    n = int(np.prod(shape)) * DSIZE[dtype]
    off = (self.limit - n) // 32 * 32
    assert off >= self.top, ('SBUF phase overflow (top)', name, off, self.top)
    b = self.P.sb_at(name, off, shape, dtype, parts)
    if not hasattr(self, 'tops'):
        self.tops = []
    self.tops.append((b, self.limit))
    self.limit = off
    return b


def _free_top(self, b):
    tb_, prev = self.tops.pop()
    assert tb_ is b, 'top allocations must be freed LIFO'
    self.P.free(b)
    self.limit = prev


Alloc.get_top = _alloc_top
Alloc.free_top = _free_top


def bc4(ap, n=4):
    return ap.unsqueeze(1).broadcast_to([ap.shape[0], n, ap.shape[1]])


def v3(ap, c):
    return ap.rearrange('p (c t) -> p c t', c=c)


def load_w_in(self, l, c0, n, buf, key=None):
    wv = self.W['w_in'][l].rearrange('(c p) f -> p c f', p=128)
    self.dma('pool', buf[:, :, 0:n], wv[:, :, c0:c0 + n], (), [buf if key is None else (buf, key)])


K.load_w_in = load_w_in


def mixer(self, l):
    A, P = self.A, self.P
    A.release()
    br = self.branches if hasattr(self, 'branches') else ('nsa', 'sb', 'mla')
    self.hT_freed = False
    hT = A.get_top('hT', [8, S], BF16)
    mk0 = A.mark()
    tmp = self.norm_tmp()
    for i in range(NT):
        self.norm_tile(i, PV_GMIX, hT, i * 128, i, tmp, 6 + (i % 2))
    A.release(mk0)
    self.hT = hT
    self.hk = lambda t0, t1: [(hT, i) for i in range(t0, t1)]
    cm = self.cm = A.get('cm', [768], BF16)
    self.dma('pool', cm[:], self.W['cb'][:, CB_NSTRICT:CB_NSTRICT + 768], (), [cm])
    self.nstrict, self.nincl, self.nfar = cm[:, 0:128], cm[:, 128:256], cm[:, 256:384]
    self.tri, self.nones, self.ones = cm[:, 384:512], cm[:, 512:640], cm[:, 640:768]
    self.oT = {}
    if 'nsa' in br:
        self.oT['nsa'] = A.get('oT_nsa', [4, S], BF16)
        self.nsa(l)
    if 'sb' in br:
        self.oT['sb'] = A.get('oT_sb', [4, S], BF16)
        self.sb(l)
    if 'mla' in br:
        self.oT['mla'] = A.get('oT_mla', [4, S], BF16)
        self.mla(l)
    if self.debug:
        for bi, b in enumerate(('sb', 'mla', 'nsa')):
            if b in self.oT:
                self.dma('pool', self.dbg[bi], self.oT[b][:, :, :], [self.oT[b]], [(self.P.dram('dbg', self.dbg), bi)])
    if getattr(self, 'hT_freed', False) and not getattr(self, 'skip_merge', False):
        hT = self.hT = A.get_top('hT', [8, S], BF16)
        self.hk = lambda t0, t1: [(hT, i) for i in range(t0, t1)]
        mk1 = A.mark()
        tmp = self.norm_tmp()
        for i in range(NT):
            self.norm_tile(i, PV_GMIX, hT, i * 128, i, tmp, 6 + (i % 2), use_saved=True)
        A.release(mk1)
        self.hT_freed = False
    if not getattr(self, 'skip_merge', False):
        self.merge(l)
    if not self.hT_freed:
        A.free_top(hT)
    A.release()


K.mixer = mixer


def run_pipeline(blocks, nstage):
    n = len(blocks)
    for t in range(n + nstage - 1):
        for k in reversed(range(nstage)):
            b = t - k
            if 0 <= b < n:
                blocks[b][k]()


def sb(self, l):
    A, P, hT = self.A, self.P, self.hT
    m0 = A.mark()
    wq = [A.get('sbw%d' % j, [8, 384], BF16) for j in range(2)]
    qP = [[A.get('sbq%d%d' % (j, h), [S], BF16) for h in range(2)] for j in range(2)]
    kT = [A.get('sbk%d' % j, [S], BF16) for j in range(2)]
    vv = [A.get('sbv%d' % j, [NT, 128], BF16) for j in range(2)]
    e_t = [A.get('sbe%d' % j, [512], F32) for j in range(2)]
    sp_t = [A.get('sbs%d' % j, [512], BF16) for j in range(3)]
    w_t = [A.get('sbp%d' % j, [512], BF16) for j in range(3)]
    sacc = [A.get('sba%d' % j, [512], BF16) for j in range(2)]
    oT = self.oT['sb']
    ps = self.ps
    ident = self.ident
    for j in range(2):
        for h in range(2):
            self.memset('dve', qP[j][h][:], 0.0, [qP[j][h]])
    wv = self.W['w_in'][l].rearrange('(c p) f -> p c f', p=128)

    def proj_chunks(pr):
        w = wq[pr % 2]
        q_, k_, v_ = qP[pr % 2], kT[pr % 2], vv[pr % 2]
        chunks = []

        def c_dma():
            for j, c0 in enumerate((O_SBQ, O_SBK, O_SBV)):
                self.dma('pool', w[:, :, j * 128:(j + 1) * 128], wv[:, :, c0 + pr * 128:c0 + (pr + 1) * 128], (), [(w, j)])
        chunks.append(c_dma)
        for tg in range(4):
            for j in range(2):
                def c_qk(tg=tg, j=j):
                    tcs = slice(tg * 512, (tg + 1) * 512)
                    pb = ps[6 + (tg * 2 + j) % 2]
                    for k_i in range(8):
                        self.mm(pb[:, :], w[:, k_i, j * 128:(j + 1) * 128], hT[:, k_i, tcs], k_i == 0, k_i == 7,
                                self.hk(4 * tg, 4 * tg + 4) + [(w, j)], [pb])
                    if j == 0:
                        for h in range(2):
                            self.act(q_[h][64 * h:64 * h + 64, tcs], pb[64 * h:64 * h + 64, :], AF.Copy, [pb], [(q_[h], tg)], scale=0.125)
                    else:
                        self.cp('dve', k_[:, tcs], pb[:, :], [pb], [(k_, tg)])
                chunks.append(c_qk)
        for i4 in range(4):
            def c_v(i4=i4):
                pb = ps[6 + i4 % 2]
                for ii in range(4):
                    i = i4 * 4 + ii
                    for k_i in range(8):
                        self.mm(pb[:, ii * 128:(ii + 1) * 128], hT[:, k_i, i * 128:(i + 1) * 128], w[:, k_i, 256:384], k_i == 0 and ii == 0, k_i == 7,
                                [(hT, i), (w, 2)], [pb], skip=True)
                self.cp('dve', v_[:, i4 * 4:i4 * 4 + 4, :], v3(pb[:, :], 4), [pb], [(v_, i4)])
            chunks.append(c_v)
        return chunks

    for c_ in proj_chunks(0):
        c_()
    for pr in range(4):
        q_, k_, v_ = qP[pr % 2], kT[pr % 2], vv[pr % 2]
        blocks = []
        bi = 0
        gi = 0
        for hh in range(2):
            pp = slice(64 * hh, 64 * hh + 64)
            for qg in range(4):
                sa = sacc[gi % 2]
                pc = ps[4 + gi % 2]
                gi += 1
                top = 4 * qg + 3
                for kb in range(top, -1, -1):
                    a = kb - 4 * qg
                    diag = a >= 0
                    c0 = 128 * a if diag else 0
                    cs = slice(c0, 512)
                    qc = slice(qg * 512 + c0, (qg + 1) * 512)
                    kc = slice(kb * 128, (kb + 1) * 128)
                    pa, pbk = ps[bi % 2], ps[2 + bi % 2]
                    et, st, wt = e_t[bi % 2], sp_t[bi % 3], w_t[bi % 3]
                    bi += 1
                    qh = q_[hh]
                    rq = [(qh, qg), (k_, kb // 4)]

                    def s0(pa=pa, cs=cs, c0=c0, kc=kc, qc=qc, diag=diag, rq=rq, qh=qh):
                        self.mm(pa[:, cs], k_[:, kc], qh[:, qc], True, not diag, rq, [pa])
                        if diag:
                            self.mm(pa[:, c0:c0 + 128], ident[:], self.nstrict, False, True, [self.ident, self.cm], [pa])

                    def s1(pa=pa, cs=cs, et=et, st=st):
                        self.act(et[:, cs], pa[:, cs], AF.Exp, [pa], [et])
                        self.act(st[:, cs], et[:, cs], AF.Ln, [et], [st], bias=1.0)

                    def s2(pbk=pbk, cs=cs, c0=c0, kc=kc, qc=qc, diag=diag, rq=rq, qh=qh, st=st, sa=sa, kb=kb, top=top):
                        if kb == top:
                            self.memset('dve', sa[:], 0.0, [sa])
                        self.mm(pbk[:, cs], self.tri, st[:, cs], True, False, [st, self.cm], [pbk])
                        if kb < top:
                            self.mm(pbk[:, cs], self.nones, sa[:, cs], False, False, [sa, self.cm], [pbk])
                        self.mm(pbk[:, cs], k_[:, kc], qh[:, qc], False, not diag, rq, [pbk])
                        if diag:
                            self.mm(pbk[:, c0:c0 + 128], ident[:], self.nstrict, False, True, [self.ident, self.cm], [pbk])
                        if kb > 0:
                            self.tt('dve', sa[:, cs], sa[:, cs], st[:, cs], ALU.add, [sa, st], [sa])

                    def s3(pbk=pbk, cs=cs, wt=wt):
                        self.act(wt[:, cs], pbk[:, cs], AF.Exp, [pbk], [wt])

                    def s4(pc=pc, cs=cs, wt=wt, kb=kb, top=top, pp=pp, qg=qg, hh=hh):
                        self.mm(pc[:, cs], v_[:, kb, :], wt[:, cs], kb == top, kb == 0, [(v_, kb // 4), wt], [pc], skip=True)
                        if kb == 0:
                            self.cp('dve', oT[pp, pr, qg * 512:(qg + 1) * 512], pc[pp, :], [pc], [(oT, (pr, hh, qg))])

                    blocks.append([s0, s1, s2, s3, s4])
        if pr + 1 < 4:
            nop = lambda: None
            ch = proj_chunks(pr + 1)
            step = max(1, (len(blocks) - 8) // len(ch))
            for ci, c_ in enumerate(ch):
                blocks.insert(min(len(blocks), 2 + ci * (step + 1)), [c_, nop, nop, nop, nop])
        run_pipeline(blocks, 5)
    A.release(m0)


K.sb = sb


def headnorm_rope(self, src, nh, hd, rope0, half, gain_ap, cos_ap, sin_ap, dst, scale, tmpb, R, Wr, gain_full=None, gR=None):
    sq, qn, st, r1, r2 = tmpb['sq'], tmpb['qn'], tmpb['st'], tmpb['r1'], tmpb['r2']
    n = nh * hd
    npart = src.shape[0]
    sqv = sq[0:npart, 0:n].rearrange('p (h d) -> p h d', h=nh)
    qnv = qn[0:npart, 0:n].rearrange('p (h d) -> p h d', h=nh)
    self.act(sqv, src, AF.Square, R, [sq])
    ss = st[0:npart, 0:nh]
    sd = st[0:npart, nh:2 * nh]
    self.P.op('dve', lambda e: e.tensor_reduce(out=ss, in_=sqv, axis=AX.X, op=ALU.add), [sq], [(st, 0)])
    self.act(sd, ss, AF.Sqrt, [(st, 0)], [(st, 1)], scale=1.0 / hd, bias=EPS)
    self.recip(ss, sd, [(st, 1)], [(st, 0)])
    if scale != 1.0:
        self.ts('dve', ss, ss, float(scale), None, ALU.mult, None, [(st, 0)], [(st, 0)])
    self.tt('dve', qnv, src, ss.unsqueeze(2).broadcast_to([npart, nh, hd]), ALU.mult, list(R) + [(st, 0)], [qn])
    if gain_full is not None:
        self.tt('dve', qnv, qnv, gain_full, ALU.mult, [qn] + list(gR), [qn])
    else:
        self.tt('dve', qnv, qnv, gain_ap.unsqueeze(1).broadcast_to([npart, nh, hd]), ALU.mult, [qn, self.pv], [qn])
    x1 = qnv[:, :, rope0:rope0 + half]
    x2 = qnv[:, :, rope0 + half:rope0 + 2 * half]
    cb = cos_ap.unsqueeze(1).broadcast_to([npart, nh, half])
    sb_ = sin_ap.unsqueeze(1).broadcast_to([npart, nh, half])
    r1v = r1[0:npart, 0:nh * half].rearrange('p (h d) -> p h d', h=nh)
    r2v = r2[0:npart, 0:nh * half].rearrange('p (h d) -> p h d', h=nh)
    cR = [qn, self.cfm]
    self.tt('dve', r1v, x2, sb_, ALU.mult, cR, [r1])
    self.tt('dve', r2v, x1, sb_, ALU.mult, cR, [r2])
    if rope0 > 0:
        self.cp('act', dst[:, :, 0:rope0], qnv[:, :, 0:rope0], [qn], [(Wr[0], 'a')] if isinstance(Wr[0], Buf) else Wr)
    o1 = dst[:, :, rope0:rope0 + half]
    o2 = dst[:, :, rope0 + half:rope0 + 2 * half]
    self.tt('dve', x1, x1, cb, ALU.mult, cR, [qn])
    self.tt('dve', x2, x2, cb, ALU.mult, cR, [qn])
    self.tt('dve', o1, x1, r1v, ALU.subtract, [qn, r1], Wr)
    self.tt('dve', o2, x2, r2v, ALU.add, [qn, r2], Wr)


K.headnorm_rope = headnorm_rope


def hn_tmp(self, A, n, nh, half):
    return {'sq': A.get('hn_sq', [n], F32), 'qn': A.get('hn_qn', [n], F32), 'st': A.get('hn_st', [2 * nh], F32),
            'r1': A.get('hn_r1', [nh * half], F32), 'r2': A.get('hn_r2', [nh * half], F32)}


K.hn_tmp = hn_tmp


def mla(self, l):
    A, P, hT, ps, ident = self.A, self.P, self.hT, self.ps, self.ident
    m0 = A.mark()
    oT = self.oT['mla']
    cfm = self.cfm = A.get('cfm_mla', [512], F32)
    self.dma('sp', cfm[:], self.W['cf'][:, CF_MC:CF_MC + 512], (), [cfm])
    wuq = A.get('wuq', [2, 768], BF16)
    self.dma('pool', wuq[:, :, :], self.W['mla_w_uq'][l].rearrange('(c p) f -> p c f', p=128), (), [wuq])
    wukv = A.get('wukv', [1024], BF16)
    self.dma('pool', wukv[:], self.W['mla_w_ukv'][l], (), [wukv])
    cqT = A.get('cqT', [2, S], BF16)
    ckvT = A.get('ckvT', [S], BF16)
    krt = A.get('krt', [NT, 32], F32)
    ms1 = A.mark()
    wc = A.get('mla_wc', [8, 416], BF16)
    self.load_w_in(l, O_CQ, 416, wc)
    st = A.get('mla_st', [8], F32)
    junk = A.get('mla_junk', [256], BF16)
    xq = [A.get('mla_xq%d' % j, [384], BF16) for j in range(2)]
    for i in range(NT):
        pb = ps[6 + i % 2]
        for k in range(8):
            self.mm(pb[:, 0:416], hT[:, k, i * 128:(i + 1) * 128], wc[:, k, :], k == 0, k == 7, [(hT, i), wc], [pb])
        x_ = xq[i % 2]
        sk = (st, i % 2)
        o = (i % 2) * 4
        self.memset('dve', st[:, o:o + 2], 0.0, [sk])
        self.act(junk[:, 0:256], pb[:, 0:256], AF.Square, [pb, sk], [junk, sk], accum=st[:, o:o + 1])
        self.act(junk[:, 0:128], pb[:, 256:384], AF.Square, [pb, sk], [junk, sk], accum=st[:, o + 1:o + 2])
        self.act(st[:, o + 2:o + 3], st[:, o:o + 1], AF.Sqrt, [sk], [sk], scale=1.0 / 256, bias=EPS)
        self.act(st[:, o + 3:o + 4], st[:, o + 1:o + 2], AF.Sqrt, [sk], [sk], scale=1.0 / 128, bias=EPS)
        self.recip(st[:, o:o + 2], st[:, o + 2:o + 4], [sk], [sk])
        self.act(x_[:, 0:256], pb[:, 0:256], AF.Copy, [pb, sk], [x_], scale=st[:, o:o + 1])
        self.act(x_[:, 256:384], pb[:, 256:384], AF.Copy, [pb, sk], [x_], scale=st[:, o + 1:o + 2])
        self.cp('dve', krt[:, i, :], pb[:, 384:416], [pb], [(krt, i)])
        pt = ps[4 + i % 2]
        for c in range(3):
            self.mm(pt[:, c * 128:(c + 1) * 128], x_[:, c * 128:(c + 1) * 128], ident[:], c == 0, True, [x_, self.ident], [pt], skip=True)
        self.tt('dve', cqT[:, :, i * 128:(i + 1) * 128], v3(pt[:, 0:256], 2),
                self.pv[:, PV_QN:PV_QN + 2].unsqueeze(2).broadcast_to([128, 2, 128]), ALU.mult, [pt, self.pv], [(cqT, i)])
        self.ts('dve', ckvT[:, i * 128:(i + 1) * 128], pt[:, 256:384], self.pv[:, PV_KVN:PV_KVN + 1], None, ALU.mult, None, [pt, self.pv], [(ckvT, i)])
    A.release(ms1)
    A.free_top(self.hT)
    self.hT_freed = True
    m1 = A.mark()
    qks = [A.get('mla_qkT%d' % j, [4, S], BF16) for j in range(2)]
    vvs = [A.get('mla_v%d' % j, [NT, 128], BF16) for j in range(2)]
    for j in range(2):
        self.memset('dve', qks[j][:, :, :], 0.0, [qks[j]])
    sq = A.get('mq_sq', [768], F32)
    qn = A.get('mq_qn', [768], F32)
    st = A.get('mq_st', [16], F32)
    r1 = A.get('mq_r1', [128], F32)
    r2 = A.get('mq_r2', [128], F32)
    qst = A.get('mq_qst', [768], F32)
    g4 = A.get('mla_g4', [4, 96], F32)
    qr_ = A.get('mla_qr', [768], BF16)
    p_t = [A.get('mla_p%d' % j, [512], BF16) for j in range(3)]
    rec = [A.get('mla_rec%d' % j, [512], F32) for j in range(2)]
    gq = self.pv[:, PV_GQ:PV_GQ + 96]
    gk = self.pv[:, PV_GK:PV_GK + 96]
    self.ts('dve', g4[:, 0, :], gq, float(96 ** -0.5), None, ALU.mult, None, [self.pv], [g4])
    self.ts('dve', g4[:, 1, :], gq, float(96 ** -0.5), None, ALU.mult, None, [self.pv], [g4])
    self.cp('dve', g4[:, 2, :], gk, [self.pv], [g4])
    self.cp('dve', g4[:, 3, :], gk, [self.pv], [g4])
    NPS = 14
    nop = lambda: None
    v4 = lambda ap: ap.rearrange('p (t h d) -> p t h d', t=2, h=4)
    v8 = lambda ap: ap.rearrange('p (h d) -> p h d', h=8)

    def prep_stages(pr, i2):
        qk, vv = qks[pr % 2], vvs[pr % 2]
        pqs = [ps[6], ps[7]]
        tiles = (2 * i2, 2 * i2 + 1)
        qst4, qn4, qr4 = v4(qst[:, :]), v4(qn[:, :]), v4(qr_[:, :])
        cos = cfm[:, i2 * 32:(i2 + 1) * 32].rearrange('p (t d) -> p t d', t=2).unsqueeze(2).broadcast_to([128, 2, 4, 16])
        sin = cfm[:, 256 + i2 * 32:256 + (i2 + 1) * 32].rearrange('p (t d) -> p t d', t=2).unsqueeze(2).broadcast_to([128, 2, 4, 16])
        r14 = r1[:, :].rearrange('p (t h d) -> p t h d', t=2, h=4)
        r24 = r2[:, :].rearrange('p (t h d) -> p t h d', t=2, h=4)
        ss, sd = st[:, 0:8], st[:, 8:16]
        x1, x2 = qn4[:, :, :, 64:80], qn4[:, :, :, 80:96]

        def p0():
            for t_, i in enumerate(tiles):
                pq = pqs[t_]
                for c in range(2):
                    self.mm(pq[:, 0:192], cqT[:, c, i * 128:(i + 1) * 128], wuq[:, c, pr * 192:(pr + 1) * 192], c == 0, c == 1, [(cqT, i), wuq], [pq])
                self.mm(pq[:, 256:512], ckvT[:, i * 128:(i + 1) * 128], wukv[:, pr * 256:(pr + 1) * 256], False, True, [(ckvT, i), wukv], [pq], skip=True)

        def p1():
            for t_, i in enumerate(tiles):
                pq = pqs[t_]
                kvv = v3(pq[:, 256:512], 2)
                self.cp('act', qst4[:, t_, 0:2, :], v3(pq[:, 0:192], 2), [pq], [(qst, (t_, 0))])
                self.cp('act', qst4[:, t_, 2:4, 0:64], kvv[:, :, 0:64], [pq], [(qst, (t_, 1))])
                self.cp('act', qst4[:, t_, 2:4, 64:96], krt[:, i, :].unsqueeze(1).broadcast_to([128, 2, 32]), [(krt, i)], [(qst, (t_, 2))])

        def p2():
            self.act(sq[:, :], qst[:, :], AF.Square, [qst], [sq])
            for t_, i in enumerate(tiles):
                kvv = v3(pqs[t_][:, 256:512], 2)
                self.cp('dve', vv[:, i, :].rearrange('p (h d) -> p h d', h=2), kvv[:, :, 64:128], [pqs[t_]], [(vv, i)])

        def p3():
            self.P.op('dve', lambda e: e.tensor_reduce(out=ss, in_=v8(sq[:, :]), axis=AX.X, op=ALU.add), [sq], [(st, 0)])

        def p4():
            self.act(sd, ss, AF.Ln, [(st, 0)], [(st, 1)], scale=1.0 / 96, bias=EPS)
            self.act(ss, sd, AF.Exp, [(st, 1)], [(st, 0)], scale=-0.5)

        def p5():
            self.tt('dve', v8(qn[:, :]), v8(qst[:, :]), ss.unsqueeze(2).broadcast_to([128, 8, 96]), ALU.mult, [qst, (st, 0)], [qn])
            self.tt('dve', qn4, qn4, g4[:, :, :].unsqueeze(1).broadcast_to([128, 2, 4, 96]), ALU.mult, [qn, g4], [qn])
            self.cp('dve', qr4[:, :, :, 0:64], qn4[:, :, :, 0:64], [qn], [qr_])
            cR = [qn, cfm]
            self.tt('dve', r14, x2, sin, ALU.mult, cR, [r1])
            self.tt('dve', r24, x1, sin, ALU.mult, cR, [r2])
            self.tt('dve', x1, x1, cos, ALU.mult, cR, [qn])
            self.tt('dve', x2, x2, cos, ALU.mult, cR, [qn])
            self.tt('dve', qr4[:, :, :, 64:80], x1, r14, ALU.subtract, [qn, r1], [qr_])
            self.tt('dve', qr4[:, :, :, 80:96], x2, r24, ALU.add, [qn, r2], [qr_])

        pts = [ps[4], ps[5]]

        def p6():
            for t_, i in enumerate(tiles):
                pt = pts[t_]
                for j in range(4):
                    self.mm(pt[0:96, j * 128:(j + 1) * 128], qr4[:, t_, j, :], ident[:], j == 0, True, [qr_, self.ident], [pt], skip=True)

        def p7():
            for t_, i in enumerate(tiles):
                self.cp('act', qk[0:96, :, i * 128:(i + 1) * 128], v3(pts[t_][0:96, :], 4), [pts[t_]], [(qk, i)])

        return [p0, p1, p2, p3, p4, p5] + [nop] * (NPS - 8) + [p6, p7]

    SP = 10
    NCH = NT // 2
    blocks = []
    for i2 in range(NCH):
        blocks.append(prep_stages(0, i2))
        for _ in range(SP - 1):
            blocks.append([nop] * NPS)
    run_pipeline(blocks, NPS)
    for pr in range(4):
        qk, vv = qks[pr % 2], vvs[pr % 2]
        blocks = []
        bi = 0
        gi = 0
        for hh in range(2):
            pp = slice(64 * hh, 64 * hh + 64)
            for qg in range(4):
                pc, pd = ps[2], ps[3]
                rc = rec[gi % 2]
                gi += 1
                top = 4 * qg + 3
                for kb in range(0, top + 1):
                    a = kb - 4 * qg
                    diag = a >= 0
                    c0 = 128 * a if diag else 0
                    cs = slice(c0, 512)
                    qc = slice(qg * 512 + c0, (qg + 1) * 512)
                    kc = slice(kb * 128, (kb + 1) * 128)
                    pa = ps[bi % 2]
                    pt_ = p_t[bi % 3]
                    bi += 1
                    rq = [(qk, i_) for i_ in range(4 * qg, 4 * qg + 4)] + [(qk, kb)]

                    def s0(pa=pa, cs=cs, c0=c0, kc=kc, qc=qc, diag=diag, rq=rq, hh=hh, qk=qk):
                        self.mm(pa[:, cs], qk[:, 2 + hh, kc], qk[:, hh, qc], True, not diag, rq, [pa])
                        if diag:
                            self.mm(pa[:, c0:c0 + 128], ident[:], self.nincl, False, True, [self.ident, self.cm], [pa])

                    def s1(pa=pa, cs=cs, pt_=pt_):
                        self.act(pt_[:, cs], pa[:, cs], AF.Exp, [pa], [pt_])

                    def s2(pc=pc, pd=pd, cs=cs, pt_=pt_, kb=kb, top=top, pp=pp, rc=rc, qg=qg, hh=hh, vv=vv, pr=pr):
                        self.mm(pc[:, cs], vv[:, kb, :], pt_[:, cs], kb == 0, kb == top, [(vv, kb), pt_], [pc])
                        self.mm(pd[:, cs], self.ones, pt_[:, cs], kb == 0, kb == top, [self.cm, pt_], [pd])
                        if kb == top:
                            self.act(rc[pp, :], pd[pp, :], AF.Ln, [pd], [rc])
                            self.act(rc[pp, :], rc[pp, :], AF.Exp, [rc], [rc], scale=-1.0)
                            self.tt('dve', oT[pp, pr, qg * 512:(qg + 1) * 512], pc[pp, :], rc[pp, :], ALU.mult, [pc, rc], [(oT, (pr, hh, qg))])

                    blocks.append([s0, s1, s2] + [nop] * (NPS - 3))
        if pr + 1 < 4:
            assert len(blocks) >= SP * NCH
            for i2 in range(NCH):
                blocks.insert(i2 * SP, prep_stages(pr + 1, i2))
        run_pipeline(blocks, NPS)
    A.release(m0)


K.mla = mla


def nsa(self, l):
    A, P, hT, ps, ident = self.A, self.P, self.hT, self.ps, self.ident
    m0 = A.mark()
    oT = self.oT['nsa']
    qT = A.get('nq', [4, S], BF16)
    kTs = A.get('nks', [2, S], BF16)
    kTw = A.get('nkw', [2, S], BF16)
    vs = A.get('nvs', [NT, 2, 65], BF16)
    vw = A.get('nvw', [NT, 2, 65], BF16)
    gts = A.get('ngt', [NT, 24], F32)
    kcT = A.get('nkc', [2, 128], BF16)
    cmpV = A.get('ncv', [2, 97], BF16)
    cfm = self.cfm = A.get('cfm_nsa', [1088], F32)
    self.dma('sp', cfm[:], self.W['cf'][:, CF_NC:CF_NC + 1088], (), [cfm])
    self.memset('dve', vs[:, :, :, :], 1.0, [vs])
    self.memset('dve', vw[:, :, :, :], 1.0, [vw])
    self.memset('dve', kTs[:, :, :], 0.0, [kTs])
    self.memset('dve', kTw[:, :, :], 0.0, [kTw])
    self.memset('dve', kcT[:, :, :], 0.0, [kcT])
    gq = self.pv[:, PV_NQ:PV_NQ + 64]
    gk = self.pv[:, PV_NK:PV_NK + 64]
    m1 = A.mark()
    NB = 2
    bufA = [A.get('npA%d' % j, [768], F32) for j in range(NB)]
    bufB = [A.get('npB%d' % j, [768], F32) for j in range(NB)]
    sts = [A.get('npst%d' % j, [24], F32) for j in range(NB)]
    qkrs = [A.get('npqk%d' % j, [12, 64], BF16) for j in range(NB)]
    g12 = A.get('npg12', [12, 64], F32)
    self.ts('dve', g12[:, 0:8, :], gq.unsqueeze(1).broadcast_to([128, 8, 64]), 0.125, None, ALU.mult, None, [self.pv], [g12])
    self.cp('dve', g12[:, 8:12, :], gk.unsqueeze(1).broadcast_to([128, 4, 64]), [self.pv], [g12])
    wN = A.get_top('nw', [8, 1304], BF16)
    self.load_w_in(l, O_NQ, 512, wN, 0)
    wv_ = self.W['w_in'][l].rearrange('(c p) f -> p c f', p=128)
    self.dma('pool', wN[:, :, 512:1304], wv_[:, :, O_CK:O_CK + 792], (), [(wN, 1)])
    v12 = lambda ap: ap.rearrange('p (h d) -> p h d', h=12)
    nop = lambda: None
    NPN = 10
    SPN = 5

    def prep_stages(i):
        tc_ = slice(i * 128, (i + 1) * 128)
        pq, pk, pg, pt = ps[6 + i % 2], ps[4 + i % 2], ps[2 + i % 2], ps[i % 2]
        bA, bB, st, qkr = bufA[i % NB], bufB[i % NB], sts[i % NB], qkrs[i % NB]
        A12, B12 = v12(bA[:, :]), v12(bB[:, :])
        ss, sd = st[:, 0:12], st[:, 12:24]
        cos = cfm[:, i * 32:(i + 1) * 32].unsqueeze(1).broadcast_to([128, 12, 32])
        sin = cfm[:, 512 + i * 32:512 + (i + 1) * 32].unsqueeze(1).broadcast_to([128, 12, 32])
        r1 = bA[:, 0:384].rearrange('p (h d) -> p h d', h=12)
        r2 = bA[:, 384:768].rearrange('p (h d) -> p h d', h=12)
        x1, x2 = B12[:, :, 0:32], B12[:, :, 32:64]

        def p0():
            for k_i in range(8):
                self.mm(pq[:, :], hT[:, k_i, tc_], wN[:, k_i, 0:512], k_i == 0, k_i == 7, [(hT, i), (wN, 0)], [pq])
            for k_i in range(8):
                self.mm(pk[:, :], hT[:, k_i, tc_], wN[:, k_i, 768:1280], k_i == 0, k_i == 7, [(hT, i), (wN, 1)], [pk])
            for k_i in range(8):
                self.mm(pg[:, 0:24], hT[:, k_i, tc_], wN[:, k_i, 1280:1304], k_i == 0, k_i == 7, [(hT, i), (wN, 1)], [pg])

        def p1():
            self.cp('act', A12[:, 0:8, :], v3(pq[:, :], 8), [pq], [(bA, 0)])
            self.cp('act', A12[:, 8:10, :], v3(pk[:, 0:128], 2), [pk], [(bA, 1)])
            self.cp('act', A12[:, 10:12, :], v3(pk[:, 256:384], 2), [pk], [(bA, 2)])
            self.act(gts[:, i, :], pg[:, 0:24], AF.Tanh, [pg], [(gts, i)], scale=0.5)

        def p2():
            self.act(bB[:, :], bA[:, :], AF.Square, [bA], [bB])
            self.ts('dve', gts[:, i, :], gts[:, i, :], 0.5, 0.5, ALU.mult, ALU.add, [(gts, i)], [(gts, i)])
            self.cp('dve', vs[:, i, :, 0:64], v3(pk[:, 128:256], 2), [pk], [(vs, i)])
            self.cp('dve', vw[:, i, :, 0:64], v3(pk[:, 384:512], 2), [pk], [(vw, i)])

        def p3():
            self.P.op('dve', lambda e: e.tensor_reduce(out=ss, in_=B12, axis=AX.X, op=ALU.add), [bB], [(st, 0)])

        def p4():
            self.act(sd, ss, AF.Ln, [(st, 0)], [(st, 1)], scale=1.0 / 64, bias=EPS)
            self.act(ss, sd, AF.Exp, [(st, 1)], [(st, 0)], scale=-0.5)

        def p5():
            self.tt('dve', B12, A12, ss.unsqueeze(2).broadcast_to([128, 12, 64]), ALU.mult, [bA, (st, 0)], [bB])
            self.tt('dve', B12, B12, g12[:, :, :], ALU.mult, [bB, g12], [bB])
            cR = [bB, cfm]
            self.tt('dve', r1, x2, sin, ALU.mult, cR, [(bA, 'r1')])
            self.tt('dve', r2, x1, sin, ALU.mult, cR, [(bA, 'r2')])
            self.tt('dve', x1, x1, cos, ALU.mult, cR, [bB])
            self.tt('dve', x2, x2, cos, ALU.mult, cR, [bB])
            self.tt('dve', qkr[:, :, 0:32], x1, r1, ALU.subtract, [bB, (bA, 'r1')], [qkr])
            self.tt('dve', qkr[:, :, 32:64], x2, r2, ALU.add, [bB, (bA, 'r2')], [qkr])

        def p8():
            for r in range(4):
                for g in range(2):
                    self.mm(pt[64 * g:64 * g + 64, r * 128:(r + 1) * 128], qkr[:, 4 * g + r, :], ident[:], r == 0, True, [qkr, self.ident], [pt], skip=True)
            for b in range(2):
                for g in range(2):
                    self.mm(pq[64 * g:64 * g + 64, b * 128:(b + 1) * 128], qkr[:, 8 + 2 * b + g, :], ident[:], b == 0, True, [qkr, self.ident], [pq], skip=True)

        def p9():
            self.cp('act', qT[:, :, tc_], v3(pt[:, :], 4), [pt], [(qT, i)])
            for g in range(2):
                gs = slice(64 * g, 64 * g + 64)
                self.cp('act', kTs[gs, g, tc_], pq[gs, 0:128], [pq], [(kTs, i)])
                self.cp('act', kTw[gs, g, tc_], pq[gs, 128:256], [pq], [(kTw, i)])

        return [p0, p1, p2, p3, p4, p5, nop, nop, p8, p9]

    blocks = []
    for i in range(NT):
        blocks.append(prep_stages(i))
        for _ in range(SPN - 1):
            blocks.append([nop] * NPN)
    run_pipeline(blocks, NPN)
    A.release(m1)
    m1 = A.mark()
    ckT = A.get('nck', [S], BF16)
    cvT = A.get('ncvT', [S], BF16)
    tb = self.hn_tmp(A, 64, 1, 32)
    for tg in range(4):
        for j, dst in ((0, ckT), (1, cvT)):
            pb = ps[(tg * 2 + j) % 2]
            for k_i in range(8):
                self.mm(pb[:, :], wN[:, k_i, 512 + j * 128:640 + j * 128], hT[:, k_i, tg * 512:(tg + 1) * 512], k_i == 0, k_i == 7,
                        self.hk(4 * tg, 4 * tg + 4) + [(wN, 1)], [pb])
            self.cp('act' if j == 0 else 'dve', dst[:, tg * 512:(tg + 1) * 512], pb[:, :], [pb], [(dst, tg)])
    A.free_top(wN)
    w1 = [A.get('nw1%d' % j, [32, 128], BF16) for j in range(2)]
    w2 = [A.get('nw2%d' % j, [64], BF16) for j in range(2)]
    peT = A.get('npe', [64], BF16)
    gx = [A.get('ngx%d' % j, [128], F32) for j in range(4)]
    hid = A.get('nhid', [128], BF16)
    kc = A.get('nkcs', [64], BF16)
    for j, nm in enumerate(('cmp_wk1', 'cmp_wv1')):
        src = self.W[nm][l].rearrange('(l d) h -> d l h', d=64)
        self.dma('pool', w1[j][0:64, :, :], src, (), [(w1[j], 0)])
        self.dma('pool', w1[j][64:128, :, :], src, (), [(w1[j], 1)])
    for j, nm in enumerate(('cmp_wk2', 'cmp_wv2')):
        self.dma('pool', w2[j][:], self.W[nm][l], (), [w2[j]])
    self.dma('pool', peT[:], self.W['pekv'][l], (), [peT])
    self.dma('pool', cmpV[0:127, 0, 64:97], self.W['cb'][0:127, CB_OV:CB_OV + 33], (), [(cmpV, 'ov0')])
    self.dma('pool', cmpV[0:127, 1, 64:97], self.W['cb'][0:127, CB_OV:CB_OV + 33], (), [(cmpV, 'ov1')])
    cnt = 0
    for j, cT in ((0, ckT), (1, cvT)):
        for g in range(2):
            pg_ = slice(64 * g, 64 * g + 64)
            ph, po = ps[cnt % 2], ps[2 + cnt % 2]
            cnt += 1
            for l_ in range(32):
                self.mm(ph[:, 0:127], w1[j][pg_, l_, :], cT[pg_, l_:l_ + 16 * 126 + 1:16], l_ == 0, False, [cT, (w1[j], g)], [ph])
            for l_ in range(32):
                self.mm(ph[:, 0:127], w1[j][pg_, l_, :], peT[pg_, j * 32 + l_:j * 32 + l_ + 1].broadcast_to([64, 127]), False, l_ == 31,
                        [peT, (w1[j], g)], [ph])
            x, x2, u, th = [b_[:, 0:127] for b_ in gx]
            self.cp('act', x, ph[:, 0:127], [ph], [gx[0]])
            self.tt('dve', x2, x, x, ALU.mult, [gx[0]], [gx[1]])
            self.ts('dve', x2, x2, 0.044715, 1.0, ALU.mult, ALU.add, [gx[1]], [gx[1]])
            self.tt('dve', u, x, x2, ALU.mult, [gx[0], gx[1]], [gx[2]])
            self.act(th, u, AF.Tanh, [gx[2]], [gx[3]], scale=0.7978845608028654)
            self.ts('dve', th, th, 0.5, 0.5, ALU.mult, ALU.add, [gx[3]], [gx[3]])
            self.tt('dve', hid[:, 0:127], x, th, ALU.mult, [gx[0], gx[3]], [hid])
            self.mm(po[0:127, 0:64], hid[:, 0:127], w2[j][:], True, True, [hid, w2[j]], [po])
            if j == 0:
                self.headnorm_rope(po[0:127, 0:64].unsqueeze(1), 1, 64, 0, 32, gk[0:127, :], cfm[0:127, 1024:1056], cfm[0:127, 1056:1088],
                                   kc[0:127, :].unsqueeze(1), 1.0, tb, [po], [kc])
                pt = ps[4 + g]
                self.mm(pt[pg_, 0:127], kc[0:127, :], ident[0:127, 0:127], True, True, [kc, self.ident], [pt])
                self.cp('dve', kcT[pg_, g, 0:127], pt[pg_, 0:127], [pt], [(kcT, g)])
            else:
                self.cp('dve', cmpV[0:127, g, 0:64], po[0:127, 0:64], [po], [(cmpV, g)])
    A.release(m1)
    ct = A.get('nct', [1024], F32)
    self.dma('sp', ct[:], self.W['cf'][:, CF_KEEP:CF_KEEP + 1024], (), [ct])
    cmpb = A.get('ncb', [S], BF16)
    self.dma('pool', cmpb[:], self.W['cb'][:, CB_CMPB:CB_CMPB + S], (), [cmpb])
    E = A.get('nE', [S], BF16)
    self.dma('pool', E[:], self.W['cb'][:, CB_E:CB_E + S], (), [E])
    selt = [A.get('nsel%d' % j, [128], BF16) for j in range(4)]
    for j in range(4):
        self.memset('dve', selt[j][:], 0.0, [selt[j]])
    pTb = [A.get('npT%d' % j, [512], BF16) for j in range(3)]
    oacc = [A.get('noa%d' % j, [4, 64], F32) for j in range(2)]
    otmp = A.get('notmp', [4, 64], F32)
    ob = [A.get('nob%d' % j, [8, 64], BF16) for j in range(2)]
    sm = [A.get('nsm%d' % j, [160], F32) for j in range(2)]
    selb = [A.get('nselb%d' % j, [32], BF16) for j in range(4)]
    cnt = [0]
    NST = 14
    nop = lambda: None

    def nxt():
        b = cnt[0]
        cnt[0] += 1
        return ps[b % 3], pTb[b % 3]

    def ctx(idx):
        i, g = idx // 2, idx % 2
        s_ = sm[idx % 2]
        return dict(i=i, g=g, tc_=slice(i * 128, (i + 1) * 128), s_=s_, oa=oacc[idx % 2], sb_=selb[idx % 4], st_=selt[idx % 4],
                    ob_=ob[i % 2], den=s_[:, 0:4], rec=s_[:, 4:8], coef=s_[:, 8:12], imp=s_[:, 16:48], sc=s_[:, 48:80],
                    sc2=s_[:, 80:112], m8a=s_[:, 112:120], m8b=s_[:, 120:128], sk=[s_])

    def cmp_block(idx):
        c = ctx(idx)
        i, g, tc_, s_, sk = c['i'], c['g'], c['tc_'], c['s_'], c['sk']
        pb, pT = nxt()
        po = ps[3]

        def s0():
            self.mm(v3(pb[0:127, :], 4), kcT[:, g, 0:127], qT[:, :, tc_], True, False, [(kcT, g), (qT, i)], [pb])
            self.mm(v3(pb[0:127, :], 4), ident[0:127, 0:127], bc4(cmpb[0:127, tc_]), False, True, [self.ident, cmpb], [pb])

        def s1():
            self.act(pT[0:127, :], pb[0:127, :], AF.Exp, [pb], [pT])

        def s2():
            den, rec, coef, imp, sc, sc2, m8a, m8b = (c[n] for n in ('den', 'rec', 'coef', 'imp', 'sc', 'sc2', 'm8a', 'm8b'))
            oa, sb_, st_ = c['oa'], c['sb_'], c['st_']
            for cc in range(4):
                self.mm(po[:, cc * 97:(cc + 1) * 97], pT[0:127, cc * 128:(cc + 1) * 128], cmpV[0:127, g, :], cc == 0, True, [pT, cmpV], [po], skip=True)
            pov = v3(po[:, 0:388], 4)
            self.ts('dve', den, pov[:, :, 96], 1e-30, None, ALU.max, None, [po], sk)
            self.recip(rec, den, sk, sk)
            self.ts('dve', imp, pov[:, 0, 64:96], rec[:, 0:1], None, ALU.mult, None, [po] + sk, sk)
            for cc in range(1, 4):
                self.stt('dve', imp, pov[:, cc, 64:96], rec[:, cc:cc + 1], imp, ALU.mult, ALU.add, [po] + sk, sk)
            self.tt('dve', coef, rec, gts[:, i, 4 * g:4 * g + 4], ALU.mult, sk + [(gts, i)], sk)
            self.tt('dve', oa[:, :, :], pov[:, :, 0:64], coef.unsqueeze(2).broadcast_to([128, 4, 64]), ALU.mult, [po] + sk, [oa])
            self.tt('dve', sc, imp, ct[:, i * 32:(i + 1) * 32], ALU.mult, sk + [ct], sk)
            self.tt('dve', sc, sc, ct[:, 512 + i * 32:512 + (i + 1) * 32], ALU.add, sk + [ct], sk)
            self.P.op('dve', lambda e: e.max(out=m8a, in_=sc), sk, sk, strict=True)
            self.P.op('dve', lambda e: e.match_replace(out=sc2, in_to_replace=m8a, in_values=sc, imm_value=-1e30), sk, sk, strict=True)
            self.P.op('dve', lambda e: e.max(out=m8b, in_=sc2), sk, sk, strict=True)
            self.ts('dve', sb_[:, :], sc, m8b[:, 7:8], 1.0, ALU.is_ge, ALU.subtract, sk, [sb_])

        def s12():
            self.mm(ps[6][0:32, 0:128], c['sb_'][:, :], ident[:], True, True, [c['sb_'], self.ident], [ps[6]])

        def s13():
            self.cp('act', c['st_'][0:32, :], ps[6][0:32, 0:128], [ps[6]], [c['st_']])

        return [s0, s1, s2] + [nop] * (NST - 5) + [s12, s13]

    def att_block(idx, kind, kb, first, last):
        c = ctx(idx)
        i, g, tc_, s_, sk = c['i'], c['g'], c['tc_'], c['s_'], c['sk']
        pb, pT = nxt()
        kc_ = slice(kb * 128, (kb + 1) * 128)
        kT_, v_, pacc, goff = (kTs, vs, ps[4], 8) if kind == 'slc' else (kTw, vw, ps[5], 16)
        dg = (kb == i)
        far = (kind == 'win' and kb == i - 4)

        def s0():
            nb = (1 if kind == 'slc' else 0) + (1 if dg else 0) + (1 if far else 0)
            self.mm(v3(pb[:, :], 4), kT_[:, g, kc_], qT[:, :, tc_], True, nb == 0, [(kT_, kb), (qT, i)], [pb])
            if kind == 'slc':
                nb -= 1
                self.mm(v3(pb[:, :], 4), E[:, kc_], bc4(c['st_'][:, :]), False, nb == 0, [E, c['st_']], [pb])
            if dg:
                nb -= 1
                self.mm(v3(pb[:, :], 4), ident[:], bc4(self.nincl), False, nb == 0, [self.ident, self.cm], [pb])
            if far:
                nb -= 1
                self.mm(v3(pb[:, :], 4), ident[:], bc4(self.nfar), False, nb == 0, [self.ident, self.cm], [pb])

        def s1():
            self.act(pT[:, :], pb[:, :], AF.Exp, [pb], [pT])

        def s2():
            for cc in range(4):
                self.mm(pacc[:, cc * 65:(cc + 1) * 65], pT[:, cc * 128:(cc + 1) * 128], v_[:, kb, g, :], first and cc == 0, last,
                        [pT, (v_, kb)], [pacc], skip=True)
            if not last:
                return
            rec, coef, oa, ob_ = c['rec'], c['coef'], c['oa'], c['ob_']
            pav = v3(pacc[:, 0:260], 4)
            self.recip(rec, pav[:, :, 64], [pacc], sk)
            self.tt('dve', coef, rec, gts[:, i, goff + 4 * g:goff + 4 * g + 4], ALU.mult, sk + [(gts, i)], sk)
            self.tt('dve', otmp[:, :, :], pav[:, :, 0:64], coef.unsqueeze(2).broadcast_to([128, 4, 64]), ALU.mult, [pacc] + sk, [otmp])
            if kind == 'win':
                self.tt('dve', oa[:, :, :], oa[:, :, :], otmp[:, :, :], ALU.add, [oa, otmp], [oa])
            else:
                self.tt('dve', ob_[:, 4 * g:4 * g + 4, :], oa[:, :, :], otmp[:, :, :], ALU.add, [oa, otmp], [(ob_, g)])
                if g == 1:
                    pm2 = ps[7]
                    for p_ in range(4):
                        self.mm(pm2[:, p_ * 128:(p_ + 1) * 128], ob_[:, 2 * p_:2 * p_ + 2, :].rearrange('p h d -> p (h d)'), ident[:], p_ == 0, True,
                                [ob_, self.ident], [pm2], skip=True)
                    self.cp('act', oT[:, :, tc_], v3(pm2[:, :], 4), [pm2], [(oT, i)])

        return [s0, s1, s2] + [nop] * (NST - 3)

    blocks = [cmp_block(0)]
    for idx in range(2 * NT):
        i = idx // 2
        if idx + 1 < 2 * NT:
            blocks.append(cmp_block(idx + 1))
        kb0 = max(0, i - 4)
        for kb in range(kb0, i + 1):
            blocks.append(att_block(idx, 'win', kb, kb == kb0, kb == i))
        for kb in range(0, i + 1):
            blocks.append(att_block(idx, 'slc', kb, kb == 0, kb == i))
    run_pipeline(blocks, NST)
    A.release(m0)


K.nsa = nsa


def merge(self, l):
    A, P, hT, ps, X = self.A, self.P, self.hT, self.ps, self.X
    m0 = A.mark()
    yT = A.get('yT', [8, S], BF16)
    wg = [A.get('mwg%d' % j, [8, 3, 128], BF16) for j in range(2)]
    pw = [A.get('mpw%d' % j, [4, 3, 128], BF16) for j in range(1)]
    sg = [A.get('msg%d' % j, [512], BF16) for j in range(3)]
    tt_ = [A.get('mtt%d' % j, [512], F32) for j in range(2)]
    wv = self.W['w_in'][l].rearrange('(c p) f -> p c f', p=128)
    pj = [self.W[n][l].rearrange('(k p) d -> p k d', p=128) for n in ('proj_sb', 'proj_mla', 'proj_nsa')]
    oTs = [self.oT.get(n) for n in ('sb', 'mla', 'nsa')]
    cnt = 0
    for c in range(8):
        wg_, pw_ = wg[c % 2], pw[0]
        for b in range(3):
            self.dma('pool', wg_[:, :, b, :], wv[:, :, O_MG + b * 1024 + c * 128:O_MG + b * 1024 + (c + 1) * 128], (), [(wg_, b)])
            self.dma('pool', pw_[:, :, b, :], pj[b][:, :, c * 128:(c + 1) * 128], (), [(pw_, b)])
        for tg in range(4):
            tcs = slice(tg * 512, (tg + 1) * 512)
            gb, pb = [], []
            for b in range(3):
                g_ = ps[cnt % 8]
                cnt += 1
                for k in range(8):
                    self.mm(g_[:, :], wg_[:, k, b, :], hT[:, k, tcs], k == 0, k == 7, self.hk(4 * tg, 4 * tg + 4) + [(wg_, b)], [g_])
                self.act(sg[b][:], g_[:, :], AF.Sigmoid, [g_], [sg[b]])
            for b in range(3):
                p_ = ps[cnt % 8]
                cnt += 1
                pb.append(p_)
                if oTs[b] is None:
                    continue
                for kk in range(4):
                    self.mm(p_[:, :], pw_[:, kk, b, :], oTs[b][:, kk, tcs], kk == 0, kk == 3, [oTs[b], (pw_, b)], [p_])
            act = [b for b in range(3) if oTs[b] is not None]
            t0, t1 = tt_
            self.tt('dve', t0[:], sg[act[0]][:], pb[act[0]][:, :], ALU.mult, [sg[act[0]], pb[act[0]]], [t0])
            for b in act[1:]:
                self.tt('dve', t1[:], sg[b][:], pb[b][:, :], ALU.mult, [sg[b], pb[b]], [t1])
                self.tt('dve', t0[:], t0[:], t1[:], ALU.add, [t0, t1], [t0])
            self.cp('dve', yT[:, c, tcs], t0[:], [t0], [(yT, (c, tg))])
    A.free_top(hT)
    self.hT_freed = True
    wo = A.get_top('wout', [8, D], BF16)
    wov = self.W['w_out'][l].rearrange('(k p) d -> p k d', p=128)
    for h2 in range(2):
        self.dma('pool', wo[:, 4 * h2:4 * h2 + 4, :], wov[:, 4 * h2:4 * h2 + 4, :], (), [(wo, h2)])
    for i in range(NT):
        for dh in range(2):
            p_ = ps[cnt % 8]
            cnt += 1
            for k in range(8):
                self.mm(p_[:, :], yT[:, k, i * 128:(i + 1) * 128], wo[:, k, dh * 512:(dh + 1) * 512], k == 0, k == 7,
                        [(yT, (k, i // 4)), (wo, k // 4)], [p_])
            self.tt('dve', X[:, i, dh * 512:(dh + 1) * 512], X[:, i, dh * 512:(dh + 1) * 512], p_[:, :], ALU.add, [p_, (X, i)], [(X, i)])
    A.free_top(wo)
    A.release(m0)


K.merge = merge
```

```python
import numpy as np
import concourse.bass as bass
import concourse.mybir as mybir

F32 = mybir.dt.float32
BF16 = mybir.dt.bfloat16
U8 = mybir.dt.uint8
I32 = mybir.dt.int32
AF = mybir.ActivationFunctionType
ALU = mybir.AluOpType
AX = mybir.AxisListType

ENGS = ['pe', 'act', 'dve', 'pool', 'sp']
DSIZE = {F32: 4, BF16: 2, U8: 1, I32: 4}
SAME_ENG_SYNC = True
SAME_ENG_GAP = 1
DMA_K = {'sp': 12, 'pool': 12, 'act': 6}


class Buf:
    __slots__ = ('name', 'ap', 'state', 'space', 'off', 'size')

    def __init__(self, name, ap, space='sb', off=0, size=0):
        self.name = name
        self.ap = ap
        self.state = {}
        self.space = space
        self.off = off
        self.size = size

    def __getitem__(self, k):
        return self.ap[k]


class Op:
    __slots__ = ('eng', 'fn', 'idx', 'eidx', 'waits', 'inc', 'tick', 'is_dma', 'dsem', 'dval', 'vc', 'vcd', 'pewr')


class Prog:
    def __init__(self, nc, arena_bytes=200 * 1024):
        self.nc = nc
        self.ops = []
        self.eops = {e: [] for e in ENGS}
        self.known = {e: {d: -1 for d in ENGS} for e in ENGS}
        self.known_dma = {e: {} for e in ENGS}
        self.ndma = {e: 0 for e in ENGS}
        self.dma_hist = {e: [] for e in ENGS}
        self.arena_bytes = arena_bytes
        self.arena = nc.alloc_sbuf_tensor('arena', [128, arena_bytes], U8).ap()
        self.psum = [nc.alloc_psum_tensor('psb%d' % i, [128, 512], F32).ap() for i in range(8)]
        self.sb_top = 0
        self.freed = []
        self.live = {}
        self.all_dma = []
        self.spacer = {}

    def sb_at(self, name, off, shape, dtype, parts=128):
        n = int(np.prod(shape)) * DSIZE[dtype]
        assert off % 4 == 0
        assert off + n <= self.arena_bytes, (name, off, n, self.arena_bytes)
        ap = self.arena[0:parts, off:off + n].bitcast(dtype)
        if len(shape) > 1:
            names = ' '.join('d%d' % i for i in range(len(shape)))
            kw = {'d%d' % i: shape[i] for i in range(len(shape))}
            ap = ap.rearrange('p (%s) -> p %s' % (names, names), **kw)
        b = Buf(name, ap, 'sb', off, n)
        inh_w = []
        for (fo, fs, ops_) in self.freed:
            if fo < off + n and off < fo + fs:
                inh_w.extend(ops_)
        if inh_w:
            b.state[None] = [None, list(inh_w), list(inh_w)]
        return b

    def free(self, b):
        s = []
        for k, st in b.state.items():
            if st[0] is not None:
                s.append(st[0])
            s.extend(st[1])
            if len(st) > 2:
                s.extend(st[2])
        self.freed.append((b.off, b.size, self._compress(s)))
        if len(self.freed) > 400:
            self.freed = self.freed[-400:]

    def psb(self, bank, name=None):
        return Buf(name or ('ps%d' % bank), self.psum[bank], 'ps')

    def dram(self, name, ap):
        return Buf(name, ap, 'dram')

    def _compress(self, lst):
        best = {}
        out = []
        for i in set(lst):
            o = self.ops[i]
            if o.is_dma:
                out.append(i)
            else:
                if o.eng not in best or best[o.eng] < i:
                    best[o.eng] = i
        out.extend(best.values())
        return out

    def _deps_for(self, region, is_write, opidx, eng=None):
        if isinstance(region, tuple):
            buf, key = region
        else:
            buf, key = region, None
        st = buf.state
        deps = []
        psum = (buf.space == 'ps')
        if key is None:
            keys = list(st.keys())
        else:
            keys = [k for k in (key, None) if k in st]
        for k in keys:
            e = st[k]
            if e[0] is not None:
                deps.append(e[0])
            if is_write:
                deps.extend(e[1])
            elif psum:
                deps.extend(r for r in e[1] if self.ops[r].eng != eng)
            if len(e) > 2:
                deps.extend(e[2])
        return deps

    def _update(self, region, is_write, opidx):
        if isinstance(region, tuple):
            buf, key = region
        else:
            buf, key = region, None
        st = buf.state
        if key is None:
            if is_write:
                buf.state = {None: [opidx, []]}
            else:
                if None not in st:
                    st[None] = [None, []]
                for k in st:
                    st[k][1].append(opidx)
                    if len(st[k][1]) > 24:
                        st[k][1] = self._compress(st[k][1])
        else:
            if is_write:
                st[key] = [opidx, []]
            else:
                if key not in st:
                    st[key] = [None, []]
                st[key][1].append(opidx)
                if len(st[key][1]) > 24:
                    st[key][1] = self._compress(st[key][1])

    def _mkop(self, eng, fn, dma):
        o = Op()
        o.eng = eng
        o.fn = fn
        o.idx = len(self.ops)
        o.eidx = len(self.eops[eng])
        o.is_dma = dma
        o.inc = dma
        o.tick = None
        o.pewr = False
        o.waits = []
        return o

    def op(self, eng, fn, reads=(), writes=(), dma=False, pewr=False, strict=False):
        deps = []
        for r in reads:
            deps.extend(self._deps_for(r, False, None, eng))
        for w in writes:
            deps.extend(self._deps_for(w, True, None, eng))
        if eng in self.spacer and SAME_ENG_SYNC and not strict:
            last = len(self.eops[eng]) - 1
            for d in set(deps):
                od = self.ops[d]
                if (not od.is_dma) and od.eng == eng and od.eidx == last:
                    sp = self._mkop(eng, self.spacer[eng], False)
                    sp.vc = dict(self.known[eng])
                    sp.vcd = dict(self.known_dma[eng])
                    self.ops.append(sp)
                    self.eops[eng].append(sp)
                    break
        o = self._mkop(eng, fn, dma)
        if dma:
            k = DMA_K[eng]
            i = self.ndma[eng]
            o.dsem = (eng, i % k)
            o.dval = 16 * (i // k + 1)
            if i >= k:
                deps.append(self.dma_hist[eng][i - k])
            self.ndma[eng] += 1
            self.dma_hist[eng].append(o.idx)
            self.all_dma.append(o.idx)
        kn = self.known[eng]
        kd = self.known_dma[eng]
        for d in sorted(set(deps)):
            od = self.ops[d]
            if od.is_dma:
                if kd.get(od.dsem, 0) >= od.dval:
                    continue
                o.waits.append(d)
                kd[od.dsem] = od.dval
                for e2, v in od.vc.items():
                    if kn[e2] < v:
                        kn[e2] = v
                for k2, v in od.vcd.items():
                    if kd.get(k2, 0) < v:
                        kd[k2] = v
            else:
                if od.eng == eng:
                    if eng == 'pe' or not SAME_ENG_SYNC:
                        continue
                    if eng in self.spacer and not strict:
                        continue
                if kn[od.eng] >= d:
                    continue
                o.waits.append(d)
                od.inc = True
                for e2, v in od.vc.items():
                    if kn[e2] < v:
                        kn[e2] = v
                if kn[od.eng] < d:
                    kn[od.eng] = d
                for k2, v in od.vcd.items():
                    if kd.get(k2, 0) < v:
                        kd[k2] = v
        o.vc = dict(kn)
        o.vcd = dict(kd)
        self.ops.append(o)
        self.eops[eng].append(o)
        for r in reads:
            self._update(r, False, o.idx)
        for w in writes:
            self._update(w, True, o.idx)
        return o

    def emit(self):
        nc = self.nc
        fin_deps = list(self.all_dma[-64:])
        o = Op()
        o.eng = 'sp'; o.fn = None; o.idx = len(self.ops); o.is_dma = False; o.inc = False
        o.tick = None; o.waits = []; o.pewr = False
        kd = self.known_dma['sp']
        for q in DMA_K:
            for d in self.dma_hist[q][-DMA_K[q]:]:
                od = self.ops[d]
                if kd.get(od.dsem, 0) < od.dval:
                    o.waits.append(d)
        o.vc = {}; o.vcd = {}
        self.ops.append(o)
        self.eops['sp'].append(o)

        for e in ENGS:
            t = 0
            for op in self.eops[e]:
                if not op.is_dma and op.inc:
                    t += 1
                    op.tick = t
        import contextlib
        with contextlib.ExitStack() as es:
            esem = {e: es.enter_context(nc.semaphore('s_' + e)) for e in ENGS}
            dsem = {}
            for q, k in DMA_K.items():
                for j in range(k):
                    dsem[(q, j)] = es.enter_context(nc.semaphore('d_%s%d' % (q, j)))
            block = es.enter_context(nc.Block())
            ops = self.ops

            def run(e, h):
                for op in self.eops[e]:
                    for d in op.waits:
                        od = ops[d]
                        if od.is_dma:
                            h.wait_ge(dsem[od.dsem], od.dval)
                        else:
                            h.wait_ge(esem[od.eng], od.tick)
                    if op.fn is None:
                        continue
                    ins = op.fn(h)
                    if op.is_dma:
                        ins.then_inc(dsem[op.dsem], 16)
                    elif op.inc:
                        ins.then_inc(esem[e], 1)

            @block.tensor
            def _(h):
                run('pe', h)

            @block.scalar
            def _(h):
                run('act', h)

            @block.vector
            def _(h):
                run('dve', h)

            @block.gpsimd
            def _(h):
                run('pool', h)

            @block.sync
            def _(h):
                run('sp', h)

    def stats(self):
        s = {e: len(self.eops[e]) for e in ENGS}
        w = {e: sum(len(o.waits) for o in self.eops[e]) for e in ENGS}
        inc = {e: sum(1 for o in self.eops[e] if o.inc) for e in ENGS}
        return s, w, inc
import math, os
from concourse.bass_utils import run_bass_kernel_spmd

S = 2048
D = 1024
DFF = 2816
NT = 16
NL = 2
EPS = 1e-6
ARENA = 207 * 1024
NEG = -30000.0
N_IN = 6328
NPV = 352
PV_GFF1, PV_GMIX, PV_GFF2, PV_QN, PV_KVN, PV_GQ, PV_GK, PV_NQ, PV_NK = 0, 8, 16, 24, 26, 27, 123, 219, 283
O_SBQ, O_SBK, O_SBV, O_CQ, O_CKV, O_KR, O_NQ = 0, 512, 1024, 1536, 1792, 1920, 1952
O_CK, O_CV, O_SK, O_SV, O_WK, O_WV, O_NG, O_MG = 2464, 2592, 2720, 2848, 2976, 3104, 3232, 3256
CF_MC, CF_MS, CF_NC, CF_NS, CF_CC, CF_CS, CF_KEEP, CF_ADD, NCF = 0, 256, 512, 1024, 1536, 1568, 1600, 2112, 2624
CB_ID, CB_NSTRICT, CB_NINCL, CB_NFAR, CB_TRI, CB_NONES, CB_ONES, CB_CMPB, CB_E, CB_OV, NCB = 0, 128, 256, 384, 512, 640, 768, 896, 2944, 4992, 5056


class Alloc:
    def __init__(self, P, base, limit):
        self.P, self.base, self.limit, self.top, self.bufs = P, base, limit, base, []

    def get(self, name, shape, dtype, parts=128):
        n = int(np.prod(shape)) * DSIZE[dtype]
        off = (self.top + 31) // 32 * 32
        assert off + n <= self.limit, ('SBUF phase overflow', name, off, n, self.limit)
        b = self.P.sb_at(name, off, shape, dtype, parts)
        self.top = off + n
        self.peak = max(getattr(self, 'peak', 0), self.top)
        if os.environ.get('MEMDBG'):
            print('ALLOC %-10s off=%6d n=%6d top=%6d limit=%6d slack=%6d' % (name, off, n, self.top, self.limit, self.limit - self.top))
        self.bufs.append(b)
        return b

    def mark(self):
        return (self.top, len(self.bufs))

    def release(self, mark=None):
        top, nb = mark if mark is not None else (self.base, 0)
        for b in self.bufs[nb:]:
            self.P.free(b)
        self.bufs = self.bufs[:nb]
        self.top = top


class K:
    def __init__(self, nc, depth=NL, debug=None, stages=None):
        self.nc = nc
        self.depth = depth
        self.debug = debug
        self.stages = stages
        P = self.P = Prog(nc, arena_bytes=ARENA)
        dt = lambda name, shape, kind="ExternalInput": nc.dram_tensor(name, shape, F32, kind=kind).ap()
        self.x = dt('x', [S, D])
        self.out = dt('out', [S, D], "ExternalOutput")
        W = self.W = {}
        for name, shape in [('ffn1_wi', [NL, D, 2 * DFF]), ('ffn1_wo', [NL, DFF, D]), ('w_in', [NL, D, N_IN]),
                            ('mla_w_uq', [NL, 256, 768]), ('mla_w_ukv', [NL, 128, 1024]),
                            ('cmp_wk1', [NL, 2048, 128]), ('cmp_wk2', [NL, 128, 64]), ('cmp_wv1', [NL, 2048, 128]),
                            ('cmp_wv2', [NL, 128, 64]), ('proj_sb', [NL, 512, D]), ('proj_mla', [NL, 512, D]),
                            ('proj_nsa', [NL, 512, D]), ('w_out', [NL, D, D]), ('ffn2_wi', [NL, D, 2 * DFF]),
                            ('ffn2_wo', [NL, DFF, D]), ('pvec', [NL, 128, NPV]), ('pekv', [NL, 128, 64]),
                            ('cf', [128, NCF]), ('cb', [128, NCB])]:
            W[name] = dt(name, shape)
        if debug:
            self.dbg = dt('dbg', list(debug), "ExternalOutput")
        self.X = P.sb_at('X', 0, [NT, D], F32)
        cbase = NT * D * 4
        self.ident = P.sb_at('ident', cbase, [128], BF16)
        self.pv = P.sb_at('pv', cbase + 256, [NPV], F32)
        self.rstd = P.sb_at('rstd', cbase + 256 + NPV * 4, [NT], F32)
        self.small = P.sb_at('small', cbase + 256 + NPV * 4 + 64, [64], F32)
        self.A = Alloc(P, cbase + 4096, ARENA)
        self.ps = [P.psb(i) for i in range(8)]
        self.cnt = 0
        sm = self.small
        P.op('dve', lambda e: e.memset(sm[:, :], 0.0), (), [sm])
        if os.environ.get('SPACER'): P.spacer['dve'] = lambda e: e.memset(sm[:, 32:40], 0.0)
        if os.environ.get('SPACER'): P.spacer['act'] = lambda e: e.activation(out=sm[:, 40:48], in_=sm[:, 48:56], func=AF.Copy)

    def mm(self, out, lhsT, rhs, start, stop, R, Wr, skip=False):
        self.P.op('pe', lambda e: e.matmul(out, lhsT=lhsT, rhs=rhs, start=start, stop=stop, skip_group_check=skip), R, Wr)

    def tr(self, out, in_, ident, R, Wr):
        self.P.op('pe', lambda e: e.transpose(out=out, in_=in_, identity=ident), R, Wr)

    def act(self, out, in_, func, R, Wr, bias=None, scale=None, accum=None):
        kw = {}
        if bias is not None:
            kw['bias'] = bias
        if scale is not None:
            kw['scale'] = scale
        if accum is not None:
            kw['accum_out'] = accum
        strict = (bias is not None and not isinstance(bias, (int, float))) or (scale is not None and not isinstance(scale, (int, float)))
        self.P.op('act', lambda e: e.activation(out=out, in_=in_, func=func, **kw), R, Wr, strict=strict)

    def tt(self, eng, out, in0, in1, op, R, Wr):
        self.P.op(eng, lambda e: e.tensor_tensor(out=out, in0=in0, in1=in1, op=op), R, Wr)

    def ts(self, eng, out, in0, s1, s2, op0, op1, R, Wr, strict=False):
        strict = strict or not isinstance(s1, (int, float)) or not (s2 is None or isinstance(s2, (int, float)))
        if op1 is None:
            self.P.op(eng, lambda e: e.tensor_scalar(out=out, in0=in0, scalar1=s1, scalar2=None, op0=op0), R, Wr, strict=strict)
        else:
            self.P.op(eng, lambda e: e.tensor_scalar(out=out, in0=in0, scalar1=s1, scalar2=s2, op0=op0, op1=op1), R, Wr, strict=strict)

    def stt(self, eng, out, in0, scalar, in1, op0, op1, R, Wr):
        strict = not isinstance(scalar, (int, float))
        self.P.op(eng, lambda e: e.scalar_tensor_tensor(out=out, in0=in0, scalar=scalar, in1=in1, op0=op0, op1=op1), R, Wr, strict=strict)

    def cp(self, eng, out, in_, R, Wr):
        if eng == 'act':
            self.P.op('act', lambda e: e.copy(out=out, in_=in_), R, Wr)
        else:
            self.P.op(eng, lambda e: e.tensor_copy(out=out, in_=in_), R, Wr)

    def dma(self, q, out, in_, R, Wr):
        self.P.op(q, lambda e: e.dma_start(out=out, in_=in_), R, Wr, dma=True)

    def memset(self, eng, ap, val, Wr):
        self.P.op(eng, lambda e: e.memset(ap, val), (), Wr)

    def recip(self, out, in_, R, Wr):
        self.P.op('dve', lambda e: e.reciprocal(out=out, in_=in_), R, Wr)

    def load_x(self):
        xv = self.x.rearrange('(i p) d -> p i d', p=128)
        for c in range(4):
            self.dma('sp', self.X[:, 4 * c:4 * c + 4, :], xv[:, 4 * c:4 * c + 4, :], (), [(self.X, i) for i in range(4 * c, 4 * c + 4)])
        self.dma('pool', self.ident[:], self.W['cb'][:, CB_ID:CB_ID + 128], (), [self.ident])

    def store_x(self):
        ov = self.out.rearrange('(i p) d -> p i d', p=128)
        od = self.P.dram('out', self.out)
        for c in range(8):
            self.dma('sp', ov[:, 2 * c:2 * c + 2, :], self.X[:, 2 * c:2 * c + 2, :], [(self.X, i) for i in range(2 * c, 2 * c + 2)], [(od, c)])

    def load_pv(self, l):
        self.dma('sp', self.pv[:], self.W['pvec'][l], (), [self.pv])

    def norm_tile(self, i, gcol, hT, tcol, hkey, tmp, bank, save_rstd=True, use_saved=False):
        X = self.X
        ss, junk, xn = tmp
        k = self.cnt
        self.cnt += 1
        xn_ = xn[k % 2]
        rs = self.rstd[:, i:i + 1]
        if not use_saved:
            s_ = ss[:, (k % 2) * 2:(k % 2) * 2 + 1]
            sd = ss[:, (k % 2) * 2 + 1:(k % 2) * 2 + 2]
            sk = (ss, k % 2)
            self.memset('dve', s_, 0.0, [sk])
            self.act(junk[:], X[:, i, :], AF.Square, [(X, i), sk], [junk, sk], accum=s_)
            self.act(sd, s_, AF.Sqrt, [sk], [sk], scale=1.0 / D, bias=EPS)
            self.recip(rs, sd, [sk], [(self.rstd, i)])
        self.act(xn_[:], X[:, i, :], AF.Copy, [(X, i), (self.rstd, i)], [xn_], scale=rs)
        pb = self.ps[bank]
        pbv = pb.ap.bitcast(BF16)
        for c in range(8):
            self.tr(pbv[:, c * 128:(c + 1) * 128], xn_[:, c * 128:(c + 1) * 128], self.ident[:], [xn_, self.ident], [pb])
        self.tt('dve', hT[:, 0:8, tcol:tcol + 128], pbv[:, :].rearrange('p (c t) -> p c t', c=8),
                self.pv[:, gcol:gcol + 8].unsqueeze(2).broadcast_to([128, 8, 128]), ALU.mult, [pb, self.pv], [(hT, hkey)])

    def norm_tmp(self):
        A = self.A
        ss = A.get('ss', [4], F32)
        junk = A.get('junk', [D], BF16)
        xn = [A.get('xn%d' % j, [D], BF16) for j in range(2)]
        return (ss, junk, xn)

    def ffn(self, l, which):
        A, P, X = self.A, self.P, self.X
        A.release()
        wi = self.W['ffn%d_wi' % which][l].rearrange('(c p) f -> p c f', p=128)
        wo = self.W['ffn%d_wo' % which][l].rearrange('(j p) d -> p j d', p=128)
        gcol = PV_GFF1 if which == 1 else PV_GFF2
        wo_sb = A.get('wo_sb', [22, D], BF16)
        hT1_ = A.get_top('hT1', [8, 1024], BF16)
        hT0_ = A.get_top('hT0', [8, 1024], BF16)
        hTs = [hT0_, hT1_]
        uT = A.get('uT', [22, 1024], BF16)
        wib = [A.get('wib%d' % j, [8, 256], BF16) for j in range(2)]
        tmp = self.norm_tmp()
        sl = [A.get('sl%d' % j, [512], F32) for j in range(2)]
        it = 0
        for ti in range(8):
            self.norm_tile(ti, gcol, hTs[0], ti * 128, ti, tmp, 4 + (ti % 2))
        for hf in range(2):
            hT = hTs[hf]
            for j in range(22):
                if hf == 0 and j % 2 == 1 and j // 2 < 8:
                    ti = j // 2
                    self.norm_tile(8 + ti, gcol, hTs[1], ti * 128, ti, tmp, 4 + (ti % 2))
                wb = wib[j % 2]
                self.dma('pool', wb[:, :, 0:128], wi[:, :, j * 128:(j + 1) * 128], (), [(wb, 0)])
                self.dma('pool', wb[:, :, 128:256], wi[:, :, DFF + j * 128:DFF + (j + 1) * 128], (), [(wb, 1)])
                if hf == 0 and j % 2 == 0:
                    self.dma('pool', wo_sb[:, j:j + 2, :], wo[:, j:j + 2, :], (), [(wo_sb, j), (wo_sb, j + 1)])
                for tg in range(2):
                    pa, pbk = self.ps[(it % 2) * 2], self.ps[(it % 2) * 2 + 1]
                    s_ = sl[it % 2]
                    it += 1
                    hk = [(hT, 4 * tg + a) for a in range(4)]
                    for k in range(8):
                        self.mm(pa[:, :], wb[:, k, 0:128], hT[:, k, tg * 512:(tg + 1) * 512], k == 0, k == 7, hk + [(wb, 0)], [pa])
                    for k in range(8):
                        self.mm(pbk[:, :], wb[:, k, 128:256], hT[:, k, tg * 512:(tg + 1) * 512], k == 0, k == 7, hk + [(wb, 1)], [pbk])
                    self.act(s_[:], pa[:, :], AF.Silu, [pa], [s_])
                    self.tt('dve', uT[:, j, tg * 512:(tg + 1) * 512], s_[:], pbk[:, :], ALU.mult, [s_, pbk], [(uT, (j, tg))])
            for ti in range(8):
                i = 8 * hf + ti
                for dh in range(2):
                    pb = self.ps[4 + (2 * ti + dh) % 4]
                    for j in range(22):
                        self.mm(pb[:, :], uT[:, j, ti * 128:(ti + 1) * 128], wo_sb[:, j, dh * 512:(dh + 1) * 512], j == 0, j == 21,
                                [(uT, (j, ti // 4)), (wo_sb, j)], [pb])
                    self.stt('dve', X[:, i, dh * 512:(dh + 1) * 512], pb[:, :], 0.5, X[:, i, dh * 512:(dh + 1) * 512], ALU.mult, ALU.add,
                             [pb, (X, i)], [(X, i)])
        A.free_top(hTs[0])
        A.free_top(hTs[1])
        A.release()

    def build(self):
        self.load_x()
        for l in range(self.depth):
            self.load_pv(l)
            st = self.stages or ('ffn1', 'mix', 'ffn2')
            if 'ffn1' in st:
                self.ffn(l, 1)
            if 'mix' in st:
                self.mixer(l)
            if 'ffn2' in st:
                self.ffn(l, 2)
        self.store_x()
        self.P.emit()


def host_consts():
    cf = np.zeros((128, NCF), np.float32)
    p = np.arange(128)[:, None]
    pos = (np.arange(NT)[None, :] * 128 + p).astype(np.float32)

    def ropetab(d, posv):
        half = d // 2
        inv = np.exp(np.float32(-math.log(10000.0)) * np.arange(half, dtype=np.float32) * np.float32(2.0 / d)).astype(np.float32)
        ang = (posv[..., None].astype(np.float32) * inv).astype(np.float32)
        return np.cos(ang).astype(np.float32), np.sin(ang).astype(np.float32)
    c, s = ropetab(32, pos)
    cf[:, CF_MC:CF_MC + 256] = c.reshape(128, 256)
    cf[:, CF_MS:CF_MS + 256] = s.reshape(128, 256)
    c, s = ropetab(64, pos)
    cf[:, CF_NC:CF_NC + 512] = c.reshape(128, 512)
    cf[:, CF_NS:CF_NS + 512] = s.reshape(128, 512)
    ends = (np.arange(128) * 16 + 31).astype(np.float32)
    c, s = ropetab(64, ends)
    cf[:, CF_CC:CF_CC + 32] = c
    cf[:, CF_CS:CF_CS + 32] = s
    blk = np.arange(32)[None, None, :]
    cur = (pos // 64).astype(np.int64)[:, :, None]
    forced = (blk == 0) | (blk == cur) | (blk == cur - 1)
    fut = blk > cur
    keep = (~forced) & (~fut)
    add = np.where(fut, -1.0, np.where(forced, 1e3, 0.0))
    cf[:, CF_KEEP:CF_KEEP + 512] = keep.astype(np.float32).reshape(128, 512)
    cf[:, CF_ADD:CF_ADD + 512] = add.astype(np.float32).reshape(128, 512)
    cb = np.zeros((128, NCB), np.float32)
    a = np.arange(128)[:, None]
    b = np.arange(128)[None, :]
    cb[:, CB_ID:CB_ID + 128] = (a == b)
    cb[:, CB_NSTRICT:CB_NSTRICT + 128] = np.where(a < b, 0.0, NEG)
    cb[:, CB_NINCL:CB_NINCL + 128] = np.where(a <= b, 0.0, NEG)
    cb[:, CB_NFAR:CB_NFAR + 128] = np.where(a > b, 0.0, NEG)
    cb[:, CB_TRI:CB_TRI + 128] = np.where(a >= b, -1.0, 0.0)
    cb[:, CB_NONES:CB_NONES + 128] = -1.0
    cb[:, CB_ONES:CB_ONES + 128] = 1.0
    n = np.arange(128)[:, None]
    t = np.arange(S)[None, :]
    cb[:, CB_CMPB:CB_CMPB + S] = np.where(16 * n + 31 <= t, 0.0, NEG)
    j = np.arange(128)[:, None]
    cb[:, CB_E:CB_E + S] = (j == (t // 64)) * 30000.0
    c0 = np.arange(128)[:, None] * 16
    s0 = np.arange(32)[None, :] * 64
    ov = np.clip(np.minimum(c0 + 32, s0 + 64) - np.maximum(c0, s0), 0, None) / 32.0
    cb[:, CB_OV:CB_OV + 32] = ov
    cb[:, CB_OV + 32] = 1.0
    return cf, cb


def host_pvec(inp):
    pv = np.zeros((NL, 128, NPV), np.float32)
    for l in range(NL):
        pv[l, :, PV_GFF1:PV_GFF1 + 8] = inp['ffn1_norm'][l].reshape(8, 128).T
        pv[l, :, PV_GMIX:PV_GMIX + 8] = inp['mix_norm'][l].reshape(8, 128).T
        pv[l, :, PV_GFF2:PV_GFF2 + 8] = inp['ffn2_norm'][l].reshape(8, 128).T
        pv[l, :, PV_QN:PV_QN + 2] = inp['mla_q_norm'][l].reshape(2, 128).T
        pv[l, :, PV_KVN:PV_KVN + 1] = inp['mla_kv_norm'][l].reshape(1, 128).T
        pv[l, :, PV_GQ:PV_GQ + 96] = inp['mla_qk_gain_q'][l][None, :]
        pv[l, :, PV_GK:PV_GK + 96] = inp['mla_qk_gain_k'][l][None, :]
        pv[l, :, PV_NQ:PV_NQ + 64] = inp['nsa_q_gain'][l][None, :]
        pv[l, :, PV_NK:PV_NK + 64] = inp['nsa_k_gain'][l][None, :]
    pe = np.zeros((NL, 128, 64), np.float32)
    for l in range(NL):
        pe[l, :, 0:32] = np.tile(inp['cmp_pos_k'][l].T, (2, 1))
        pe[l, :, 32:64] = np.tile(inp['cmp_pos_v'][l].T, (2, 1))
    return pv, pe


_CACHE = {}


def make_maps(inputs, ncores=8):
    f32 = lambda a: np.ascontiguousarray(np.asarray(a, dtype=np.float32))
    inp = {k: f32(v) for k, v in inputs.items()}
    cf, cb = host_consts()
    pv, pe = host_pvec(inp)
    shared = {k: inp[k] for k in ['ffn1_wi', 'ffn1_wo', 'w_in', 'mla_w_uq', 'mla_w_ukv', 'cmp_wk1', 'cmp_wk2', 'cmp_wv1', 'cmp_wv2',
                                  'proj_sb', 'proj_mla', 'proj_nsa', 'w_out', 'ffn2_wi', 'ffn2_wo']}
    shared.update({'pvec': pv, 'pekv': pe, 'cf': cf, 'cb': cb})
    maps = []
    for c in range(ncores):
        m = dict(shared)
        m['x'] = np.ascontiguousarray(inp['x'][c])
        maps.append(m)
    return maps


def kernel(**inputs):
    if 'nc' not in _CACHE:
        nc = bass.Bass("TRN2", target_bir_lowering=False)
        K(nc).build()
        _CACHE['nc'] = nc
    nc = _CACHE['nc']
    maps = make_maps(inputs, 8)
    res = run_bass_kernel_spmd(nc, maps, core_ids=list(range(8)))
    return np.stack([np.asarray(r['out'], dtype=np.float32) for r in res.results], axis=0)


def _alloc_top(self, name, shape, dtype, parts=128):
    n = int(np.prod(shape)) * DSIZE[dtype]
    off = (self.limit - n) // 32 * 32
    assert off >= self.top, ('SBUF phase overflow (top)', name, off, self.top)
    b = self.P.sb_at(name, off, shape, dtype, parts)
    if not hasattr(self, 'tops'):
        self.tops = []
    self.tops.append((b, self.limit))
    self.limit = off
    return b


def _free_top(self, b):
    tb_, prev = self.tops.pop()
    assert tb_ is b, 'top allocations must be freed LIFO'
    self.P.free(b)
    self.limit = prev


Alloc.get_top = _alloc_top
Alloc.free_top = _free_top


def bc4(ap, n=4):
    return ap.unsqueeze(1).broadcast_to([ap.shape[0], n, ap.shape[1]])


def v3(ap, c):
    return ap.rearrange('p (c t) -> p c t', c=c)


def load_w_in(self, l, c0, n, buf, key=None):
    wv = self.W['w_in'][l].rearrange('(c p) f -> p c f', p=128)
    self.dma('pool', buf[:, :, 0:n], wv[:, :, c0:c0 + n], (), [buf if key is None else (buf, key)])


K.load_w_in = load_w_in


def mixer(self, l):
    A, P = self.A, self.P
    A.release()
    br = self.branches if hasattr(self, 'branches') else ('nsa', 'sb', 'mla')
    self.hT_freed = False
    hT = A.get_top('hT', [8, S], BF16)
    mk0 = A.mark()
    tmp = self.norm_tmp()
    for i in range(NT):
        self.norm_tile(i, PV_GMIX, hT, i * 128, i, tmp, 6 + (i % 2))
    A.release(mk0)
    self.hT = hT
    self.hk = lambda t0, t1: [(hT, i) for i in range(t0, t1)]
    cm = self.cm = A.get('cm', [768], BF16)
    self.dma('pool', cm[:], self.W['cb'][:, CB_NSTRICT:CB_NSTRICT + 768], (), [cm])
    self.nstrict, self.nincl, self.nfar = cm[:, 0:128], cm[:, 128:256], cm[:, 256:384]
    self.tri, self.nones, self.ones = cm[:, 384:512], cm[:, 512:640], cm[:, 640:768]
    self.oT = {}
    if 'nsa' in br:
        self.oT['nsa'] = A.get('oT_nsa', [4, S], BF16)
        self.nsa(l)
    if 'sb' in br:
        self.oT['sb'] = A.get('oT_sb', [4, S], BF16)
        self.sb(l)
    if 'mla' in br:
        self.oT['mla'] = A.get('oT_mla', [4, S], BF16)
        self.mla(l)
    if self.debug:
        for bi, b in enumerate(('sb', 'mla', 'nsa')):
            if b in self.oT:
                self.dma('pool', self.dbg[bi], self.oT[b][:, :, :], [self.oT[b]], [(self.P.dram('dbg', self.dbg), bi)])
    if getattr(self, 'hT_freed', False) and not getattr(self, 'skip_merge', False):
        hT = self.hT = A.get_top('hT', [8, S], BF16)
        self.hk = lambda t0, t1: [(hT, i) for i in range(t0, t1)]
        mk1 = A.mark()
        tmp = self.norm_tmp()
        for i in range(NT):
            self.norm_tile(i, PV_GMIX, hT, i * 128, i, tmp, 6 + (i % 2), use_saved=True)
        A.release(mk1)
        self.hT_freed = False
    if not getattr(self, 'skip_merge', False):
        self.merge(l)
    if not self.hT_freed:
        A.free_top(hT)
    A.release()


K.mixer = mixer


def run_pipeline(blocks, nstage):
    n = len(blocks)
    for t in range(n + nstage - 1):
        for k in reversed(range(nstage)):
            b = t - k
            if 0 <= b < n:
                blocks[b][k]()


def sb(self, l):
    A, P, hT = self.A, self.P, self.hT
    m0 = A.mark()
    wq = [A.get('sbw%d' % j, [8, 384], BF16) for j in range(2)]
    qP = [[A.get('sbq%d%d' % (j, h), [S], BF16) for h in range(2)] for j in range(2)]
    kT = [A.get('sbk%d' % j, [S], BF16) for j in range(2)]
    vv = [A.get('sbv%d' % j, [NT, 128], BF16) for j in range(2)]
    e_t = [A.get('sbe%d' % j, [512], F32) for j in range(2)]
    sp_t = [A.get('sbs%d' % j, [512], BF16) for j in range(3)]
    w_t = [A.get('sbp%d' % j, [512], BF16) for j in range(3)]
    sacc = [A.get('sba%d' % j, [512], BF16) for j in range(2)]
    oT = self.oT['sb']
    ps = self.ps
    ident = self.ident
    for j in range(2):
        for h in range(2):
            self.memset('dve', qP[j][h][:], 0.0, [qP[j][h]])
    wv = self.W['w_in'][l].rearrange('(c p) f -> p c f', p=128)

    def proj_chunks(pr):
        w = wq[pr % 2]
        q_, k_, v_ = qP[pr % 2], kT[pr % 2], vv[pr % 2]
        chunks = []

        def c_dma():
            for j, c0 in enumerate((O_SBQ, O_SBK, O_SBV)):
                self.dma('pool', w[:, :, j * 128:(j + 1) * 128], wv[:, :, c0 + pr * 128:c0 + (pr + 1) * 128], (), [(w, j)])
        chunks.append(c_dma)
        for tg in range(4):
            for j in range(2):
                def c_qk(tg=tg, j=j):
                    tcs = slice(tg * 512, (tg + 1) * 512)
                    pb = ps[6 + (tg * 2 + j) % 2]
                    for k_i in range(8):
                        self.mm(pb[:, :], w[:, k_i, j * 128:(j + 1) * 128], hT[:, k_i, tcs], k_i == 0, k_i == 7,
                                self.hk(4 * tg, 4 * tg + 4) + [(w, j)], [pb])
                    if j == 0:
                        for h in range(2):
                            self.ts('dve', q_[h][64 * h:64 * h + 64, tcs], pb[64 * h:64 * h + 64, :], 0.125, None, ALU.mult, None, [pb], [(q_[h], tg)])
                    else:
                        self.cp('dve', k_[:, tcs], pb[:, :], [pb], [(k_, tg)])
                chunks.append(c_qk)
        for i4 in range(4):
            def c_v(i4=i4):
                pb = ps[6 + i4 % 2]
                for ii in range(4):
                    i = i4 * 4 + ii
                    for k_i in range(8):
                        self.mm(pb[:, ii * 128:(ii + 1) * 128], hT[:, k_i, i * 128:(i + 1) * 128], w[:, k_i, 256:384], k_i == 0 and ii == 0, k_i == 7,
                                [(hT, i), (w, 2)], [pb], skip=True)
                self.cp('dve', v_[:, i4 * 4:i4 * 4 + 4, :], v3(pb[:, :], 4), [pb], [(v_, i4)])
            chunks.append(c_v)
        return chunks

    for c_ in proj_chunks(0):
        c_()
    for pr in range(4):
        q_, k_, v_ = qP[pr % 2], kT[pr % 2], vv[pr % 2]
        blocks = []
        bi = 0
        gi = 0
        for hh in range(2):
            pp = slice(64 * hh, 64 * hh + 64)
            for qg in range(4):
                sa = sacc[gi % 2]
                pc = ps[4 + gi % 2]
                gi += 1
                top = 4 * qg + 3
                for kb in range(top, -1, -1):
                    a = kb - 4 * qg
                    diag = a >= 0
                    c0 = 128 * a if diag else 0
                    cs = slice(c0, 512)
                    qc = slice(qg * 512 + c0, (qg + 1) * 512)
                    kc = slice(kb * 128, (kb + 1) * 128)
                    pa, pbk = ps[bi % 2], ps[2 + bi % 2]
                    et, st, wt = e_t[bi % 2], sp_t[bi % 3], w_t[bi % 3]
                    bi += 1
                    qh = q_[hh]
                    rq = [(qh, qg), (k_, kb // 4)]

                    def s0(pa=pa, cs=cs, c0=c0, kc=kc, qc=qc, diag=diag, rq=rq, qh=qh):
                        self.mm(pa[:, cs], k_[:, kc], qh[:, qc], True, not diag, rq, [pa])
                        if diag:
                            self.mm(pa[:, c0:c0 + 128], ident[:], self.nstrict, False, True, [self.ident, self.cm], [pa])

                    def s1(pa=pa, cs=cs, et=et, st=st):
                        self.act(et[:, cs], pa[:, cs], AF.Exp, [pa], [et])
                        self.act(st[:, cs], et[:, cs], AF.Ln, [et], [st], bias=1.0)

                    def s2(pbk=pbk, cs=cs, c0=c0, kc=kc, qc=qc, diag=diag, rq=rq, qh=qh, st=st, sa=sa, kb=kb, top=top):
                        if kb == top:
                            self.memset('dve', sa[:], 0.0, [sa])
                        self.mm(pbk[:, cs], self.tri, st[:, cs], True, False, [st, self.cm], [pbk])
                        if kb < top:
                            self.mm(pbk[:, cs], self.nones, sa[:, cs], False, False, [sa, self.cm], [pbk])
                        self.mm(pbk[:, cs], k_[:, kc], qh[:, qc], False, not diag, rq, [pbk])
                        if diag:
                            self.mm(pbk[:, c0:c0 + 128], ident[:], self.nstrict, False, True, [self.ident, self.cm], [pbk])
                        if kb > 0:
                            self.tt('dve', sa[:, cs], sa[:, cs], st[:, cs], ALU.add, [sa, st], [sa])

                    def s3(pbk=pbk, cs=cs, wt=wt):
                        self.act(wt[:, cs], pbk[:, cs], AF.Exp, [pbk], [wt])

                    def s4(pc=pc, cs=cs, wt=wt, kb=kb, top=top, pp=pp, qg=qg, hh=hh):
                        self.mm(pc[:, cs], v_[:, kb, :], wt[:, cs], kb == top, kb == 0, [(v_, kb // 4), wt], [pc], skip=True)
                        if kb == 0:
                            self.cp('dve', oT[pp, pr, qg * 512:(qg + 1) * 512], pc[pp, :], [pc], [(oT, (pr, hh, qg))])

                    blocks.append([s0, s1, s2, s3, s4])
        if pr + 1 < 4:
            nop = lambda: None
            ch = proj_chunks(pr + 1)
            step = max(1, (len(blocks) - 8) // len(ch))
            for ci, c_ in enumerate(ch):
                blocks.insert(min(len(blocks), 2 + ci * (step + 1)), [c_, nop, nop, nop, nop])
        run_pipeline(blocks, 5)
    A.release(m0)


K.sb = sb


def headnorm_rope(self, src, nh, hd, rope0, half, gain_ap, cos_ap, sin_ap, dst, scale, tmpb, R, Wr, gain_full=None, gR=None):
    sq, qn, st, r1, r2 = tmpb['sq'], tmpb['qn'], tmpb['st'], tmpb['r1'], tmpb['r2']
    n = nh * hd
    npart = src.shape[0]
    sqv = sq[0:npart, 0:n].rearrange('p (h d) -> p h d', h=nh)
    qnv = qn[0:npart, 0:n].rearrange('p (h d) -> p h d', h=nh)
    self.act(sqv, src, AF.Square, R, [sq])
    ss = st[0:npart, 0:nh]
    sd = st[0:npart, nh:2 * nh]
    self.P.op('dve', lambda e: e.tensor_reduce(out=ss, in_=sqv, axis=AX.X, op=ALU.add), [sq], [(st, 0)])
    self.act(sd, ss, AF.Sqrt, [(st, 0)], [(st, 1)], scale=1.0 / hd, bias=EPS)
    self.recip(ss, sd, [(st, 1)], [(st, 0)])
    if scale != 1.0:
        self.ts('dve', ss, ss, float(scale), None, ALU.mult, None, [(st, 0)], [(st, 0)])
    self.tt('dve', qnv, src, ss.unsqueeze(2).broadcast_to([npart, nh, hd]), ALU.mult, list(R) + [(st, 0)], [qn])
    if gain_full is not None:
        self.tt('dve', qnv, qnv, gain_full, ALU.mult, [qn] + list(gR), [qn])
    else:
        self.tt('dve', qnv, qnv, gain_ap.unsqueeze(1).broadcast_to([npart, nh, hd]), ALU.mult, [qn, self.pv], [qn])
    x1 = qnv[:, :, rope0:rope0 + half]
    x2 = qnv[:, :, rope0 + half:rope0 + 2 * half]
    cb = cos_ap.unsqueeze(1).broadcast_to([npart, nh, half])
    sb_ = sin_ap.unsqueeze(1).broadcast_to([npart, nh, half])
    r1v = r1[0:npart, 0:nh * half].rearrange('p (h d) -> p h d', h=nh)
    r2v = r2[0:npart, 0:nh * half].rearrange('p (h d) -> p h d', h=nh)
    cR = [qn, self.cfm]
    self.tt('dve', r1v, x2, sb_, ALU.mult, cR, [r1])
    self.tt('dve', r2v, x1, sb_, ALU.mult, cR, [r2])
    if rope0 > 0:
        self.cp('act', dst[:, :, 0:rope0], qnv[:, :, 0:rope0], [qn], [(Wr[0], 'a')] if isinstance(Wr[0], Buf) else Wr)
    o1 = dst[:, :, rope0:rope0 + half]
    o2 = dst[:, :, rope0 + half:rope0 + 2 * half]
    self.tt('dve', x1, x1, cb, ALU.mult, cR, [qn])
    self.tt('dve', x2, x2, cb, ALU.mult, cR, [qn])
    self.tt('dve', o1, x1, r1v, ALU.subtract, [qn, r1], Wr)
    self.tt('dve', o2, x2, r2v, ALU.add, [qn, r2], Wr)


K.headnorm_rope = headnorm_rope


def hn_tmp(self, A, n, nh, half):
    return {'sq': A.get('hn_sq', [n], F32), 'qn': A.get('hn_qn', [n], F32), 'st': A.get('hn_st', [2 * nh], F32),
            'r1': A.get('hn_r1', [nh * half], F32), 'r2': A.get('hn_r2', [nh * half], F32)}


K.hn_tmp = hn_tmp


def mla(self, l):
    A, P, hT, ps, ident = self.A, self.P, self.hT, self.ps, self.ident
    m0 = A.mark()
    oT = self.oT['mla']
    cfm = self.cfm = A.get('cfm_mla', [512], F32)
    self.dma('sp', cfm[:], self.W['cf'][:, CF_MC:CF_MC + 512], (), [cfm])
    wuq = A.get('wuq', [2, 768], BF16)
    self.dma('pool', wuq[:, :, :], self.W['mla_w_uq'][l].rearrange('(c p) f -> p c f', p=128), (), [wuq])
    wukv = A.get('wukv', [1024], BF16)
    self.dma('pool', wukv[:], self.W['mla_w_ukv'][l], (), [wukv])
    cqT = A.get('cqT', [2, S], BF16)
    ckvT = A.get('ckvT', [S], BF16)
    krt = A.get('krt', [NT, 32], F32)
    ms1 = A.mark()
    wc = A.get('mla_wc', [8, 416], BF16)
    self.load_w_in(l, O_CQ, 416, wc)
    st = A.get('mla_st', [8], F32)
    junk = A.get('mla_junk', [256], BF16)
    xq = [A.get('mla_xq%d' % j, [384], BF16) for j in range(2)]
    for i in range(NT):
        pb = ps[6 + i % 2]
        for k in range(8):
            self.mm(pb[:, 0:416], hT[:, k, i * 128:(i + 1) * 128], wc[:, k, :], k == 0, k == 7, [(hT, i), wc], [pb])
        x_ = xq[i % 2]
        sk = (st, i % 2)
        o = (i % 2) * 4
        self.memset('dve', st[:, o:o + 2], 0.0, [sk])
        self.act(junk[:, 0:256], pb[:, 0:256], AF.Square, [pb, sk], [junk, sk], accum=st[:, o:o + 1])
        self.act(junk[:, 0:128], pb[:, 256:384], AF.Square, [pb, sk], [junk, sk], accum=st[:, o + 1:o + 2])
        self.act(st[:, o + 2:o + 3], st[:, o:o + 1], AF.Sqrt, [sk], [sk], scale=1.0 / 256, bias=EPS)
        self.act(st[:, o + 3:o + 4], st[:, o + 1:o + 2], AF.Sqrt, [sk], [sk], scale=1.0 / 128, bias=EPS)
        self.recip(st[:, o:o + 2], st[:, o + 2:o + 4], [sk], [sk])
        self.act(x_[:, 0:256], pb[:, 0:256], AF.Copy, [pb, sk], [x_], scale=st[:, o:o + 1])
        self.act(x_[:, 256:384], pb[:, 256:384], AF.Copy, [pb, sk], [x_], scale=st[:, o + 1:o + 2])
        self.cp('dve', krt[:, i, :], pb[:, 384:416], [pb], [(krt, i)])
        pt = ps[4 + i % 2]
        for c in range(3):
            self.mm(pt[:, c * 128:(c + 1) * 128], x_[:, c * 128:(c + 1) * 128], ident[:], c == 0, True, [x_, self.ident], [pt], skip=True)
        self.tt('dve', cqT[:, :, i * 128:(i + 1) * 128], v3(pt[:, 0:256], 2),
                self.pv[:, PV_QN:PV_QN + 2].unsqueeze(2).broadcast_to([128, 2, 128]), ALU.mult, [pt, self.pv], [(cqT, i)])
        self.ts('dve', ckvT[:, i * 128:(i + 1) * 128], pt[:, 256:384], self.pv[:, PV_KVN:PV_KVN + 1], None, ALU.mult, None, [pt, self.pv], [(ckvT, i)])
    A.release(ms1)
    A.free_top(self.hT)
    self.hT_freed = True
    m1 = A.mark()
    qks = [A.get('mla_qkT%d' % j, [4, S], BF16) for j in range(2)]
    vvs = [A.get('mla_v%d' % j, [NT, 128], BF16) for j in range(2)]
    for j in range(2):
        self.memset('dve', qks[j][:, :, :], 0.0, [qks[j]])
    sq = A.get('mq_sq', [768], F32)
    qn = A.get('mq_qn', [768], F32)
    st = A.get('mq_st', [16], F32)
    r1 = A.get('mq_r1', [128], F32)
    r2 = A.get('mq_r2', [128], F32)
    qst = A.get('mq_qst', [768], F32)
    g4 = A.get('mla_g4', [4, 96], F32)
    qr_ = A.get('mla_qr', [768], BF16)
    p_t = [A.get('mla_p%d' % j, [512], BF16) for j in range(3)]
    rec = [A.get('mla_rec%d' % j, [512], F32) for j in range(2)]
    gq = self.pv[:, PV_GQ:PV_GQ + 96]
    gk = self.pv[:, PV_GK:PV_GK + 96]
    self.ts('dve', g4[:, 0, :], gq, float(96 ** -0.5), None, ALU.mult, None, [self.pv], [g4])
    self.ts('dve', g4[:, 1, :], gq, float(96 ** -0.5), None, ALU.mult, None, [self.pv], [g4])
    self.cp('dve', g4[:, 2, :], gk, [self.pv], [g4])
    self.cp('dve', g4[:, 3, :], gk, [self.pv], [g4])
    NPS = 14
    nop = lambda: None
    v4 = lambda ap: ap.rearrange('p (t h d) -> p t h d', t=2, h=4)
    v8 = lambda ap: ap.rearrange('p (h d) -> p h d', h=8)

    def prep_stages(pr, i2):
        qk, vv = qks[pr % 2], vvs[pr % 2]
        pqs = [ps[6], ps[7]]
        tiles = (2 * i2, 2 * i2 + 1)
        qst4, qn4, qr4 = v4(qst[:, :]), v4(qn[:, :]), v4(qr_[:, :])
        cos = cfm[:, i2 * 32:(i2 + 1) * 32].rearrange('p (t d) -> p t d', t=2).unsqueeze(2).broadcast_to([128, 2, 4, 16])
        sin = cfm[:, 256 + i2 * 32:256 + (i2 + 1) * 32].rearrange('p (t d) -> p t d', t=2).unsqueeze(2).broadcast_to([128, 2, 4, 16])
        r14 = r1[:, :].rearrange('p (t h d) -> p t h d', t=2, h=4)
        r24 = r2[:, :].rearrange('p (t h d) -> p t h d', t=2, h=4)
        ss, sd = st[:, 0:8], st[:, 8:16]
        x1, x2 = qn4[:, :, :, 64:80], qn4[:, :, :, 80:96]

        def p0():
            for t_, i in enumerate(tiles):
                pq = pqs[t_]
                for c in range(2):
                    self.mm(pq[:, 0:192], cqT[:, c, i * 128:(i + 1) * 128], wuq[:, c, pr * 192:(pr + 1) * 192], c == 0, c == 1, [(cqT, i), wuq], [pq])
                self.mm(pq[:, 256:512], ckvT[:, i * 128:(i + 1) * 128], wukv[:, pr * 256:(pr + 1) * 256], False, True, [(ckvT, i), wukv], [pq], skip=True)

        def p1():
            for t_, i in enumerate(tiles):
                pq = pqs[t_]
                kvv = v3(pq[:, 256:512], 2)
                self.cp('act', qst4[:, t_, 0:2, :], v3(pq[:, 0:192], 2), [pq], [(qst, (t_, 0))])
                self.cp('act', qst4[:, t_, 2:4, 0:64], kvv[:, :, 0:64], [pq], [(qst, (t_, 1))])
                self.cp('act', qst4[:, t_, 2:4, 64:96], krt[:, i, :].unsqueeze(1).broadcast_to([128, 2, 32]), [(krt, i)], [(qst, (t_, 2))])

        def p2():
            self.act(sq[:, :], qst[:, :], AF.Square, [qst], [sq])
            for t_, i in enumerate(tiles):
                kvv = v3(pqs[t_][:, 256:512], 2)
                self.cp('dve', vv[:, i, :].rearrange('p (h d) -> p h d', h=2), kvv[:, :, 64:128], [pqs[t_]], [(vv, i)])

        def p3():
            self.P.op('dve', lambda e: e.tensor_reduce(out=ss, in_=v8(sq[:, :]), axis=AX.X, op=ALU.add), [sq], [(st, 0)])

        def p4():
            self.act(sd, ss, AF.Ln, [(st, 0)], [(st, 1)], scale=1.0 / 96, bias=EPS)
            self.act(ss, sd, AF.Exp, [(st, 1)], [(st, 0)], scale=-0.5)

        def p5():
            self.tt('dve', v8(qn[:, :]), v8(qst[:, :]), ss.unsqueeze(2).broadcast_to([128, 8, 96]), ALU.mult, [qst, (st, 0)], [qn])
            self.tt('dve', qn4, qn4, g4[:, :, :].unsqueeze(1).broadcast_to([128, 2, 4, 96]), ALU.mult, [qn, g4], [qn])
            self.cp('dve', qr4[:, :, :, 0:64], qn4[:, :, :, 0:64], [qn], [qr_])
            cR = [qn, cfm]
            self.tt('dve', r14, x2, sin, ALU.mult, cR, [r1])
            self.tt('dve', r24, x1, sin, ALU.mult, cR, [r2])
            self.tt('dve', x1, x1, cos, ALU.mult, cR, [qn])
            self.tt('dve', x2, x2, cos, ALU.mult, cR, [qn])
            self.tt('dve', qr4[:, :, :, 64:80], x1, r14, ALU.subtract, [qn, r1], [qr_])
            self.tt('dve', qr4[:, :, :, 80:96], x2, r24, ALU.add, [qn, r2], [qr_])

        pts = [ps[4], ps[5]]

        def p6():
            for t_, i in enumerate(tiles):
                pt = pts[t_]
                for j in range(4):
                    self.mm(pt[0:96, j * 128:(j + 1) * 128], qr4[:, t_, j, :], ident[:], j == 0, True, [qr_, self.ident], [pt], skip=True)

        def p7():
            for t_, i in enumerate(tiles):
                self.cp('act', qk[0:96, :, i * 128:(i + 1) * 128], v3(pts[t_][0:96, :], 4), [pts[t_]], [(qk, i)])

        return [p0, p1, p2, p3, p4, p5] + [nop] * (NPS - 8) + [p6, p7]

    SP = 10
    NCH = NT // 2
    blocks = []
    for i2 in range(NCH):
        blocks.append(prep_stages(0, i2))
        for _ in range(SP - 1):
            blocks.append([nop] * NPS)
    run_pipeline(blocks, NPS)
    for pr in range(4):
        qk, vv = qks[pr % 2], vvs[pr % 2]
        blocks = []
        bi = 0
        gi = 0
        for hh in range(2):
            pp = slice(64 * hh, 64 * hh + 64)
            for qg in range(4):
                pc, pd = ps[2], ps[3]
                rc = rec[gi % 2]
                gi += 1
                top = 4 * qg + 3
                for kb in range(0, top + 1):
                    a = kb - 4 * qg
                    diag = a >= 0
                    c0 = 128 * a if diag else 0
                    cs = slice(c0, 512)
                    qc = slice(qg * 512 + c0, (qg + 1) * 512)
                    kc = slice(kb * 128, (kb + 1) * 128)
                    pa = ps[bi % 2]
                    pt_ = p_t[bi % 3]
                    bi += 1
                    rq = [(qk, i_) for i_ in range(4 * qg, 4 * qg + 4)] + [(qk, kb)]

                    def s0(pa=pa, cs=cs, c0=c0, kc=kc, qc=qc, diag=diag, rq=rq, hh=hh, qk=qk):
                        self.mm(pa[:, cs], qk[:, 2 + hh, kc], qk[:, hh, qc], True, not diag, rq, [pa])
                        if diag:
                            self.mm(pa[:, c0:c0 + 128], ident[:], self.nincl, False, True, [self.ident, self.cm], [pa])

                    def s1(pa=pa, cs=cs, pt_=pt_):
                        self.act(pt_[:, cs], pa[:, cs], AF.Exp, [pa], [pt_])

                    def s2(pc=pc, pd=pd, cs=cs, pt_=pt_, kb=kb, top=top, pp=pp, rc=rc, qg=qg, hh=hh, vv=vv, pr=pr):
                        self.mm(pc[:, cs], vv[:, kb, :], pt_[:, cs], kb == 0, kb == top, [(vv, kb), pt_], [pc])
                        self.mm(pd[:, cs], self.ones, pt_[:, cs], kb == 0, kb == top, [self.cm, pt_], [pd])
                        if kb == top:
                            self.act(rc[pp, :], pd[pp, :], AF.Ln, [pd], [rc])
                            self.act(rc[pp, :], rc[pp, :], AF.Exp, [rc], [rc], scale=-1.0)
                            self.tt('dve', oT[pp, pr, qg * 512:(qg + 1) * 512], pc[pp, :], rc[pp, :], ALU.mult, [pc, rc], [(oT, (pr, hh, qg))])

                    blocks.append([s0, s1, s2] + [nop] * (NPS - 3))
        if pr + 1 < 4:
            assert len(blocks) >= SP * NCH
            for i2 in range(NCH):
                blocks.insert(i2 * SP, prep_stages(pr + 1, i2))
        run_pipeline(blocks, NPS)
    A.release(m0)


K.mla = mla


def nsa(self, l):
    A, P, hT, ps, ident = self.A, self.P, self.hT, self.ps, self.ident
    m0 = A.mark()
    oT = self.oT['nsa']
    qT = A.get('nq', [4, S], BF16)
    kTs = A.get('nks', [2, S], BF16)
    kTw = A.get('nkw', [2, S], BF16)
    vs = A.get('nvs', [NT, 2, 65], BF16)
    vw = A.get('nvw', [NT, 2, 65], BF16)
    gts = A.get('ngt', [NT, 24], F32)
    kcT = A.get('nkc', [2, 128], BF16)
    cmpV = A.get('ncv', [2, 97], BF16)
    cfm = self.cfm = A.get('cfm_nsa', [1088], F32)
    self.dma('sp', cfm[:], self.W['cf'][:, CF_NC:CF_NC + 1088], (), [cfm])
    self.memset('dve', vs[:, :, :, :], 1.0, [vs])
    self.memset('dve', vw[:, :, :, :], 1.0, [vw])
    self.memset('dve', kTs[:, :, :], 0.0, [kTs])
    self.memset('dve', kTw[:, :, :], 0.0, [kTw])
    self.memset('dve', kcT[:, :, :], 0.0, [kcT])
    gq = self.pv[:, PV_NQ:PV_NQ + 64]
    gk = self.pv[:, PV_NK:PV_NK + 64]
    m1 = A.mark()
    NB = 2
    bufA = [A.get('npA%d' % j, [768], F32) for j in range(NB)]
    bufB = [A.get('npB%d' % j, [768], F32) for j in range(NB)]
    sts = [A.get('npst%d' % j, [24], F32) for j in range(NB)]
    qkrs = [A.get('npqk%d' % j, [12, 64], BF16) for j in range(NB)]
    g12 = A.get('npg12', [12, 64], F32)
    self.ts('dve', g12[:, 0:8, :], gq.unsqueeze(1).broadcast_to([128, 8, 64]), 0.125, None, ALU.mult, None, [self.pv], [g12])
    self.cp('dve', g12[:, 8:12, :], gk.unsqueeze(1).broadcast_to([128, 4, 64]), [self.pv], [g12])
    wN = A.get_top('nw', [8, 1304], BF16)
    self.load_w_in(l, O_NQ, 512, wN, 0)
    wv_ = self.W['w_in'][l].rearrange('(c p) f -> p c f', p=128)
    self.dma('pool', wN[:, :, 512:1304], wv_[:, :, O_CK:O_CK + 792], (), [(wN, 1)])
    v12 = lambda ap: ap.rearrange('p (h d) -> p h d', h=12)
    nop = lambda: None
    NPN = 10
    SPN = 5

    def prep_stages(i):
        tc_ = slice(i * 128, (i + 1) * 128)
        pq, pk, pg, pt = ps[6 + i % 2], ps[4 + i % 2], ps[2 + i % 2], ps[i % 2]
        bA, bB, st, qkr = bufA[i % NB], bufB[i % NB], sts[i % NB], qkrs[i % NB]
        A12, B12 = v12(bA[:, :]), v12(bB[:, :])
        ss, sd = st[:, 0:12], st[:, 12:24]
        cos = cfm[:, i * 32:(i + 1) * 32].unsqueeze(1).broadcast_to([128, 12, 32])
        sin = cfm[:, 512 + i * 32:512 + (i + 1) * 32].unsqueeze(1).broadcast_to([128, 12, 32])
        r1 = bA[:, 0:384].rearrange('p (h d) -> p h d', h=12)
        r2 = bA[:, 384:768].rearrange('p (h d) -> p h d', h=12)
        x1, x2 = B12[:, :, 0:32], B12[:, :, 32:64]

        def p0():
            for k_i in range(8):
                self.mm(pq[:, :], hT[:, k_i, tc_], wN[:, k_i, 0:512], k_i == 0, k_i == 7, [(hT, i), (wN, 0)], [pq])
            for k_i in range(8):
                self.mm(pk[:, :], hT[:, k_i, tc_], wN[:, k_i, 768:1280], k_i == 0, k_i == 7, [(hT, i), (wN, 1)], [pk])
            for k_i in range(8):
                self.mm(pg[:, 0:24], hT[:, k_i, tc_], wN[:, k_i, 1280:1304], k_i == 0, k_i == 7, [(hT, i), (wN, 1)], [pg])

        def p1():
            self.cp('act', A12[:, 0:8, :], v3(pq[:, :], 8), [pq], [(bA, 0)])
            self.cp('act', A12[:, 8:10, :], v3(pk[:, 0:128], 2), [pk], [(bA, 1)])
            self.cp('act', A12[:, 10:12, :], v3(pk[:, 256:384], 2), [pk], [(bA, 2)])
            self.act(gts[:, i, :], pg[:, 0:24], AF.Tanh, [pg], [(gts, i)], scale=0.5)

        def p2():
            self.act(bB[:, :], bA[:, :], AF.Square, [bA], [bB])
            self.ts('dve', gts[:, i, :], gts[:, i, :], 0.5, 0.5, ALU.mult, ALU.add, [(gts, i)], [(gts, i)])
            self.cp('dve', vs[:, i, :, 0:64], v3(pk[:, 128:256], 2), [pk], [(vs, i)])
            self.cp('dve', vw[:, i, :, 0:64], v3(pk[:, 384:512], 2), [pk], [(vw, i)])

        def p3():
            self.P.op('dve', lambda e: e.tensor_reduce(out=ss, in_=B12, axis=AX.X, op=ALU.add), [bB], [(st, 0)])

        def p4():
            self.act(sd, ss, AF.Ln, [(st, 0)], [(st, 1)], scale=1.0 / 64, bias=EPS)
            self.act(ss, sd, AF.Exp, [(st, 1)], [(st, 0)], scale=-0.5)

        def p5():
            self.tt('dve', B12, A12, ss.unsqueeze(2).broadcast_to([128, 12, 64]), ALU.mult, [bA, (st, 0)], [bB])
            self.tt('dve', B12, B12, g12[:, :, :], ALU.mult, [bB, g12], [bB])
            cR = [bB, cfm]
            self.tt('dve', r1, x2, sin, ALU.mult, cR, [(bA, 'r1')])
            self.tt('dve', r2, x1, sin, ALU.mult, cR, [(bA, 'r2')])
            self.tt('dve', x1, x1, cos, ALU.mult, cR, [bB])
            self.tt('dve', x2, x2, cos, ALU.mult, cR, [bB])
            self.tt('dve', qkr[:, :, 0:32], x1, r1, ALU.subtract, [bB, (bA, 'r1')], [qkr])
            self.tt('dve', qkr[:, :, 32:64], x2, r2, ALU.add, [bB, (bA, 'r2')], [qkr])

        def p8():
            for r in range(4):
                for g in range(2):
                    self.mm(pt[64 * g:64 * g + 64, r * 128:(r + 1) * 128], qkr[:, 4 * g + r, :], ident[:], r == 0, True, [qkr, self.ident], [pt], skip=True)
            for b in range(2):
                for g in range(2):
                    self.mm(pq[64 * g:64 * g + 64, b * 128:(b + 1) * 128], qkr[:, 8 + 2 * b + g, :], ident[:], b == 0, True, [qkr, self.ident], [pq], skip=True)

        def p9():
            self.cp('act', qT[:, :, tc_], v3(pt[:, :], 4), [pt], [(qT, i)])
            for g in range(2):
                gs = slice(64 * g, 64 * g + 64)
                self.cp('act', kTs[gs, g, tc_], pq[gs, 0:128], [pq], [(kTs, i)])
                self.cp('act', kTw[gs, g, tc_], pq[gs, 128:256], [pq], [(kTw, i)])

        return [p0, p1, p2, p3, p4, p5, nop, nop, p8, p9]

    blocks = []
    for i in range(NT):
        blocks.append(prep_stages(i))
        for _ in range(SPN - 1):
            blocks.append([nop] * NPN)
    run_pipeline(blocks, NPN)
    A.release(m1)
    m1 = A.mark()
    ckT = A.get('nck', [S], BF16)
    cvT = A.get('ncvT', [S], BF16)
    tb = self.hn_tmp(A, 64, 1, 32)
    for tg in range(4):
        for j, dst in ((0, ckT), (1, cvT)):
            pb = ps[(tg * 2 + j) % 2]
            for k_i in range(8):
                self.mm(pb[:, :], wN[:, k_i, 512 + j * 128:640 + j * 128], hT[:, k_i, tg * 512:(tg + 1) * 512], k_i == 0, k_i == 7,
                        self.hk(4 * tg, 4 * tg + 4) + [(wN, 1)], [pb])
            self.cp('act' if j == 0 else 'dve', dst[:, tg * 512:(tg + 1) * 512], pb[:, :], [pb], [(dst, tg)])
    A.free_top(wN)
    w1 = [A.get('nw1%d' % j, [32, 128], BF16) for j in range(2)]
    w2 = [A.get('nw2%d' % j, [64], BF16) for j in range(2)]
    peT = A.get('npe', [64], BF16)
    gx = [A.get('ngx%d' % j, [128], F32) for j in range(4)]
    hid = A.get('nhid', [128], BF16)
    kc = A.get('nkcs', [64], BF16)
    for j, nm in enumerate(('cmp_wk1', 'cmp_wv1')):
        src = self.W[nm][l].rearrange('(l d) h -> d l h', d=64)
        self.dma('pool', w1[j][0:64, :, :], src, (), [(w1[j], 0)])
        self.dma('pool', w1[j][64:128, :, :], src, (), [(w1[j], 1)])
    for j, nm in enumerate(('cmp_wk2', 'cmp_wv2')):
        self.dma('pool', w2[j][:], self.W[nm][l], (), [w2[j]])
    self.dma('pool', peT[:], self.W['pekv'][l], (), [peT])
    self.dma('pool', cmpV[0:127, 0, 64:97], self.W['cb'][0:127, CB_OV:CB_OV + 33], (), [(cmpV, 'ov0')])
    self.dma('pool', cmpV[0:127, 1, 64:97], self.W['cb'][0:127, CB_OV:CB_OV + 33], (), [(cmpV, 'ov1')])
    cnt = 0
    for j, cT in ((0, ckT), (1, cvT)):
        for g in range(2):
            pg_ = slice(64 * g, 64 * g + 64)
            ph, po = ps[cnt % 2], ps[2 + cnt % 2]
            cnt += 1
            for l_ in range(32):
                self.mm(ph[:, 0:127], w1[j][pg_, l_, :], cT[pg_, l_:l_ + 16 * 126 + 1:16], l_ == 0, False, [cT, (w1[j], g)], [ph])
            for l_ in range(32):
                self.mm(ph[:, 0:127], w1[j][pg_, l_, :], peT[pg_, j * 32 + l_:j * 32 + l_ + 1].broadcast_to([64, 127]), False, l_ == 31,
                        [peT, (w1[j], g)], [ph])
            x, x2, u, th = [b_[:, 0:127] for b_ in gx]
            self.cp('act', x, ph[:, 0:127], [ph], [gx[0]])
            self.tt('dve', x2, x, x, ALU.mult, [gx[0]], [gx[1]])
            self.ts('dve', x2, x2, 0.044715, 1.0, ALU.mult, ALU.add, [gx[1]], [gx[1]])
            self.tt('dve', u, x, x2, ALU.mult, [gx[0], gx[1]], [gx[2]])
            self.act(th, u, AF.Tanh, [gx[2]], [gx[3]], scale=0.7978845608028654)
            self.ts('dve', th, th, 0.5, 0.5, ALU.mult, ALU.add, [gx[3]], [gx[3]])
            self.tt('dve', hid[:, 0:127], x, th, ALU.mult, [gx[0], gx[3]], [hid])
            self.mm(po[0:127, 0:64], hid[:, 0:127], w2[j][:], True, True, [hid, w2[j]], [po])
            if j == 0:
                self.headnorm_rope(po[0:127, 0:64].unsqueeze(1), 1, 64, 0, 32, gk[0:127, :], cfm[0:127, 1024:1056], cfm[0:127, 1056:1088],
                                   kc[0:127, :].unsqueeze(1), 1.0, tb, [po], [kc])
                pt = ps[4 + g]
                self.mm(pt[pg_, 0:127], kc[0:127, :], ident[0:127, 0:127], True, True, [kc, self.ident], [pt])
                self.cp('dve', kcT[pg_, g, 0:127], pt[pg_, 0:127], [pt], [(kcT, g)])
            else:
                self.cp('dve', cmpV[0:127, g, 0:64], po[0:127, 0:64], [po], [(cmpV, g)])
    A.release(m1)
    ct = A.get('nct', [1024], F32)
    self.dma('sp', ct[:], self.W['cf'][:, CF_KEEP:CF_KEEP + 1024], (), [ct])
    cmpb = A.get('ncb', [S], BF16)
    self.dma('pool', cmpb[:], self.W['cb'][:, CB_CMPB:CB_CMPB + S], (), [cmpb])
    E = A.get('nE', [S], BF16)
    self.dma('pool', E[:], self.W['cb'][:, CB_E:CB_E + S], (), [E])
    selt = [A.get('nsel%d' % j, [128], BF16) for j in range(4)]
    for j in range(4):
        self.memset('dve', selt[j][:], 0.0, [selt[j]])
    pTb = [A.get('npT%d' % j, [512], BF16) for j in range(3)]
    oacc = [A.get('noa%d' % j, [4, 64], F32) for j in range(2)]
    otmp = A.get('notmp', [4, 64], F32)
    ob = [A.get('nob%d' % j, [8, 64], BF16) for j in range(2)]
    sm = [A.get('nsm%d' % j, [160], F32) for j in range(2)]
    selb = [A.get('nselb%d' % j, [32], BF16) for j in range(4)]
    cnt = [0]
    NST = 14
    nop = lambda: None

    def nxt():
        b = cnt[0]
        cnt[0] += 1
        return ps[b % 3], pTb[b % 3]

    def ctx(idx):
        i, g = idx // 2, idx % 2
        s_ = sm[idx % 2]
        return dict(i=i, g=g, tc_=slice(i * 128, (i + 1) * 128), s_=s_, oa=oacc[idx % 2], sb_=selb[idx % 4], st_=selt[idx % 4],
                    ob_=ob[i % 2], den=s_[:, 0:4], rec=s_[:, 4:8], coef=s_[:, 8:12], imp=s_[:, 16:48], sc=s_[:, 48:80],
                    sc2=s_[:, 80:112], m8a=s_[:, 112:120], m8b=s_[:, 120:128], sk=[s_])

    def cmp_block(idx):
        c = ctx(idx)
        i, g, tc_, s_, sk = c['i'], c['g'], c['tc_'], c['s_'], c['sk']
        pb, pT = nxt()
        po = ps[3]

        def s0():
            self.mm(v3(pb[0:127, :], 4), kcT[:, g, 0:127], qT[:, :, tc_], True, False, [(kcT, g), (qT, i)], [pb])
            self.mm(v3(pb[0:127, :], 4), ident[0:127, 0:127], bc4(cmpb[0:127, tc_]), False, True, [self.ident, cmpb], [pb])

        def s1():
            self.act(pT[0:127, :], pb[0:127, :], AF.Exp, [pb], [pT])

        def s2():
            den, rec, coef, imp, sc, sc2, m8a, m8b = (c[n] for n in ('den', 'rec', 'coef', 'imp', 'sc', 'sc2', 'm8a', 'm8b'))
            oa, sb_, st_ = c['oa'], c['sb_'], c['st_']
            for cc in range(4):
                self.mm(po[:, cc * 97:(cc + 1) * 97], pT[0:127, cc * 128:(cc + 1) * 128], cmpV[0:127, g, :], cc == 0, True, [pT, cmpV], [po], skip=True)
            pov = v3(po[:, 0:388], 4)
            self.ts('dve', den, pov[:, :, 96], 1e-30, None, ALU.max, None, [po], sk)
            self.recip(rec, den, sk, sk)
            self.ts('dve', imp, pov[:, 0, 64:96], rec[:, 0:1], None, ALU.mult, None, [po] + sk, sk)
            for cc in range(1, 4):
                self.stt('dve', imp, pov[:, cc, 64:96], rec[:, cc:cc + 1], imp, ALU.mult, ALU.add, [po] + sk, sk)
            self.tt('dve', coef, rec, gts[:, i, 4 * g:4 * g + 4], ALU.mult, sk + [(gts, i)], sk)
            self.tt('dve', oa[:, :, :], pov[:, :, 0:64], coef.unsqueeze(2).broadcast_to([128, 4, 64]), ALU.mult, [po] + sk, [oa])
            self.tt('dve', sc, imp, ct[:, i * 32:(i + 1) * 32], ALU.mult, sk + [ct], sk)
            self.tt('dve', sc, sc, ct[:, 512 + i * 32:512 + (i + 1) * 32], ALU.add, sk + [ct], sk)
            self.P.op('dve', lambda e: e.max(out=m8a, in_=sc), sk, sk, strict=True)
            self.P.op('dve', lambda e: e.match_replace(out=sc2, in_to_replace=m8a, in_values=sc, imm_value=-1e30), sk, sk, strict=True)
            self.P.op('dve', lambda e: e.max(out=m8b, in_=sc2), sk, sk, strict=True)
            self.ts('dve', sb_[:, :], sc, m8b[:, 7:8], 1.0, ALU.is_ge, ALU.subtract, sk, [sb_])

        def s12():
            self.mm(ps[6][0:32, 0:128], c['sb_'][:, :], ident[:], True, True, [c['sb_'], self.ident], [ps[6]])

        def s13():
            self.cp('act', c['st_'][0:32, :], ps[6][0:32, 0:128], [ps[6]], [c['st_']])

        return [s0, s1, s2] + [nop] * (NST - 5) + [s12, s13]

    def att_block(idx, kind, kb, first, last):
        c = ctx(idx)
        i, g, tc_, s_, sk = c['i'], c['g'], c['tc_'], c['s_'], c['sk']
        pb, pT = nxt()
        kc_ = slice(kb * 128, (kb + 1) * 128)
        kT_, v_, pacc, goff = (kTs, vs, ps[4], 8) if kind == 'slc' else (kTw, vw, ps[5], 16)
        dg = (kb == i)
        far = (kind == 'win' and kb == i - 4)

        def s0():
            nb = (1 if kind == 'slc' else 0) + (1 if dg else 0) + (1 if far else 0)
            self.mm(v3(pb[:, :], 4), kT_[:, g, kc_], qT[:, :, tc_], True, nb == 0, [(kT_, kb), (qT, i)], [pb])
            if kind == 'slc':
                nb -= 1
                self.mm(v3(pb[:, :], 4), E[:, kc_], bc4(c['st_'][:, :]), False, nb == 0, [E, c['st_']], [pb])
            if dg:
                nb -= 1
                self.mm(v3(pb[:, :], 4), ident[:], bc4(self.nincl), False, nb == 0, [self.ident, self.cm], [pb])
            if far:
                nb -= 1
                self.mm(v3(pb[:, :], 4), ident[:], bc4(self.nfar), False, nb == 0, [self.ident, self.cm], [pb])

        def s1():
            self.act(pT[:, :], pb[:, :], AF.Exp, [pb], [pT])

        def s2():
            for cc in range(4):
                self.mm(pacc[:, cc * 65:(cc + 1) * 65], pT[:, cc * 128:(cc + 1) * 128], v_[:, kb, g, :], first and cc == 0, last,
                        [pT, (v_, kb)], [pacc], skip=True)
            if not last:
                return
            rec, coef, oa, ob_ = c['rec'], c['coef'], c['oa'], c['ob_']
            pav = v3(pacc[:, 0:260], 4)
            self.recip(rec, pav[:, :, 64], [pacc], sk)
            self.tt('dve', coef, rec, gts[:, i, goff + 4 * g:goff + 4 * g + 4], ALU.mult, sk + [(gts, i)], sk)
            self.tt('dve', otmp[:, :, :], pav[:, :, 0:64], coef.unsqueeze(2).broadcast_to([128, 4, 64]), ALU.mult, [pacc] + sk, [otmp])
            if kind == 'win':
                self.tt('dve', oa[:, :, :], oa[:, :, :], otmp[:, :, :], ALU.add, [oa, otmp], [oa])
            else:
                self.tt('dve', ob_[:, 4 * g:4 * g + 4, :], oa[:, :, :], otmp[:, :, :], ALU.add, [oa, otmp], [(ob_, g)])
                if g == 1:
                    pm2 = ps[7]
                    for p_ in range(4):
                        self.mm(pm2[:, p_ * 128:(p_ + 1) * 128], ob_[:, 2 * p_:2 * p_ + 2, :].rearrange('p h d -> p (h d)'), ident[:], p_ == 0, True,
                                [ob_, self.ident], [pm2], skip=True)
                    self.cp('act', oT[:, :, tc_], v3(pm2[:, :], 4), [pm2], [(oT, i)])

        return [s0, s1, s2] + [nop] * (NST - 3)

    blocks = [cmp_block(0)]
    for idx in range(2 * NT):
        i = idx // 2
        if idx + 1 < 2 * NT:
            blocks.append(cmp_block(idx + 1))
        kb0 = max(0, i - 4)
        for kb in range(kb0, i + 1):
            blocks.append(att_block(idx, 'win', kb, kb == kb0, kb == i))
        for kb in range(0, i + 1):
            blocks.append(att_block(idx, 'slc', kb, kb == 0, kb == i))
    run_pipeline(blocks, NST)
    A.release(m0)


K.nsa = nsa


def merge(self, l):
    A, P, hT, ps, X = self.A, self.P, self.hT, self.ps, self.X
    m0 = A.mark()
    yT = A.get('yT', [8, S], BF16)
    wg = [A.get('mwg%d' % j, [8, 3, 128], BF16) for j in range(2)]
    pw = [A.get('mpw%d' % j, [4, 3, 128], BF16) for j in range(1)]
    sg = [A.get('msg%d' % j, [512], BF16) for j in range(3)]
    tt_ = [A.get('mtt%d' % j, [512], F32) for j in range(2)]
    wv = self.W['w_in'][l].rearrange('(c p) f -> p c f', p=128)
    pj = [self.W[n][l].rearrange('(k p) d -> p k d', p=128) for n in ('proj_sb', 'proj_mla', 'proj_nsa')]
    oTs = [self.oT.get(n) for n in ('sb', 'mla', 'nsa')]
    cnt = 0
    for c in range(8):
        wg_, pw_ = wg[c % 2], pw[0]
        for b in range(3):
            self.dma('pool', wg_[:, :, b, :], wv[:, :, O_MG + b * 1024 + c * 128:O_MG + b * 1024 + (c + 1) * 128], (), [(wg_, b)])
            self.dma('pool', pw_[:, :, b, :], pj[b][:, :, c * 128:(c + 1) * 128], (), [(pw_, b)])
        for tg in range(4):
            tcs = slice(tg * 512, (tg + 1) * 512)
            gb, pb = [], []
            for b in range(3):
                g_ = ps[cnt % 8]
                cnt += 1
                for k in range(8):
                    self.mm(g_[:, :], wg_[:, k, b, :], hT[:, k, tcs], k == 0, k == 7, self.hk(4 * tg, 4 * tg + 4) + [(wg_, b)], [g_])
                self.act(sg[b][:], g_[:, :], AF.Sigmoid, [g_], [sg[b]])
            for b in range(3):
                p_ = ps[cnt % 8]
                cnt += 1
                pb.append(p_)
                if oTs[b] is None:
                    continue
                for kk in range(4):
                    self.mm(p_[:, :], pw_[:, kk, b, :], oTs[b][:, kk, tcs], kk == 0, kk == 3, [oTs[b], (pw_, b)], [p_])
            act = [b for b in range(3) if oTs[b] is not None]
            t0, t1 = tt_
            self.tt('dve', t0[:], sg[act[0]][:], pb[act[0]][:, :], ALU.mult, [sg[act[0]], pb[act[0]]], [t0])
            for b in act[1:]:
                self.tt('dve', t1[:], sg[b][:], pb[b][:, :], ALU.mult, [sg[b], pb[b]], [t1])
                self.tt('dve', t0[:], t0[:], t1[:], ALU.add, [t0, t1], [t0])
            self.cp('dve', yT[:, c, tcs], t0[:], [t0], [(yT, (c, tg))])
    A.free_top(hT)
    self.hT_freed = True
    wo = A.get_top('wout', [8, D], BF16)
    wov = self.W['w_out'][l].rearrange('(k p) d -> p k d', p=128)
    for h2 in range(2):
        self.dma('pool', wo[:, 4 * h2:4 * h2 + 4, :], wov[:, 4 * h2:4 * h2 + 4, :], (), [(wo, h2)])
    for i in range(NT):
        for dh in range(2):
            p_ = ps[cnt % 8]
            cnt += 1
            for k in range(8):
                self.mm(p_[:, :], yT[:, k, i * 128:(i + 1) * 128], wo[:, k, dh * 512:(dh + 1) * 512], k == 0, k == 7,
                        [(yT, (k, i // 4)), (wo, k // 4)], [p_])
            self.tt('dve', X[:, i, dh * 512:(dh + 1) * 512], X[:, i, dh * 512:(dh + 1) * 512], p_[:, :], ALU.add, [p_, (X, i)], [(X, i)])
    A.free_top(wo)
    A.release(m0)


K.merge = merge
```
